# Optimizing a Trainium2 kernel written in Bass

```python
import math
import jax, jax.numpy as jnp
from jax import lax
import numpy as np

D_MODEL = 1024
BATCH = 8
SEQ = 2048
DEPTH = 1

CHUNK = 64
MEM_LEN = 256
Q_BLOCK = 128
EPS = 1e-6
RET_HEADS = 4
RET_DK = D_MODEL // RET_HEADS
RET_DV = 2 * RET_DK
RET_THETA_BASE = 10000.0
DIFF_HEADS = 8
DIFF_DK = D_MODEL // (2 * DIFF_HEADS)
DIFF_DV = 2 * DIFF_DK
ROPE_THETA = 500000.0
ROT_DIM = DIFF_DK // 4
LAMBDA_INIT_SCALE = 0.1
MEM_HEADS = 4
MEM_DH = D_MODEL // MEM_HEADS
D_FF = 2816
N_BRANCHES = 3

RET_QK_W = RET_HEADS * RET_DK
RET_V_W = RET_HEADS * RET_DV
DIFF_QK_W = DIFF_HEADS * 2 * DIFF_DK
DIFF_V_W = DIFF_HEADS * DIFF_DV
MEM_Q_W = MEM_HEADS * MEM_DH
GATE_W = N_BRANCHES * D_MODEL
IN_SPLITS = (RET_QK_W, RET_QK_W, RET_V_W, RET_V_W, DIFF_QK_W, DIFF_QK_W, DIFF_V_W, MEM_Q_W, GATE_W)
D_IN = int(sum(IN_SPLITS))
IN_OFFSETS = tuple(int(o) for o in np.cumsum(IN_SPLITS)[:-1])

kernel_name = "hybrid_retention_diffattn_memory_macaron"


def _rms(x, w=None):
    xf = x.astype(jnp.float32)
    y = xf * lax.rsqrt(jnp.mean(xf * xf, axis=-1, keepdims=True) + EPS)
    if w is not None:
        y = y * w.astype(jnp.float32)
    return y.astype(x.dtype)


def _swiglu(h, wg, wu, wd):
    return (jax.nn.silu(h @ wg) * (h @ wu)) @ wd


def _rope_partial(x, cos, sin):
    x1 = x[..., :ROT_DIM // 2]
    x2 = x[..., ROT_DIM // 2:ROT_DIM]
    return jnp.concatenate([x1 * cos - x2 * sin, x2 * cos + x1 * sin, x[..., ROT_DIM:]], axis=-1)


def _rot_interleaved(x, cos, sin):
    xe = x[..., 0::2]
    xo = x[..., 1::2]
    return jnp.stack([xe * cos - xo * sin, xo * cos + xe * sin], axis=-1).reshape(x.shape)


def _retention(q, k, v):
    B, S = q.shape[0], q.shape[1]
    N = S // CHUNK
    dt = v.dtype
    log_g = jnp.log(1.0 - 2.0 ** (-5.0 - jnp.arange(RET_HEADS, dtype=jnp.float32)))
    idx = jnp.arange(CHUNK, dtype=jnp.float32)
    d_intra = jnp.exp(log_g[:, None, None] * jnp.abs(idx[:, None] - idx[None, :])).astype(dt)
    q_dec = jnp.exp(log_g[:, None] * (idx[None, :] + 1.0)).astype(dt)
    k_dec = jnp.exp(log_g[:, None] * (CHUNK - 1.0 - idx[None, :])).astype(dt)
    c_dec = jnp.exp(log_g * CHUNK).astype(dt)

    def to_chunks(t):
        return t.reshape(B, N, CHUNK, RET_HEADS, t.shape[-1]).transpose(1, 0, 3, 2, 4)

    qc, kc, vc = to_chunks(q), to_chunks(k), to_chunks(v)
    scores = jnp.einsum('nbhcd,nbhed->nbhce', qc, kc) * d_intra
    o_intra = jnp.einsum('nbhce,nbhef->nbhcf', scores, vc)

    def step(state, xs):
        qn, kn, vn = xs
        o = jnp.einsum('bhcd,bhde->bhce', qn * q_dec[None, :, :, None], state)
        state = state * c_dec[None, :, None, None] + jnp.einsum(
            'bhcd,bhce->bhde', kn * k_dec[None, :, :, None], vn)
        return state, o

    s0 = jnp.zeros((B, RET_HEADS, RET_DK, RET_DV), dt)
    _, o_inter = lax.scan(step, s0, (qc, kc, vc))
    o = (o_intra + o_inter).transpose(1, 0, 3, 2, 4)
    return o.reshape(B, S, RET_HEADS, RET_DV)


def _diff_attention(q, k, v, lam):
    S = q.shape[1]
    scale = 1.0 / math.sqrt(DIFF_DK)
    q = q.transpose(0, 2, 3, 1, 4)
    k = k.transpose(0, 2, 3, 1, 4)
    v = v.transpose(0, 2, 1, 3)
    chunk_id = jnp.arange(S) // CHUNK
    outs = []
    for i in range(S // Q_BLOCK):
        q0, q1 = i * Q_BLOCK, (i + 1) * Q_BLOCK
        s = jnp.einsum('bhrqd,bhrkd->bhrqk', q[:, :, :, q0:q1], k[:, :, :, :q1]).astype(jnp.float32) * scale
        mask = chunk_id[q0:q1, None] >= chunk_id[None, :q1]
        p = jax.nn.softmax(jnp.where(mask, s, -jnp.inf), axis=-1)
        a = p[:, :, 0] - lam * p[:, :, 1]
        outs.append(jnp.einsum('bhqk,bhkd->bhqd', a.astype(v.dtype), v[:, :, :q1]))
    return jnp.concatenate(outs, axis=2).transpose(0, 2, 1, 3)


def _mem_attention(q, k, v):
    s = jnp.einsum('bshd,bmhd->bhsm', q, k).astype(jnp.float32) * (1.0 / math.sqrt(MEM_DH))
    p = jax.nn.softmax(s, axis=-1)
    return jnp.einsum('bhsm,bmhd->bshd', p.astype(v.dtype), v)


def setup_inputs(seed: int = 0) -> dict:
    key = jax.random.key(seed)
    ks = jax.random.split(key, 32)
    f32 = jnp.float32
    L = DEPTH

    def w(k, shape, fan_in):
        return jax.random.normal(k, shape, f32) * (fan_in ** -0.5)

    def gain(k, shape):
        return 1.0 + 0.05 * jax.random.normal(k, shape, f32)

    offset = jax.random.randint(ks[2], (BATCH, 1), 0, 64) * CHUNK
    positions = (offset + jnp.arange(SEQ)[None, :]).astype(jnp.int32)
    return {
        "x": jax.random.normal(ks[0], (BATCH, SEQ, D_MODEL), f32),
        "mem": jax.random.normal(ks[1], (BATCH, MEM_LEN, D_MODEL), f32),
        "positions": positions,
        "ffn1_norm": gain(ks[3], (L, D_MODEL)),
        "ffn1_w_gate": w(ks[4], (L, D_MODEL, D_FF), D_MODEL),
        "ffn1_w_up": w(ks[5], (L, D_MODEL, D_FF), D_MODEL),
        "ffn1_w_down": w(ks[6], (L, D_FF, D_MODEL), D_FF),
        "mix_norm": gain(ks[7], (L, D_MODEL)),
        "w_in": w(ks[8], (L, D_MODEL, D_IN), D_MODEL),
        "b_gate": 0.01 * jax.random.normal(ks[9], (L, GATE_W), f32),
        "ret_w_o": w(ks[10], (L, RET_V_W, D_MODEL), RET_V_W),
        "diff_q_norm": gain(ks[11], (L, DIFF_DK)),
        "diff_k_norm": gain(ks[12], (L, DIFF_DK)),
        "diff_lambda_q1": LAMBDA_INIT_SCALE * jax.random.normal(ks[13], (L, DIFF_DK), f32),
        "diff_lambda_k1": LAMBDA_INIT_SCALE * jax.random.normal(ks[14], (L, DIFF_DK), f32),
        "diff_lambda_q2": LAMBDA_INIT_SCALE * jax.random.normal(ks[15], (L, DIFF_DK), f32),
        "diff_lambda_k2": LAMBDA_INIT_SCALE * jax.random.normal(ks[16], (L, DIFF_DK), f32),
        "diff_subln": gain(ks[17], (L, DIFF_DV)),
        "diff_w_o": w(ks[18], (L, DIFF_V_W, D_MODEL), DIFF_V_W),
        "mem_norm": gain(ks[19], (L, D_MODEL)),
        "mem_w_kv": w(ks[20], (L, D_MODEL, 2 * MEM_Q_W), D_MODEL),
        "mem_q_norm": gain(ks[21], (L, MEM_DH)),
        "mem_k_norm": gain(ks[22], (L, MEM_DH)),
        "mem_w_o": w(ks[23], (L, MEM_Q_W, D_MODEL), MEM_Q_W),
        "w_out": w(ks[24], (L, D_MODEL, D_MODEL), D_MODEL),
        "ffn2_norm": gain(ks[25], (L, D_MODEL)),
        "ffn2_w_gate": w(ks[26], (L, D_MODEL, D_FF), D_MODEL),
        "ffn2_w_up": w(ks[27], (L, D_MODEL, D_FF), D_MODEL),
        "ffn2_w_down": w(ks[28], (L, D_FF, D_MODEL), D_FF),
        "final_norm": gain(ks[29], (L, D_MODEL)),
    }


def reference(x, mem, positions, ffn1_norm, ffn1_w_gate, ffn1_w_up, ffn1_w_down, mix_norm, w_in, b_gate,
              ret_w_o, diff_q_norm, diff_k_norm, diff_lambda_q1, diff_lambda_k1, diff_lambda_q2,
              diff_lambda_k2, diff_subln, diff_w_o, mem_norm, mem_w_kv, mem_q_norm, mem_k_norm, mem_w_o,
              w_out, ffn2_norm, ffn2_w_gate, ffn2_w_up, ffn2_w_down, final_norm):
    B, S = x.shape[0], x.shape[1]
    M = mem.shape[1]
    dt = x.dtype
    pos = positions.astype(jnp.float32)[..., None]
    ret_inv = 1.0 / (RET_THETA_BASE ** jnp.linspace(0.0, 1.0, RET_DK // 2, dtype=jnp.float32))
    ret_ang = pos * ret_inv
    r_cos = jnp.cos(ret_ang)[:, :, None, :].astype(dt)
    r_sin = jnp.sin(ret_ang)[:, :, None, :].astype(dt)
    rope_inv = 1.0 / (ROPE_THETA ** (jnp.arange(0, ROT_DIM, 2, dtype=jnp.float32) / ROT_DIM))
    d_ang = pos * rope_inv
    d_cos = jnp.cos(d_ang)[:, :, None, None, :].astype(dt)
    d_sin = jnp.sin(d_ang)[:, :, None, None, :].astype(dt)

    for l in range(DEPTH):
        x = x + 0.5 * _swiglu(_rms(x, ffn1_norm[l]), ffn1_w_gate[l], ffn1_w_up[l], ffn1_w_down[l])

        h = _rms(x, mix_norm[l])
        rq, rk, rv, rg, dq, dk, dv, mq, gates = jnp.split(h @ w_in[l], IN_OFFSETS, axis=-1)

        rq = _rot_interleaved(rq.reshape(B, S, RET_HEADS, RET_DK), r_cos, r_sin)
        rk = _rot_interleaved(rk.reshape(B, S, RET_HEADS, RET_DK), r_cos, r_sin) * (RET_DK ** -0.5)
        ro = _retention(rq, rk, rv.reshape(B, S, RET_HEADS, RET_DV))
        ro = _rms(ro).reshape(B, S, RET_V_W) * jax.nn.silu(rg)
        ret_out = ro @ ret_w_o[l]

        dq = _rope_partial(_rms(dq.reshape(B, S, DIFF_HEADS, 2, DIFF_DK), diff_q_norm[l]), d_cos, d_sin)
        dk = _rope_partial(_rms(dk.reshape(B, S, DIFF_HEADS, 2, DIFF_DK), diff_k_norm[l]), d_cos, d_sin)
        lam_init = 0.8 - 0.6 * math.exp(-0.3 * l)
        lam = (jnp.exp(jnp.sum(diff_lambda_q1[l].astype(jnp.float32) * diff_lambda_k1[l].astype(jnp.float32)))
               - jnp.exp(jnp.sum(diff_lambda_q2[l].astype(jnp.float32) * diff_lambda_k2[l].astype(jnp.float32)))
               + lam_init)
        do = _diff_attention(dq, dk, dv.reshape(B, S, DIFF_HEADS, DIFF_DV), lam)
        do = _rms(do, diff_subln[l]) * (1.0 - lam_init)
        diff_out = do.reshape(B, S, DIFF_V_W) @ diff_w_o[l]

        mk, mv = jnp.split(_rms(mem, mem_norm[l]) @ mem_w_kv[l], 2, axis=-1)
        mq = _rms(mq.reshape(B, S, MEM_HEADS, MEM_DH), mem_q_norm[l])
        mk = _rms(mk.reshape(B, M, MEM_HEADS, MEM_DH), mem_k_norm[l])
        mo = _mem_attention(mq, mk, mv.reshape(B, M, MEM_HEADS, MEM_DH))
        mem_out = mo.reshape(B, S, MEM_Q_W) @ mem_w_o[l]

        g = jax.nn.sigmoid(gates + b_gate[l]).reshape(B, S, N_BRANCHES, D_MODEL)
        merged = g[:, :, 0] * ret_out + g[:, :, 1] * diff_out + g[:, :, 2] * mem_out
        x = x + merged @ w_out[l]

        x = x + 0.5 * _swiglu(_rms(x, ffn2_norm[l]), ffn2_w_gate[l], ffn2_w_up[l], ffn2_w_down[l])
        x = _rms(x, final_norm[l])
    return x
```

```python
import os
import math
import numpy as np
from contextlib import ExitStack
import concourse.bass as bass
import concourse.mybir as mybir
from concourse.bass_utils import run_bass_kernel_spmd

F32 = mybir.dt.float32
BF16 = mybir.dt.bfloat16
I32 = mybir.dt.int32
AF = mybir.ActivationFunctionType
ALU = mybir.AluOpType
AX = mybir.AxisListType

S = 2048
D = 1024
NT = 16
NB = 4
DFF = 2816
NG = 11
EPS = 1e-6
SLOT = 4096
NSLOT = 4
TWO_PI = 2.0 * math.pi

STAGE = int(os.environ.get("MK_STAGE", "3"))


class Prog:
    ENG = ("pe", "act", "dve", "pool", "sp")

    def __init__(self):
        self.streams = {e: [] for e in self.ENG}
        self.nops = {e: 0 for e in self.ENG}
        self.known = {e: {} for e in self.ENG}
        self.res = {}
        self.sig = {e: set() for e in self.ENG}
        self.dcount = {}

    def _deps(self, eng, reads, writes):
        need = {}

        def add(c):
            sk, idx, clock = c
            if sk == "pe" and eng == "pe":
                return
            if self.known[eng].get(sk, 0) >= idx:
                return
            cur = need.get(sk)
            if cur is None or cur[0] < idx:
                need[sk] = (idx, clock)

        for k in reads:
            st = self.res.get(k)
            if st is not None and st[0] is not None:
                add(st[0])
        for k in writes:
            st = self.res.get(k)
            if st is not None:
                if st[0] is not None:
                    add(st[0])
                for sk, (idx, clock) in st[1].items():
                    add((sk, idx, clock))
        kn = self.known[eng]
        for sk, (idx, clock) in need.items():
            if kn.get(sk, 0) >= idx:
                continue
            self.streams[eng].append(("w", sk, idx))
            if sk in self.sig:
                self.sig[sk].add(idx)
            for a, b in clock.items():
                if kn.get(a, 0) < b:
                    kn[a] = b
            kn[sk] = max(kn.get(sk, 0), idx)

    def _record(self, comp, reads, writes):
        sk, idx, clock = comp
        for k in writes:
            self.res[k] = [comp, {}]
        for k in reads:
            st = self.res.get(k)
            if st is None:
                st = [None, {}]
                self.res[k] = st
            cur = st[1].get(sk)
            if cur is None or cur[0] < idx:
                st[1][sk] = (idx, clock)

    def op(self, eng, fn, r=(), w=()):
        self._deps(eng, r, w)
        self.nops[eng] += 1
        idx = self.nops[eng]
        self.streams[eng].append(("o", fn, idx))
        clock = dict(self.known[eng])
        clock[eng] = idx
        self._record((eng, idx, clock), r, w)

    def dma(self, issuer, sem, fn, r=(), w=()):
        self._deps(issuer, r, w)
        sk = ("d", sem)
        self.dcount[sk] = self.dcount.get(sk, 0) + 1
        idx = self.dcount[sk]
        self.streams[issuer].append(("d", fn, sk))
        clock = dict(self.known[issuer])
        clock[sk] = idx
        self._record((sk, idx, clock), r, w)

    def self_wait(self, eng):
        idx = self.nops[eng]
        if idx > 0 and self.known[eng].get(eng, 0) < idx:
            self.streams[eng].append(("w", eng, idx))
            self.sig[eng].add(idx)
            self.known[eng][eng] = idx

    def barrier(self):
        pend = {}
        for st in self.res.values():
            if st[0] is not None:
                sk, idx, _ = st[0]
                pend[sk] = max(pend.get(sk, 0), idx)
            for sk, (idx, _) in st[1].items():
                pend[sk] = max(pend.get(sk, 0), idx)
        for e in self.ENG:
            kn = self.known[e]
            for sk, idx in pend.items():
                if kn.get(sk, 0) >= idx:
                    continue
                if sk == e and e == "pe":
                    pass
                self.streams[e].append(("w", sk, idx))
                if sk in self.sig:
                    self.sig[sk].add(idx)
                kn[sk] = idx
        self.res = {}

    def check(self):
        sigval = {}
        for e in self.ENG:
            m = {}
            c = 0
            for i in range(1, self.nops[e] + 1):
                if i in self.sig[e]:
                    c += 1
                    m[i] = c
            sigval[e] = m
        semv = {}
        pc = {e: 0 for e in self.ENG}
        progress = True
        while progress:
            progress = False
            for e in self.ENG:
                st = self.streams[e]
                while pc[e] < len(st):
                    ent = st[pc[e]]
                    if ent[0] == "w":
                        sk, idx = ent[1], ent[2]
                        need = 16 * idx if isinstance(sk, tuple) else sigval[sk][idx]
                        if semv.get(sk, 0) < need:
                            break
                    elif ent[0] == "o":
                        if ent[2] in self.sig[e]:
                            semv[e] = semv.get(e, 0) + 1
                    else:
                        semv[ent[2]] = semv.get(ent[2], 0) + 16
                    pc[e] += 1
                    progress = True
        stuck = {e: (pc[e], len(self.streams[e])) for e in self.ENG if pc[e] < len(self.streams[e])}
        if stuck:
            for e, (p, n) in stuck.items():
                print("STUCK", e, p, n, self.streams[e][p][:3], semv)
            raise RuntimeError("semaphore program deadlocks: %r" % (stuck,))
        return {e: len(self.streams[e]) for e in self.ENG}, {k: v for k, v in semv.items()}

    def emit(self, nc, es):
        sems = {e: es.enter_context(nc.semaphore("s_" + e)) for e in self.ENG}
        dsems = {sk: es.enter_context(nc.semaphore("d_" + sk[1])) for sk in self.dcount}
        sigval = {}
        for e in self.ENG:
            m = {}
            c = 0
            ss = self.sig[e]
            for i in range(1, self.nops[e] + 1):
                if i in ss:
                    c += 1
                    m[i] = c
            sigval[e] = m
        engobj = {"pe": "tensor", "act": "scalar", "dve": "vector", "pool": "gpsimd", "sp": "sync"}
        block = es.enter_context(nc.Block())
        for e in self.ENG:
            stream = self.streams[e]
            if not stream:
                continue

            def body(eng, stream=stream, e=e):
                for ent in stream:
                    if ent[0] == "w":
                        sk, idx = ent[1], ent[2]
                        if isinstance(sk, tuple):
                            eng.wait_ge(dsems[sk], 16 * idx)
                        else:
                            eng.wait_ge(sems[sk], sigval[sk][idx])
                    elif ent[0] == "o":
                        ins = ent[1](eng)
                        if ent[2] in self.sig[e]:
                            ins.then_inc(sems[e], 1)
                    else:
                        ins = ent[1](eng)
                        ins.then_inc(dsems[ent[2]], 16)

            getattr(block, engobj[e])(body)


def _kmaj(w):
    K, N = w.shape
    return np.ascontiguousarray(w.reshape(K // 128, 128, N).transpose(1, 0, 2))


def _cols(v):
    return np.ascontiguousarray(v.reshape(-1, 128).T)


RET_H = 4
RET_DK = 256
RET_DV = 512
OFF_RQ, OFF_RK, OFF_RV, OFF_RG = 0, 1024, 2048, 4096
OFF_DQ, OFF_DK, OFF_DV, OFF_MQ, OFF_GATE = 6144, 7168, 8192, 9216, 10240


def _prep_weights(inp):
    out = {}
    for tag in ("ffn1", "ffn2"):
        wg = inp[tag + "_w_gate"][0]
        wu = inp[tag + "_w_up"][0]
        wd = inp[tag + "_w_down"][0]
        gu = np.empty((NG, 128, 2, 8, 256), np.float32)
        dn = np.empty((NG, 128, 2, 1024), np.float32)
        for g in range(NG):
            gu[g, :, 0] = _kmaj(wg[:, g * 256:(g + 1) * 256])
            gu[g, :, 1] = _kmaj(wu[:, g * 256:(g + 1) * 256])
            dn[g] = _kmaj(wd[g * 256:(g + 1) * 256, :])
        out[tag + "_gu"] = gu.reshape(NG, 128, 4096)
        out[tag + "_dn"] = dn.reshape(NG, 128, 2048)
        g4 = np.empty((5, 128, 8, 512), np.float32)
        u4 = np.empty((5, 128, 8, 512), np.float32)
        d4 = np.empty((5, 128, 4, 1024), np.float32)
        for g in range(5):
            g4[g] = _kmaj(wg[:, g * 512:(g + 1) * 512])
            u4[g] = _kmaj(wu[:, g * 512:(g + 1) * 512])
            d4[g] = _kmaj(wd[g * 512:(g + 1) * 512, :])
        out[tag + "_g4"] = g4.reshape(5, 128, 4096)
        out[tag + "_u4"] = u4.reshape(5, 128, 4096)
        out[tag + "_d4"] = d4.reshape(5, 128, 4096)
        out[tag + "_gu"] = np.ascontiguousarray(out[tag + "_gu"][10:11])
        out[tag + "_dn"] = np.ascontiguousarray(out[tag + "_dn"][10:11])
    w_in = inp["w_in"][0]
    rqk = np.empty((RET_H, 128, 8, 4, 128), np.float32)
    rv = np.empty((RET_H, 128, 8, 512), np.float32)
    rg = np.empty((RET_H, 128, 8, 512), np.float32)
    for h in range(RET_H):
        q = w_in[:, OFF_RQ + h * 256: OFF_RQ + (h + 1) * 256]
        k = w_in[:, OFF_RK + h * 256: OFF_RK + (h + 1) * 256]
        rqk[h, :, :, 0] = _kmaj(q[:, 0::2])
        rqk[h, :, :, 1] = _kmaj(q[:, 1::2])
        rqk[h, :, :, 2] = _kmaj(k[:, 0::2])
        rqk[h, :, :, 3] = _kmaj(k[:, 1::2])
        rv[h] = _kmaj(w_in[:, OFF_RV + h * 512: OFF_RV + (h + 1) * 512])
        rg[h] = _kmaj(w_in[:, OFF_RG + h * 512: OFF_RG + (h + 1) * 512])
    out["ret_qk"] = rqk.reshape(RET_H, 128, 4096)
    out["ret_v"] = rv.reshape(RET_H, 128, 4096)
    out["ret_g"] = rg.reshape(RET_H, 128, 4096)
    dqk = np.empty((4, 128, 8, 512), np.float32)
    dv = np.empty((4, 128, 8, 256), np.float32)
    for g in range(4):
        dqk[g, :, :, 0:256] = _kmaj(w_in[:, OFF_DQ + g * 256: OFF_DQ + (g + 1) * 256])
        dqk[g, :, :, 256:512] = _kmaj(w_in[:, OFF_DK + g * 256: OFF_DK + (g + 1) * 256])
        dv[g] = _kmaj(w_in[:, OFF_DV + g * 256: OFF_DV + (g + 1) * 256])
    out["diff_qk"] = dqk.reshape(4, 128, 4096)
    out["diff_v"] = dv.reshape(4, 128, 2048)
    mq = np.empty((2, 128, 8, 512), np.float32)
    for g in range(2):
        mq[g] = _kmaj(w_in[:, OFF_MQ + g * 512: OFF_MQ + (g + 1) * 512])
    out["mem_q"] = mq.reshape(2, 128, 4096)
    mkv = inp["mem_w_kv"][0]
    kv = np.empty((4, 128, 8, 512), np.float32)
    for g in range(4):
        kv[g] = _kmaj(mkv[:, g * 512:(g + 1) * 512])
    out["mem_kv"] = kv.reshape(4, 128, 4096)
    wo_list = [inp["ret_w_o"][0][0:1024], inp["ret_w_o"][0][1024:2048], inp["diff_w_o"][0], inp["mem_w_o"][0]]
    gidx = [0, 0, 1, 2]
    ap_w = np.empty((4, 8, 128, 2, 8, 128), np.float32)
    for a in range(4):
        for dc in range(8):
            ap_w[a, dc, :, 0] = _kmaj(wo_list[a][:, dc * 128:(dc + 1) * 128])
            gc = OFF_GATE + gidx[a] * 1024 + dc * 128
            ap_w[a, dc, :, 1] = _kmaj(w_in[:, gc: gc + 128])
    out["app_w"] = ap_w.reshape(4, 8, 128, 2048)
    wout = inp["w_out"][0]
    wo2 = np.empty((2, 2, 128, 4, 512), np.float32)
    for half in range(2):
        for nh in range(2):
            wo2[half, nh] = _kmaj(wout[half * 512:(half + 1) * 512, nh * 512:(nh + 1) * 512])
    out["w_out"] = wo2.reshape(2, 2, 128, 2048)
    cols = [
        _cols(inp["ffn1_norm"][0]), _cols(inp["mix_norm"][0]), _cols(inp["ffn2_norm"][0]),
        _cols(inp["mem_norm"][0]), _cols(inp["b_gate"][0]),
        _cols(inp["diff_subln"][0]),
    ]
    out["cols"] = np.ascontiguousarray(np.concatenate(cols, axis=1))
    out["fin"] = np.ascontiguousarray(np.broadcast_to(inp["final_norm"][0][None, :], (128, 1024)))
    rep = np.concatenate([
        inp["diff_q_norm"][0], inp["diff_k_norm"][0],
        inp["mem_q_norm"][0], inp["mem_k_norm"][0],
        inp["diff_lambda_q1"][0], inp["diff_lambda_k1"][0], inp["diff_lambda_q2"][0], inp["diff_lambda_k2"][0],
    ])
    out["rep"] = np.ascontiguousarray(np.broadcast_to(rep[None, :], (128, rep.shape[0])))
    return out


COL_FFN1, COL_MIX, COL_FFN2, COL_MEMN, COL_BG, COL_SUBLN = 0, 8, 16, 24, 32, 56
NCOLS = 57
REP_DQN, REP_DKN, REP_MQN, REP_MKN, REP_LAM = 0, 64, 128, 384, 640
NREP = 640 + 256


def _prep_consts():
    c = {}
    c["ident"] = np.eye(128, dtype=np.float32)
    gam = [1.0 - 2.0 ** (-5.0 - h) for h in range(RET_H)]
    i = np.arange(128, dtype=np.float64)
    dm = np.empty((RET_H, 128, 128), np.float64)
    cc = np.empty((128, 3 * RET_H), np.float64)
    for h, g in enumerate(gam):
        lg = math.log(g)
        cI = i[None, :]
        eI = i[:, None]
        mask = (np.floor(eI / 64) <= np.floor(cI / 64))
        dm[h] = np.exp(lg * (np.abs(cI - eI) - (cI + 1.0))) * mask
        cc[:, h] = np.exp(lg * (i + 1.0))
        cc[:, RET_H + h] = np.exp(2 * lg * (i + 1.0)) / 512.0
        cc[:, 2 * RET_H + h] = np.exp(lg * (127.0 - i))
    c["dmask"] = np.ascontiguousarray(dm.transpose(1, 0, 2)).astype(np.float32)
    c["rdec"] = cc.astype(np.float32)
    ret_inv = (1.0 / (np.float32(10000.0) ** np.linspace(0.0, 1.0, 128, dtype=np.float32))).astype(np.float32)
    rope_inv = (1.0 / (np.float32(500000.0) ** (np.arange(0, 16, 2, dtype=np.float32) / np.float32(16)))).astype(np.float32)
    c["ret_inv"] = ret_inv.reshape(128, 1)
    c["rope_inv"] = np.ascontiguousarray(np.broadcast_to(rope_inv[None, :], (128, 8))).astype(np.float32)
    c["gamma128"] = [g ** 128 for g in gam]
    return c


def build_program(wshapes, stage):
    nc = bass.Bass("TRN2", target_bir_lowering=False)
    P = Prog()
    dr = {}

    def din(name, shape, dt=F32):
        dr[name] = nc.dram_tensor(name, list(shape), dt, kind="ExternalInput").ap()
        return dr[name]

    x_d = din("x", [S, D])
    mem_d = din("mem", [256, D])
    posr_d = din("pos_rep", [128, S], I32)
    post_d = din("pos_tok", [128, NT], I32)
    for k, shp in wshapes.items():
        din(k, shp)
    y_d = nc.dram_tensor("y", [S, D], F32, kind="ExternalOutput").ap()
    gamma128 = _prep_consts()["gamma128"]

    es = ExitStack()
    with es:
        def sb(name, shape, dt=F32):
            return es.enter_context(nc.sbuf_tensor("sb_" + name, list(shape), dt))

        X = sb("X", [128, NT, D])
        HT = sb("HT", [128, 8, S], BF16)
        slots = [sb(f"slot{i}", [128, SLOT], BF16) for i in range(NSLOT)]
        ident = sb("ident", [128, 128], BF16)
        ones = sb("ones", [128, 128], BF16)
        cols = sb("cols", [128, NCOLS])
        rep = sb("rep", [128, NREP])
        ss16 = sb("ss16", [128, NT])
        rs16 = sb("rs16", [128, NT])
        ARENA_BYTES = 72 * 1024 + 512
        arena = sb("arena", [128, ARENA_BYTES // 4])
        ps = [es.enter_context(nc.psum_tensor(f"ps{i}", [128, 512], F32)) for i in range(8)]
        psb = [p.bitcast(BF16) for p in ps]

        arena_b = arena.bitcast(BF16)

        def carve_f32(off_bytes, shape):
            n = int(np.prod(shape[1:]))
            o = off_bytes // 4
            ap = arena[:, o:o + n]
            if len(shape) == 3:
                ap = ap.rearrange("p (a b) -> p a b", a=shape[1])
            elif len(shape) == 4:
                ap = ap.rearrange("p (a b c) -> p a b c", a=shape[1], b=shape[2])
            return ap

        def carve_bf(off_bytes, shape):
            n = int(np.prod(shape[1:]))
            o = off_bytes // 2
            ap = arena_b[:, o:o + n]
            if len(shape) == 3:
                ap = ap.rearrange("p (a b) -> p a b", a=shape[1])
            elif len(shape) == 4:
                ap = ap.rearrange("p (a b c) -> p a b c", a=shape[1], b=shape[2])
            return ap

        def mm(out, lhsT, rhs, start, stop, r, w):
            P.op("pe", lambda e: e.matmul(out, lhsT, rhs, start=start, stop=stop), r=r, w=w)

        def tr(out, in_, r, w):
            P.op("pe", lambda e: e.transpose(out, in_, ident[:]), r=list(r) + ["ident"], w=w)

        def act(out, in_, func, r, w, bias=0.0, scale=1.0, accum_out=None, eng="act"):
            if accum_out is not None:
                P.op("act", lambda e: e.activation(out, in_, func, bias=bias, scale=scale, accum_out=accum_out), r=r, w=w)
            else:
                P.op("act", lambda e: e.activation(out, in_, func, bias=bias, scale=scale), r=r, w=w)

        def tt(out, in0, in1, op, r, w, eng="dve"):
            P.op(eng, lambda e: e.tensor_tensor(out, in0, in1, op), r=r, w=w)

        def tsc(out, in0, s1, s2, op0, op1, r, w, eng="dve"):
            if op1 is None:
                P.op(eng, lambda e: e.tensor_scalar(out, in0, s1, None, op0), r=r, w=w)
            else:
                P.op(eng, lambda e: e.tensor_scalar(out, in0, s1, s2, op0, op1), r=r, w=w)

        def stt(out, in0, scalar, in1, op0, op1, r, w):
            P.op("dve", lambda e: e.scalar_tensor_tensor(out, in0, scalar, in1, op0, op1), r=r, w=w)

        def cp(out, in_, r, w, eng="dve"):
            if eng == "act":
                P.op("act", lambda e: e.activation(out, in_, AF.Copy), r=r, w=w)
            else:
                P.op(eng, lambda e: e.tensor_copy(out, in_), r=r, w=w)

        def recip(out, in_, r, w):
            P.op("dve", lambda e: e.reciprocal(out, in_), r=r, w=w)

        def memset(ap, val, w, eng="dve"):
            P.op(eng, lambda e: e.memset(ap, val), r=(), w=w)

        slot_ctr = [0]
        slots_static = list(slots)
        cur_slots = list(slots)

        def load_slot(parts):
            i = slot_ctr[0] % len(cur_slots)
            slot_ctr[0] += 1
            slots = list(cur_slots)
            off = 0
            for (ap, n) in parts:
                o = off
                P.dma("pool", f"slot{i}", lambda e, ap=ap, o=o, n=n, st=cur_slots[i]: e.dma_start(out=st[:, o:o + n], in_=ap, max_dma_last_dim=2048),
                      r=(), w=[("slot", i)])
                off += n
            return slots[i], ("slot", i)

        def const_load(dst, src, key, sem="c"):
            P.dma("sp", sem, lambda e: e.dma_start(out=dst, in_=src), r=(), w=[key])

        P.dma("pool", "c0", lambda e: e.dma_start(out=ident[:], in_=dr["ident"]), r=(), w=["ident"])
        const_load(cols[:], dr["cols"], "cols", "c1")
        const_load(rep[:], dr["rep"], "rep", "c2")
        for q in range(4):
            P.dma("sp", f"x{q}", lambda e, q=q: e.dma_start(
                out=X[:, q * 4:(q + 1) * 4, :],
                in_=x_d[q * 512:(q + 1) * 512, :].rearrange("(t p) d -> p t d", p=128)),
                r=(), w=[("X", t) for t in range(q * 4, q * 4 + 4)])
        memset(ones[:], 1.0, w=["ones"])

        def norm_to_HT(gcol, junk, xn):
            for t in range(NT):
                act(junk, X[:, t, :], AF.Square, r=[("X", t)], w=["junk", ("ss", t)], accum_out=ss16[:, t:t + 1])
            tsc(rs16[:], ss16[:], 1.0 / D, EPS, ALU.mult, ALU.add, r=[("ss", t) for t in range(NT)], w=["rs16"])
            act(rs16[:], rs16[:], AF.Sqrt, r=["rs16"], w=["rs16"])
            recip(rs16[:], rs16[:], r=["rs16"], w=["rs16"])
            for t in range(NT):
                xb = xn[t % 2]
                act(xb, X[:, t, :], AF.Copy, r=[("X", t), "rs16"], w=[("xn", t % 2)], scale=rs16[:, t:t + 1])
                bank = 6 + (t % 2)
                for k in range(8):
                    tr(psb[bank][:, k * 128:(k + 1) * 128], xb[:, k * 128:(k + 1) * 128],
                       r=[("xn", t % 2)], w=[("ps", bank)])
                tt(HT[:, :, t * 128:(t + 1) * 128],
                   psb[bank][:, 0:1024].rearrange("p (k c) -> p k c", k=8),
                   cols[:, gcol:gcol + 8].to_broadcast([128, 8, 128]), ALU.mult,
                   r=[("ps", bank), "cols"], w=[("HT", t // 4)])

        def ffn(tag, gcol):
            junk = carve_bf(0, [128, 1024])
            xn = [carve_bf(2048, [128, 1024]), carve_bf(4096, [128, 1024])]
            AT = [carve_bf(8192 + i * 16384, [128, 4, S]) for i in range(2)]
            SG = [carve_f32(40960 + i * 2048, [128, 512]) for i in range(2)]
            xs = [arena_b[:, (45056 + i * 8192) // 2:(45056 + i * 8192) // 2 + SLOT] for i in range(3)]
            assert 45056 + 3 * 8192 <= ARENA_BYTES
            norm_to_HT(gcol, junk, xn)
            P.barrier()
            cur_slots[:] = list(slots_static) + xs
            slot_ctr[0] = 0
            cnt = 0
            ocnt = 0
            for g in range(6):
                nch = 4 if g < 5 else 2
                if nch == 4:
                    wg_, kg_ = load_slot([(dr[tag + "_g4"][g], 4096)])
                    wu_, ku_ = load_slot([(dr[tag + "_u4"][g], 4096)])
                    wd_, kd_ = load_slot([(dr[tag + "_d4"][g], 4096)])
                    wg_v = wg_[:, 0:4096].rearrange("p (k c) -> p k c", k=8)
                    wu_v = wu_[:, 0:4096].rearrange("p (k c) -> p k c", k=8)
                    wd_v = wd_[:, 0:4096].rearrange("p (c n) -> p c n", c=4)
                else:
                    wgu, kg_ = load_slot([(dr[tag + "_gu"][0], 4096)])
                    ku_ = kg_
                    wd_, kd_ = load_slot([(dr[tag + "_dn"][0], 2048)])
                    wgu_v = wgu[:, 0:4096].rearrange("p (a k c) -> p a k c", a=2, k=8)
                    wg_v = wgu_v[:, 0, :, :]
                    wu_v = wgu_v[:, 1, :, :]
                    wd_v = wd_[:, 0:2048].rearrange("p (c n) -> p c n", c=2)
                at = AT[g % 2]
                for tb in range(NB):
                    for c in range(nch):
                        bg = (cnt % 2)
                        bu = 2 + (cnt % 2)
                        for k in range(8):
                            mm(ps[bg][:, :], wg_v[:, k, c * 128:(c + 1) * 128], HT[:, k, tb * 512:(tb + 1) * 512],
                               k == 0, k == 7, r=[kg_, ("HT", tb)], w=[("ps", bg)])
                        for k in range(8):
                            mm(ps[bu][:, :], wu_v[:, k, c * 128:(c + 1) * 128], HT[:, k, tb * 512:(tb + 1) * 512],
                               k == 0, k == 7, r=[ku_, ("HT", tb)], w=[("ps", bu)])
                        sg = SG[cnt % 2]
                        act(sg, ps[bg][:, :], AF.Silu, r=[("ps", bg)], w=[("SG", cnt % 2)])
                        tt(at[:, c, tb * 512:(tb + 1) * 512], sg, ps[bu][:, :], ALU.mult,
                           r=[("SG", cnt % 2), ("ps", bu)], w=[("AT", g % 2, tb)])
                        cnt += 1
                    for t4 in range(4):
                        t = tb * 4 + t4
                        for nh in range(2):
                            bo = 4 + (ocnt % 3)
                            ocnt += 1
                            for c in range(nch):
                                mm(ps[bo][:, :], at[:, c, t * 128:(t + 1) * 128], wd_v[:, c, nh * 512:(nh + 1) * 512],
                                   c == 0, c == nch - 1, r=[("AT", g % 2, tb), kd_], w=[("ps", bo)])
                            stt(X[:, t, nh * 512:(nh + 1) * 512], ps[bo][:, :], 0.5, X[:, t, nh * 512:(nh + 1) * 512],
                                ALU.mult, ALU.add, r=[("ps", bo), ("X", t)], w=[("X", t)])
            P.barrier()
            cur_slots[:] = list(slots_static)
            slot_ctr[0] = 0

        BT_OFF = 0
        RO_ = 32768

        def mid_bcast(ap2, reps):
            a = ap2.ap
            return bass.AP(ap2.tensor, ap2.offset, [list(a[0]), [0, reps], list(a[1])])

        def range_sin(dst, src, shift, T1, T2, T2i, keys_r, key_w):
            INV = 1.0 / TWO_PI
            C1 = 6.28125
            C2 = TWO_PI - C1
            tsc(T1, src, INV, shift * INV + 0.5, ALU.mult, ALU.add, r=keys_r, w=["rsT1"])
            cp(T2i, T1, r=["rsT1"], w=["rsT2"])
            cp(T1, T2i, r=["rsT2"], w=["rsT1"])
            stt(T2, T1, -C1, src, ALU.mult, ALU.add, r=["rsT1"] + keys_r, w=["rsT2"])
            stt(T2, T1, -C2, T2, ALU.mult, ALU.add, r=["rsT1", "rsT2"], w=["rsT2"])
            if shift != 0.0:
                tsc(T2, T2, shift, None, ALU.add, None, r=["rsT2"], w=["rsT2"])
            tsc(T1, T2, math.pi, -1e30, ALU.add, ALU.mult, r=["rsT2"], w=["rsT1"])
            tsc(T1, T1, 0.0, 1.0, ALU.max, ALU.min, r=["rsT1"], w=["rsT1"])
            stt(T2, T1, TWO_PI, T2, ALU.mult, ALU.add, r=["rsT1", "rsT2"], w=["rsT2"])
            tsc(T1, T2, -math.pi, 1e30, ALU.add, ALU.mult, r=["rsT2"], w=["rsT1"])
            tsc(T1, T1, 0.0, 1.0, ALU.max, ALU.min, r=["rsT1"], w=["rsT1"])
            stt(T2, T1, -TWO_PI, T2, ALU.mult, ALU.add, r=["rsT1", "rsT2"], w=["rsT2"])
            tsc(T2, T2, math.pi - 1e-6, -(math.pi - 1e-6), ALU.min, ALU.max, r=["rsT2"], w=["rsT2"])
            act(dst, T2, AF.Sin, r=["rsT2"], w=[key_w])

        small = sb("small", [128, 136])
        rdec = sb("rdec", [128, 12])
        retinv = sb("retinv", [128, 1])
        ropeinv = sb("ropeinv", [128, 8])
        dmask = sb("dmask", [128, RET_H, 128])
        P.dma("sp", "c5", lambda e: e.dma_start(out=retinv[:], in_=dr["ret_inv"]), r=(), w=["retinv"])
        P.dma("sp", "c6", lambda e: e.dma_start(out=rdec[:], in_=dr["rdec"]), r=(), w=["rdec"])
        P.dma("sp", "c7", lambda e: e.dma_start(out=dmask[:].rearrange("p h c -> p (h c)"), in_=dr["dmask"]), r=(), w=["dmask"])
        P.dma("sp", "c9", lambda e: e.dma_start(out=ropeinv[:], in_=dr["rope_inv"]), r=(), w=["ropeinv"])

        def apply_branch(a, gidx):
            MTh = carve_bf(RO_ + 16384, [128, 4, S])
            SGa = [carve_f32(RO_ + 32768 + i * 2048, [128, 512]) for i in range(2)]
            BT = carve_bf(BT_OFF, [128, 8, S])
            cnt = 0
            ocnt = 0
            for half in range(2):
                for dcl in range(4):
                    dc = half * 4 + dcl
                    w, kw = load_slot([(dr["app_w"][a, dc], 2048)])
                    wv = w[:, 0:2048].rearrange("p (a k c) -> p a k c", a=2, k=8)
                    for tb in range(NB):
                        bp = cnt % 2
                        bgt = 2 + cnt % 2
                        for k in range(8):
                            mm(ps[bp][:, :], wv[:, 0, k, :], BT[:, k, tb * 512:(tb + 1) * 512], k == 0, k == 7,
                               r=[kw, ("BT", tb)], w=[("ps", bp)])
                        for k in range(8):
                            mm(ps[bgt][:, :], wv[:, 1, k, :], HT[:, k, tb * 512:(tb + 1) * 512], k == 0, k == 7,
                               r=[kw, ("HT", tb)], w=[("ps", bgt)])
                        sg = SGa[cnt % 2]
                        bcol = COL_BG + gidx * 8 + dc
                        act(sg, ps[bgt][:, :], AF.Sigmoid, r=[("ps", bgt), "cols"], w=[("SGa", cnt % 2)],
                            bias=cols[:, bcol:bcol + 1])
                        tt(MTh[:, dcl, tb * 512:(tb + 1) * 512], sg, ps[bp][:, :], ALU.mult,
                           r=[("SGa", cnt % 2), ("ps", bp)], w=[("MT", tb)])
                        cnt += 1
                for nh in range(2):
                    w, kw = load_slot([(dr["w_out"][half, nh], 2048)])
                    wv = w[:, 0:2048].rearrange("p (k n) -> p k n", k=4)
                    for t in range(NT):
                        bo = 4 + ocnt % 2
                        ocnt += 1
                        for k in range(4):
                            mm(ps[bo][:, :], MTh[:, k, t * 128:(t + 1) * 128], wv[:, k, :], k == 0, k == 3,
                               r=[("MT", t // 4), kw], w=[("ps", bo)])
                        tt(X[:, t, nh * 512:(nh + 1) * 512], X[:, t, nh * 512:(nh + 1) * 512], ps[bo][:, :], ALU.add,
                           r=[("ps", bo), ("X", t)], w=[("X", t)])

        def ret_tables():
            TABC = carve_f32(RO_, [128, S])
            TABS = carve_f32(RO_ + 8192, [128, S])
            T1 = carve_f32(RO_ + 16384, [128, S])
            T2 = carve_f32(RO_ + 24576, [128, S])
            T2i = T2.bitcast(I32)
            ANG = carve_f32(RO_ + 32768, [128, S])
            ANGi = ANG.bitcast(I32)
            P.dma("sp", "c4", lambda e: e.dma_start(out=ANGi, in_=posr_d), r=(), w=["ANGi"])
            cp(T1, ANGi, r=["ANGi"], w=["posf"])
            tsc(ANG, T1, retinv[:, 0:1], None, ALU.mult, None, r=["posf", "retinv", "ANGi"], w=["ANG"])
            range_sin(TABS, ANG, 0.0, T1, T2, T2i, ["ANG"], "TAB")
            range_sin(TABC, ANG, math.pi / 2, T1, T2, T2i, ["ANG"], "TAB")

        def lam_compute():
            memset(small[:, 6:7], EPS, w=["epsc"])
            pr = small[:, 64:128]
            for i, (a_, b_) in enumerate(((0, 64), (128, 192))):
                tt(pr, rep[:, REP_LAM + a_:REP_LAM + a_ + 64], rep[:, REP_LAM + b_:REP_LAM + b_ + 64], ALU.mult,
                   r=["rep"], w=["lam_pr"])
                P.op("dve", lambda e, i=i: e.reduce_sum(small[:, 1 + i:2 + i], pr, AX.X), r=["lam_pr"], w=[("lam_s", i)])
            act(small[:, 1:3], small[:, 1:3], AF.Exp, r=[("lam_s", 0), ("lam_s", 1)], w=["lam_e"])
            tt(small[:, 3:4], small[:, 2:3], small[:, 1:2], ALU.subtract, r=["lam_e"], w=["lam_d"])
            tsc(small[:, 0:1], small[:, 3:4], -0.2, None, ALU.add, None, r=["lam_d"], w=["neglam"])

        def ret_head(h, bt_base):
            BT = carve_bf(BT_OFF, [128, 8, S])
            TABC = carve_f32(RO_, [128, S])
            TABS = carve_f32(RO_ + 8192, [128, S])
            o = RO_ + 16384
            QT = carve_bf(o, [128, 2, 512]); o += 2048
            KT = carve_bf(o, [128, 2, 512]); o += 2048
            KTOK = carve_bf(o, [128, 4, 256]); o += 2048
            VTOK = carve_bf(o, [128, 4, 512]); o += 4096
            SGt = carve_bf(o, [128, 4, 512]); o += 4096
            Sf = carve_f32(o, [128, 2, 512]); o += 4096
            Sbf = carve_bf(o, [128, 2, 512]); o += 2048
            SCT = carve_bf(o, [128, 128]); o += 256
            RO = [carve_bf(o, [128, 512]), carve_bf(o + 1024, [128, 512])]; o += 2048
            T1 = carve_f32(o, [128, 512]); o += 2048
            T2 = ps[7][:, :]
            assert o <= ARENA_BYTES, o
            wqk, kqk = load_slot([(dr["ret_qk"][h], 4096)])
            wv, kv_ = load_slot([(dr["ret_v"][h], 4096)])
            wg, kg = load_slot([(dr["ret_g"][h], 4096)])
            wqk_v = wqk[:, :].rearrange("p (k j c) -> p k j c", k=8, j=4)
            wv_v = wv[:, :].rearrange("p (k n) -> p k n", k=8)
            wg_v = wg[:, :].rearrange("p (k n) -> p k n", k=8)
            g128 = float(gamma128[h])
            pending = [None]

            def flush_ro(ri, t, tb):
                for fc in range(4):
                    tr(psb[7][:, fc * 128:(fc + 1) * 128], RO[ri][:, fc * 128:(fc + 1) * 128], r=[("RO", ri)], w=[("ps", 7)])
                cp(BT[:, bt_base:bt_base + 4, t * 128:(t + 1) * 128],
                   psb[7][:, 0:512].rearrange("p (f c) -> p f c", f=4), r=[("ps", 7)], w=[("BT", tb)], eng="act")
            for tb in range(NB):
                tbs = slice(tb * 512, (tb + 1) * 512)
                for (j0, b0) in ((0, 0), (2, 4)):
                    for jj in range(2):
                        for k in range(8):
                            mm(ps[b0 + jj][:, :], wqk_v[:, k, j0 + jj, :], HT[:, k, tbs], k == 0, k == 7,
                               r=[kqk, ("HT", tb)], w=[("ps", b0 + jj)])
                for ci in range(4):
                    t = tb * 4 + ci
                    ts_ = slice(t * 128, (t + 1) * 128)
                    bv = 2 if ci % 2 == 0 else 6
                    for k in range(8):
                        mm(ps[bv][:, :], HT[:, k, ts_], wv_v[:, k, :], k == 0, k == 7, r=[kv_, ("HT", tb)], w=[("ps", bv)])
                    act(VTOK[:, ci, :], ps[bv][:, :], AF.Copy, r=[("ps", bv)], w=[("VTOK", ci)])
                C = TABC[:, tbs]
                Sn = TABS[:, tbs]
                for (b0, dst, scale, key) in ((0, QT, 1.0, "QT"), (4, KT, 1.0 / 16.0, "KT")):
                    pe_, po_ = ps[b0], ps[b0 + 1]
                    stt(T1, pe_[:, :], scale, C, ALU.mult, ALU.mult, r=[("ps", b0), "TAB"], w=["T1"])
                    stt(T2, po_[:, :], scale, Sn, ALU.mult, ALU.mult, r=[("ps", b0 + 1), "TAB"], w=[("ps", 7)])
                    tt(dst[:, 0, :], T1, T2, ALU.subtract, r=["T1", ("ps", 7)], w=[key])
                    stt(T1, po_[:, :], scale, C, ALU.mult, ALU.mult, r=[("ps", b0 + 1), "TAB"], w=["T1"])
                    stt(T2, pe_[:, :], scale, Sn, ALU.mult, ALU.mult, r=[("ps", b0), "TAB"], w=[("ps", 7)])
                    tt(dst[:, 1, :], T1, T2, ALU.add, r=["T1", ("ps", 7)], w=[key])
                for ci in range(4):
                    t = tb * 4 + ci
                    ts_ = slice(t * 128, (t + 1) * 128)
                    bg_ = 3 if ci % 2 == 0 else 6
                    for k in range(8):
                        mm(ps[bg_][:, :], HT[:, k, ts_], wg_v[:, k, :], k == 0, k == 7, r=[kg, ("HT", tb)], w=[("ps", bg_)])
                    act(SGt[:, ci, :], ps[bg_][:, :], AF.Silu, r=[("ps", bg_)], w=[("SGt", ci)])
                for ci in range(4):
                    cs = slice(ci * 128, (ci + 1) * 128)
                    for c in range(2):
                        tr(psb[4][:, c * 128:(c + 1) * 128], KT[:, c, cs], r=["KT"], w=[("ps", 4)])
                    tsc(KTOK[:, ci, :], psb[4][:, 0:256], rdec[:, 8 + h:9 + h], None, ALU.mult, None,
                        r=[("ps", 4), "rdec"], w=[("KTOK", ci)])
                for ci in range(4):
                    n = tb * 4 + ci
                    t = n
                    cs = slice(ci * 128, (ci + 1) * 128)
                    ts_ = slice(t * 128, (t + 1) * 128)
                    rob = RO[n % 2]
                    for c in range(2):
                        mm(ps[5][:, 0:128], KT[:, c, cs], QT[:, c, cs], c == 0, c == 1, r=["KT", "QT"], w=[("ps", 5)])
                    tt(SCT, ps[5][:, 0:128], dmask[:, h, :], ALU.mult, r=[("ps", 5), "dmask"], w=["SCT"])
                    if n < 15:
                        for c in range(2):
                            mm(ps[c][:, :], KTOK[:, ci, c * 128:(c + 1) * 128], VTOK[:, ci, :], True, True,
                               r=[("KTOK", ci), ("VTOK", ci)], w=[("ps", c)])
                    mm(ps[6][:, :], SCT, VTOK[:, ci, :], True, n == 0, r=["SCT", ("VTOK", ci)], w=[("ps", 6)])
                    if n > 0:
                        for c in range(2):
                            mm(ps[6][:, :], QT[:, c, cs], Sbf[:, c, :], False, c == 1, r=["QT", ("Sbf", c)], w=[("ps", 6)])
                    if n < 15:
                        for c in range(2):
                            if n == 0:
                                cp(Sf[:, c, :], ps[c][:, :], r=[("ps", c)], w=[("Sf", c)])
                            else:
                                stt(Sf[:, c, :], Sf[:, c, :], g128, ps[c][:, :], ALU.mult, ALU.add,
                                    r=[("ps", c), ("Sf", c)], w=[("Sf", c)])
                            cp(Sbf[:, c, :], Sf[:, c, :], r=[("Sf", c)], w=[("Sbf", c)], eng="act")
                    act(T1, ps[6][:, :], AF.Square, r=[("ps", 6)], w=["T1", "rss"], accum_out=small[:, 4:5])
                    tt(small[:, 5:6], small[:, 4:5], rdec[:, 4 + h:5 + h], ALU.mult, r=["rss", "rdec"], w=["rf"])
                    tsc(small[:, 5:6], small[:, 5:6], EPS, None, ALU.add, None, r=["rf"], w=["rf"])
                    act(small[:, 5:6], small[:, 5:6], AF.Sqrt, r=["rf"], w=["rf"])
                    recip(small[:, 5:6], small[:, 5:6], r=["rf"], w=["rf"])
                    tt(small[:, 5:6], small[:, 5:6], rdec[:, h:h + 1], ALU.mult, r=["rf", "rdec"], w=["rf"])
                    stt(rob, ps[6][:, :], small[:, 5:6], SGt[:, ci, :], ALU.mult, ALU.mult,
                        r=[("ps", 6), "rf", ("SGt", ci)], w=[("RO", n % 2)])
                    if pending[0] is not None:
                        flush_ro(*pending[0])
                    pending[0] = (n % 2, t, tb)
            flush_ro(*pending[0])

        def diff_tables():
            DS8 = carve_f32(RO_ + 29696, [128, 16, 8])
            DC8 = carve_f32(RO_ + 30208, [128, 16, 8])
            DSS = carve_f32(RO_ + 28672, [128, 16, 16])
            DCC = carve_f32(RO_ + 33792, [128, 16, 16])
            A = carve_f32(RO_ + 34816, [128, 16, 8])
            T1 = carve_f32(RO_ + 34816 + 512, [128, 16, 8])
            T2 = carve_f32(RO_ + 34816 + 1024, [128, 16, 8])
            T2i = T2.bitcast(I32)
            PI_ = carve_f32(RO_ + 34816 + 1536, [128, 16]).bitcast(I32)
            PF = carve_f32(RO_ + 34816 + 1600, [128, 16])
            P.dma("sp", "c8", lambda e: e.dma_start(out=PI_, in_=post_d), r=(), w=["PI"])
            cp(PF, PI_, r=["PI"], w=["PF"])
            tt(A, PF.to_broadcast([128, 16, 8]),
               mid_bcast(ropeinv[:, 0:8], 16), ALU.mult, r=["PF", "ropeinv"], w=["DA"])
            range_sin(DS8, A, 0.0, T1, T2, T2i, ["DA"], "DT8")
            range_sin(DC8, A, math.pi / 2, T1, T2, T2i, ["DA"], "DT8")
            cp(DCC[:, :, 0:8], DC8, r=["DT8"], w=["DTAB"])
            cp(DCC[:, :, 8:16], DC8, r=["DT8"], w=["DTAB"])
            cp(DSS[:, :, 8:16], DS8, r=["DT8"], w=["DTAB"])
            tsc(DSS[:, :, 0:8], DS8, -1.0, None, ALU.mult, None, r=["DT8"], w=["DTAB"])

        def diff_group(gi):
            BT = carve_bf(BT_OFF, [128, 8, S])
            QTd = carve_bf(RO_, [128, 2, S])
            KTd = carve_bf(RO_ + 8192, [128, 2, S])
            VTd = carve_bf(RO_ + 16384, [128, 16, 256])
            PT = [[carve_bf(RO_ + 24576 + (b * 2 + r_) * 1024, [128, 512]) for r_ in range(2)] for b in range(2)]
            DSS = carve_f32(RO_ + 28672, [128, 16, 16])
            SQ = carve_f32(RO_ + 30720, [128, 512])
            QKb = carve_bf(RO_ + 32768, [128, 512])
            DCC = carve_f32(RO_ + 33792, [128, 16, 16])
            A0 = carve_f32(RO_ + 34816, [128, 512])
            A1 = carve_f32(RO_ + 36864, [128, 512])
            RD = carve_f32(RO_ + 38912, [128, 512])
            QKf = A1
            XR = RD[:, 0:128].rearrange("p (g d) -> p g d", g=8)
            RA = RD[:, 128:256].rearrange("p (g d) -> p g d", g=8)
            RB = RD[:, 256:384].rearrange("p (g d) -> p g d", g=8)
            assert RO_ + 40960 <= ARENA_BYTES
            SQb = SQ.bitcast(BF16)[:, 0:512]
            wqk, kqk = load_slot([(dr["diff_qk"][gi], 4096)])
            wv, kv_ = load_slot([(dr["diff_v"][gi], 2048)])
            wqk_v = wqk[:, :].rearrange("p (k n) -> p k n", k=8)
            wv_v = wv[:, 0:2048].rearrange("p (k n) -> p k n", k=8)
            QKf3 = QKf.rearrange("p (g d) -> p g d", g=8)
            QKb3 = QKb.rearrange("p (g d) -> p g d", g=8)
            gain_ap = bass.AP(rep[:, 0:1].tensor, REP_DQN, [[NREP, 128], [64, 2], [0, 4], [1, 64]])
            gain16_ap = bass.AP(rep[:, 0:1].tensor, REP_DQN, [[NREP, 128], [64, 2], [0, 4], [1, 16]])

            def dproj(t):
                ts_ = slice(t * 128, (t + 1) * 128)
                bq = t % 2
                bv = 2 + t % 2
                for k in range(8):
                    mm(ps[bq][:, :], HT[:, k, ts_], wqk_v[:, k, :], k == 0, k == 7, r=[kqk, ("HT", t // 4)], w=[("ps", bq)])
                for k in range(8):
                    mm(ps[bv][:, 0:256], HT[:, k, ts_], wv_v[:, k, :], k == 0, k == 7, r=[kv_, ("HT", t // 4)], w=[("ps", bv)])

            def actpre(t):
                bq = t % 2
                bv = 2 + t % 2
                act(VTd[:, t, :], ps[bv][:, 0:256], AF.Copy, r=[("ps", bv)], w=[("VTd", t)])
                act(SQ, ps[bq][:, :], AF.Square, r=[("ps", bq)], w=["SQ"])

            dproj(0)
            actpre(0)
            for t in range(NT):
                ts_ = slice(t * 128, (t + 1) * 128)
                bq = t % 2
                btr = 4 + t % 2
                if t + 1 < NT:
                    dproj(t + 1)
                P.op("dve", lambda e: e.reduce_sum(small[:, 8:16], SQ.rearrange("p (g d) -> p g d", g=8), AX.X),
                     r=["SQ"], w=["ss8"])
                tsc(small[:, 8:16], small[:, 8:16], 1.0 / 64.0, EPS, ALU.mult, ALU.add, r=["ss8"], w=["ss8"])
                act(small[:, 8:16], small[:, 8:16], AF.Sqrt, r=["ss8"], w=["ss8"])
                recip(small[:, 8:16], small[:, 8:16], r=["ss8"], w=["ss8"])
                tt(QKf3, ps[bq][:, :].rearrange("p (g d) -> p g d", g=8), small[:, 8:16].to_broadcast([128, 8, 64]), ALU.mult,
                   r=[("ps", bq), "ss8"], w=["A1"])
                if t + 1 < NT:
                    actpre(t + 1)
                tt(QKb.rearrange("p (a b d) -> p a b d", a=2, b=4), QKf.rearrange("p (a b d) -> p a b d", a=2, b=4), gain_ap,
                   ALU.mult, r=["A1", "rep"], w=["QKb"])
                tt(XR.rearrange("p (a b) d -> p a b d", a=2), QKf.rearrange("p (a b d) -> p a b d", a=2, b=4)[:, :, :, 0:16], gain16_ap,
                   ALU.mult, r=["A1", "rep"], w=["RD"])
                ccb = mid_bcast(DCC[:, t, :], 8)
                tt(RA, XR, ccb, ALU.mult, r=["RD", "DTAB"], w=["RD"])
                tt(RB[:, :, 0:8], XR[:, :, 8:16], mid_bcast(DSS[:, t, 0:8], 8), ALU.mult, r=["RD", "DTAB"], w=["RD"])
                tt(RB[:, :, 8:16], XR[:, :, 0:8], mid_bcast(DSS[:, t, 8:16], 8), ALU.mult, r=["RD", "DTAB"], w=["RD"])
                tt(QKb3[:, :, 0:16], RA, RB, ALU.add, r=["RD", "QKb"], w=["QKb"])
                for j in range(4):
                    tr(psb[btr][:, j * 128:(j + 1) * 128], QKb[:, j * 128:(j + 1) * 128], r=["QKb"], w=[("ps", btr)])
                cp(QTd[:, :, ts_], psb[btr][:, 0:256].rearrange("p (h c) -> p h c", h=2), r=[("ps", btr)], w=[("QTd", t // 4)], eng="act")
                cp(KTd[:, :, ts_], psb[btr][:, 256:512].rearrange("p (h c) -> p h c", h=2), r=[("ps", btr)], w=[("KTd", t // 4)], eng="act")
            steps = []
            for hh in range(2):
                for qb in range(NB):
                    nkt = 4 * (qb + 1)
                    for kt in range(nkt):
                        steps.append((hh, qb, kt, nkt))

            def emit_scores(i):
                hh, qb, kt, nkt = steps[i]
                c0 = max(0, kt - 4 * qb) * 128
                b = i % 2
                P.self_wait("pe")
                for r_ in range(2):
                    pr = slice(r_ * 64, (r_ + 1) * 64)
                    sbank = b * 2 + r_
                    mm(ps[sbank][:, c0:512], KTd[pr, hh, kt * 128:(kt + 1) * 128],
                       QTd[pr, hh, qb * 512 + c0:(qb + 1) * 512], True, True,
                       r=[("KTd", kt // 4), ("QTd", qb)], w=[("ps", sbank)])
                P.self_wait("pe")

            def emit_exp(i):
                hh, qb, kt, nkt = steps[i]
                c0 = max(0, kt - 4 * qb) * 128
                b = i % 2
                for r_ in range(2):
                    sbank = b * 2 + r_
                    pt = PT[b][r_]
                    act(pt[:, c0:512], ps[sbank][:, c0:512], AF.Exp, r=[("ps", sbank)], w=[("PT", b, r_)], scale=0.125)
                    if kt >= 4 * qb:
                        memset(pt[64:128, c0:c0 + 64], 0.0, w=[("PT", b, r_)], eng="dve")

            def emit_pv(i):
                hh, qb, kt, nkt = steps[i]
                c0 = max(0, kt - 4 * qb) * 128
                b = i % 2
                for r_ in range(2):
                    pt = PT[b][r_]
                    mm(ps[4 + r_][:, c0:512], VTd[:, kt, hh * 128:(hh + 1) * 128], pt[:, c0:512], kt == 0, kt == nkt - 1,
                       r=[("VTd", kt), ("PT", b, r_)], w=[("ps", 4 + r_)])
                    mm(ps[6 + r_][:, c0:512], ones[:, :], pt[:, c0:512], kt == 0, kt == nkt - 1,
                       r=["ones", ("PT", b, r_)], w=[("ps", 6 + r_)])

            def finalize(hh, qb):
                h = 2 * gi + hh
                qs = slice(qb * 512, (qb + 1) * 512)
                RDb = SQ
                SQq = QKb
                act(RD, ps[6][:, :], AF.Ln, r=[("ps", 6)], w=["RD"])
                act(RD, RD, AF.Exp, r=["RD"], w=["RD"], scale=-1.0)
                act(RDb, ps[7][:, :], AF.Ln, r=[("ps", 7)], w=["SQ"])
                act(RDb, RDb, AF.Exp, r=["SQ"], w=["SQ"], scale=-1.0)
                tt(A0, ps[4][:, :], RD, ALU.mult, r=[("ps", 4), "RD"], w=["A0"])
                tt(A1, ps[5][:, :], RDb, ALU.mult, r=[("ps", 5), "SQ"], w=["A1"])
                stt(A0, A1, small[:, 0:1], A0, ALU.mult, ALU.add, r=["A0", "A1", "neglam"], w=["A0"])
                act(SQq, A0, AF.Square, r=["A0"], w=["QKb"])
                mm(ps[6][:, :], ones[:, :], SQq, True, True, r=["ones", "QKb"], w=[("ps", 6)])
                act(RD, ps[6][:, :], AF.Ln, r=[("ps", 6), "epsc"], w=["RD"], scale=1.0 / 128.0, bias=small[:, 6:7])
                act(RD, RD, AF.Exp, r=["RD"], w=["RD"], scale=-0.5)
                tt(A0, A0, RD, ALU.mult, r=["A0", "RD"], w=["A0"])
                tsc(BT[:, h, qs], A0, cols[:, COL_SUBLN:COL_SUBLN + 1], 0.8, ALU.mult, ALU.mult,
                    r=["A0", "cols"], w=[("BT", qb)])

            emit_scores(0)
            for i in range(len(steps)):
                if i + 1 < len(steps):
                    emit_scores(i + 1)
                emit_exp(i)
                emit_pv(i)
                hh, qb, kt, nkt = steps[i]
                if kt == nkt - 1:
                    finalize(hh, qb)

        def group_rms_pre(psrc, G, d, SQ, keyp, par=0):
            n = G * d
            act(SQ[:, par * 512:par * 512 + n], psrc, AF.Square, r=[keyp], w=[("SQ", par)])

        def group_rms_main(psrc, G, d, gain_off, WKf, SQ, QBb, keyp, par=0, mid=None):
            n = G * d
            o = par * 512
            c0 = 16 + 2 * par
            sm = small[:, c0:c0 + G]
            P.op("dve", lambda e: e.reduce_sum(sm, SQ[:, o:o + n].rearrange("p (g d) -> p g d", g=G), AX.X),
                 r=[("SQ", par)], w=[("ssg", par)])
            tsc(sm, sm, 1.0 / d, EPS, ALU.mult, ALU.add, r=[("ssg", par)], w=[("ssg", par)])
            act(sm, sm, AF.Sqrt, r=[("ssg", par)], w=[("ssg", par)])
            recip(sm, sm, r=[("ssg", par)], w=[("ssg", par)])
            tt(WKf[:, o:o + n].rearrange("p (g d) -> p g d", g=G), psrc.rearrange("p (g d) -> p g d", g=G),
               sm.to_broadcast([128, G, d]), ALU.mult, r=[keyp, ("ssg", par)], w=[("WKf", par)])
            if mid is not None:
                mid()
            tt(QBb[:, o:o + n].rearrange("p (g d) -> p g d", g=G), WKf[:, o:o + n].rearrange("p (g d) -> p g d", g=G),
               mid_bcast(rep[:, gain_off:gain_off + d], G), ALU.mult, r=[("WKf", par), "rep"], w=[("QBb", par)])

        def group_rms(psrc, G, d, gain_off, WKf, SQ, QBb, keyp):
            group_rms_pre(psrc, G, d, SQ, keyp, 0)
            group_rms_main(psrc, G, d, gain_off, WKf, SQ, QBb, keyp, 0)

        def mem_phase():
            BT = carve_bf(BT_OFF, [128, 8, S])
            MQT = carve_bf(RO_, [128, 4, S])
            MKT = carve_bf(RO_ + 16384, [128, 8, 256])
            MV = carve_bf(RO_ + 20480, [128, 2, 1024])
            MNT = carve_bf(RO_ + 24576, [128, 8, 256])
            PTm = [[carve_bf(RO_ + 28672 + m * 1024, [128, 512]) for m in range(2)] for b in range(2)]
            WKf = carve_f32(RO_ + 30720, [128, 1024])
            SQ = carve_f32(RO_ + 34816, [128, 1024])
            QBb = carve_bf(RO_ + 38912, [128, 1024])
            assert RO_ + 40960 <= ARENA_BYTES
            RD = SQ[:, 0:512]
            for mt in range(2):
                P.dma("sp", "mem", lambda e, mt=mt: e.dma_start(out=WKf, in_=mem_d[mt * 128:(mt + 1) * 128, :]), r=(), w=[("WKf", 0), ("WKf", 1)])
                act(SQ, WKf, AF.Square, r=[("WKf", 0), ("WKf", 1)], w=[("SQ", 0), ("SQ", 1), "mss"], accum_out=small[:, 24:25])
                tsc(small[:, 24:25], small[:, 24:25], 1.0 / D, EPS, ALU.mult, ALU.add, r=["mss"], w=["mss"])
                act(small[:, 24:25], small[:, 24:25], AF.Sqrt, r=["mss"], w=["mss"])
                recip(small[:, 24:25], small[:, 24:25], r=["mss"], w=["mss"])
                act(QBb, WKf, AF.Copy, r=[("WKf", 0), ("WKf", 1), "mss"], w=[("QBb", 0), ("QBb", 1)], scale=small[:, 24:25])
                for k in range(8):
                    tr(psb[0][:, k * 128:(k + 1) * 128], QBb[:, k * 128:(k + 1) * 128], r=[("QBb", 0), ("QBb", 1)], w=[("ps", 0)])
                tt(MNT[:, :, mt * 128:(mt + 1) * 128], psb[0][:, 0:1024].rearrange("p (k c) -> p k c", k=8),
                   cols[:, COL_MEMN:COL_MEMN + 8].to_broadcast([128, 8, 128]), ALU.mult, r=[("ps", 0), "cols"], w=["MNT"])
            for g in range(4):
                w, kw = load_slot([(dr["mem_kv"][g], 4096)])
                wv = w[:, :].rearrange("p (k n) -> p k n", k=8)
                for mt in range(2):
                    for k in range(8):
                        mm(ps[1][:, :], MNT[:, k, mt * 128:(mt + 1) * 128], wv[:, k, :], k == 0, k == 7, r=["MNT", kw], w=[("ps", 1)])
                    if g < 2:
                        group_rms(ps[1][:, :], 2, 256, REP_MKN, WKf, SQ, QBb, ("ps", 1))
                        for j in range(4):
                            tr(psb[2][:, j * 128:(j + 1) * 128], QBb[:, j * 128:(j + 1) * 128], r=[("QBb", 0)], w=[("ps", 2)])
                        cp(MKT[:, g * 4:(g + 1) * 4, mt * 128:(mt + 1) * 128], psb[2][:, 0:512].rearrange("p (j c) -> p j c", j=4),
                           r=[("ps", 2)], w=["MKT"], eng="act")
                    else:
                        act(MV[:, mt, (g - 2) * 512:(g - 1) * 512], ps[1][:, :], AF.Copy, r=[("ps", 1)], w=["MV"])
            pcnt = 0
            for half in range(2):
                w, kw = load_slot([(dr["mem_q"][half], 4096)])
                wv = w[:, :].rearrange("p (k n) -> p k n", k=8)
                def mproj(t):
                    ts_ = slice(t * 128, (t + 1) * 128)
                    bq = t % 2
                    for k in range(8):
                        mm(ps[bq][:, :], HT[:, k, ts_], wv[:, k, :], k == 0, k == 7, r=[("HT", t // 4), kw], w=[("ps", bq)])

                mproj(0)
                group_rms_pre(ps[0][:, :], 2, 256, SQ, ("ps", 0), 0)
                for t in range(NT):
                    ts_ = slice(t * 128, (t + 1) * 128)
                    bq = t % 2
                    par = t % 2
                    btr = 2 + t % 2
                    if t + 1 < NT:
                        mproj(t + 1)

                    def pre_next(t=t):
                        if t + 1 < NT:
                            group_rms_pre(ps[(t + 1) % 2][:, :], 2, 256, SQ, ("ps", (t + 1) % 2), (t + 1) % 2)

                    group_rms_main(ps[bq][:, :], 2, 256, REP_MQN, WKf, SQ, QBb, ("ps", bq), par, mid=pre_next)
                    for j in range(4):
                        tr(psb[btr][:, j * 128:(j + 1) * 128], QBb[:, par * 512 + j * 128:par * 512 + (j + 1) * 128],
                           r=[("QBb", par)], w=[("ps", btr)])
                    cp(MQT[:, :, ts_], psb[btr][:, 0:512].rearrange("p (j c) -> p j c", j=4), r=[("ps", btr)], w=[("MQT", t // 4)], eng="act")
                for hh in range(2):
                    h = half * 2 + hh
                    for qb in range(NB):
                        qs = slice(qb * 512, (qb + 1) * 512)
                        b = pcnt % 2
                        pcnt += 1
                        for mt in range(2):
                            for c in range(2):
                                mm(ps[3 + mt][:, :], MKT[:, h * 2 + c, mt * 128:(mt + 1) * 128], MQT[:, hh * 2 + c, qs], c == 0, c == 1,
                                   r=["MKT", ("MQT", qb)], w=[("ps", 3 + mt)])
                            act(PTm[b][mt], ps[3 + mt][:, :], AF.Exp, r=[("ps", 3 + mt)], w=[("PTm", 0, mt)], scale=1.0 / 16.0)
                        for c in range(2):
                            for mt in range(2):
                                mm(ps[5 + c][:, :], MV[:, mt, h * 256 + c * 128:h * 256 + (c + 1) * 128], PTm[b][mt], mt == 0, mt == 1,
                                   r=["MV", ("PTm", 0, mt)], w=[("ps", 5 + c)])
                        for mt in range(2):
                            mm(ps[7][:, :], ones[:, :], PTm[b][mt], mt == 0, mt == 1, r=["ones", ("PTm", 0, mt)], w=[("ps", 7)])
                        act(RD, ps[7][:, :], AF.Ln, r=[("ps", 7)], w=[("SQ", 0)])
                        act(RD, RD, AF.Exp, r=[("SQ", 0)], w=[("SQ", 0)], scale=-1.0)
                        for c in range(2):
                            tt(BT[:, h * 2 + c, qs], ps[5 + c][:, :], RD, ALU.mult, r=[("ps", 5 + c), ("SQ", 0)], w=[("BT", qb)])

        def mixer():
            junk = carve_bf(0, [128, 1024])
            xn = [carve_bf(2048, [128, 1024]), carve_bf(4096, [128, 1024])]
            norm_to_HT(COL_MIX, junk, xn)
            P.barrier()
            do_ret = stage in (2, 21) or stage >= 3
            do_diff = stage in (2, 22) or stage >= 3
            do_mem = stage in (2, 23) or stage >= 3
            if stage >= 20:
                do_ret, do_diff, do_mem = stage == 21, stage == 22, stage == 23
            if do_ret:
                ret_tables()
                P.barrier()
                for pair in range(2):
                    for hh in range(2):
                        ret_head(pair * 2 + hh, hh * 4)
                    P.barrier()
                    apply_branch(pair, 0)
                    P.barrier()
            if do_diff:
                lam_compute()
                diff_tables()
                P.barrier()
                for gi in range(4):
                    diff_group(gi)
                P.barrier()
                apply_branch(2, 1)
                P.barrier()
            if do_mem:
                mem_phase()
                P.barrier()
                apply_branch(3, 2)
                P.barrier()

        def final_out(do_norm=True):
            junk = carve_bf(0, [128, 1024])
            OB = [carve_f32(4096 + i * 4096, [128, 1024]) for i in range(2)]
            FIN = carve_f32(12288, [128, 1024])
            if do_norm:
                P.dma("sp", "c3", lambda e: e.dma_start(out=FIN, in_=dr["fin"]), r=(), w=["FIN"])
                for t in range(NT):
                    act(junk, X[:, t, :], AF.Square, r=[("X", t)], w=["junk", ("ss", t)], accum_out=ss16[:, t:t + 1])
                tsc(rs16[:], ss16[:], 1.0 / D, EPS, ALU.mult, ALU.add, r=[("ss", t) for t in range(NT)], w=["rs16"])
                act(rs16[:], rs16[:], AF.Sqrt, r=["rs16"], w=["rs16"])
                recip(rs16[:], rs16[:], r=["rs16"], w=["rs16"])
            for t in range(NT):
                ob = OB[t % 2]
                if do_norm:
                    stt(ob, X[:, t, :], rs16[:, t:t + 1], FIN, ALU.mult, ALU.mult,
                        r=[("X", t), "rs16", "FIN"], w=[("OB", t % 2)])
                else:
                    cp(ob, X[:, t, :], r=[("X", t)], w=[("OB", t % 2)])
                P.dma("sp", f"y{t % 2}", lambda e, ob=ob, t=t: e.dma_start(out=y_d[t * 128:(t + 1) * 128, :], in_=ob),
                      r=[("OB", t % 2)], w=[("Y", t)])

        ffn("ffn1", COL_FFN1)
        P.barrier()
        if stage >= 2:
            mixer()
            P.barrier()
        if stage >= 3 and stage < 20:
            ffn("ffn2", COL_FFN2)
            P.barrier()
        final_out(do_norm=(stage >= 3 and stage < 20))
        P.barrier()
        print('PROG check:', P.check())
        P.emit(nc, es)
    return nc


_CACHE = {}


def kernel(**inputs):
    inp = {k: np.asarray(v) for k, v in inputs.items()}
    B = inp["x"].shape[0]
    W = _prep_weights(inp)
    C = _prep_consts()
    shared = dict(W)
    shared["ident"] = C["ident"]
    shared["dmask"] = C["dmask"].reshape(128, RET_H * 128)
    shared["rdec"] = C["rdec"]
    shared["ret_inv"] = C["ret_inv"]
    shared["rope_inv"] = C["rope_inv"]
    wshapes = {k: v.shape for k, v in shared.items()}
    key = ("nc", STAGE)
    if key not in _CACHE:
        _CACHE[key] = build_program(wshapes, STAGE)
    nc = _CACHE[key]
    in_maps = []
    for b in range(B):
        m = dict(shared)
        m["x"] = np.ascontiguousarray(inp["x"][b])
        m["mem"] = np.ascontiguousarray(inp["mem"][b])
        pos = inp["positions"][b].astype(np.int32)
        m["pos_rep"] = np.ascontiguousarray(np.broadcast_to(pos[None, :], (128, S)))
        m["pos_tok"] = np.ascontiguousarray(pos.reshape(NT, 128).T)
        in_maps.append(m)
    res = run_bass_kernel_spmd(nc, in_maps, core_ids=list(range(B)))
    out = np.stack([np.asarray(r["y"]) for r in res.results], axis=0)
    return out.astype(np.float32, copy=False)
```

```python
import os
import math
import numpy as np
from contextlib import ExitStack
import concourse.bass as bass
import concourse.mybir as mybir
from concourse.bass_utils import run_bass_kernel_spmd

F32 = mybir.dt.float32
BF16 = mybir.dt.bfloat16
I32 = mybir.dt.int32
AF = mybir.ActivationFunctionType
ALU = mybir.AluOpType
AX = mybir.AxisListType

S = 2048
D = 1024
NT = 16
NB = 4
DFF = 2816
NG = 11
EPS = 1e-6
SLOT = 4096
NSLOT = 4
TWO_PI = 2.0 * math.pi

STAGE = int(os.environ.get("MK_STAGE", "3"))


class Prog:
    ENG = ("pe", "act", "dve", "pool", "sp")

    def __init__(self):
        self.streams = {e: [] for e in self.ENG}
        self.nops = {e: 0 for e in self.ENG}
        self.known = {e: {} for e in self.ENG}
        self.res = {}
        self.sig = {e: set() for e in self.ENG}
        self.dcount = {}

    def _deps(self, eng, reads, writes):
        need = {}

        def add(c):
            sk, idx, clock = c
            if sk == "pe" and eng == "pe":
                return
            if self.known[eng].get(sk, 0) >= idx:
                return
            cur = need.get(sk)
            if cur is None or cur[0] < idx:
                need[sk] = (idx, clock)

        for k in reads:
            st = self.res.get(k)
            if st is not None and st[0] is not None:
                add(st[0])
        for k in writes:
            st = self.res.get(k)
            if st is not None:
                if st[0] is not None:
                    add(st[0])
                for sk, (idx, clock) in st[1].items():
                    add((sk, idx, clock))
        kn = self.known[eng]
        for sk, (idx, clock) in need.items():
            if kn.get(sk, 0) >= idx:
                continue
            self.streams[eng].append(("w", sk, idx))
            if sk in self.sig:
                self.sig[sk].add(idx)
            for a, b in clock.items():
                if kn.get(a, 0) < b:
                    kn[a] = b
            kn[sk] = max(kn.get(sk, 0), idx)

    def _record(self, comp, reads, writes):
        sk, idx, clock = comp
        for k in writes:
            self.res[k] = [comp, {}]
        for k in reads:
            st = self.res.get(k)
            if st is None:
                st = [None, {}]
                self.res[k] = st
            cur = st[1].get(sk)
            if cur is None or cur[0] < idx:
                st[1][sk] = (idx, clock)

    def op(self, eng, fn, r=(), w=()):
        self._deps(eng, r, w)
        self.nops[eng] += 1
        idx = self.nops[eng]
        self.streams[eng].append(("o", fn, idx))
        clock = dict(self.known[eng])
        clock[eng] = idx
        self._record((eng, idx, clock), r, w)

    def dma(self, issuer, sem, fn, r=(), w=()):
        self._deps(issuer, r, w)
        sk = ("d", sem)
        self.dcount[sk] = self.dcount.get(sk, 0) + 1
        idx = self.dcount[sk]
        self.streams[issuer].append(("d", fn, sk))
        clock = dict(self.known[issuer])
        clock[sk] = idx
        self._record((sk, idx, clock), r, w)

    def barrier(self):
        pend = {}
        for st in self.res.values():
            if st[0] is not None:
                sk, idx, _ = st[0]
                pend[sk] = max(pend.get(sk, 0), idx)
            for sk, (idx, _) in st[1].items():
                pend[sk] = max(pend.get(sk, 0), idx)
        for e in self.ENG:
            kn = self.known[e]
            for sk, idx in pend.items():
                if kn.get(sk, 0) >= idx:
                    continue
                if sk == e and e == "pe":
                    pass
                self.streams[e].append(("w", sk, idx))
                if sk in self.sig:
                    self.sig[sk].add(idx)
                kn[sk] = idx
        self.res = {}

    def check(self):
        sigval = {}
        for e in self.ENG:
            m = {}
            c = 0
            for i in range(1, self.nops[e] + 1):
                if i in self.sig[e]:
                    c += 1
                    m[i] = c
            sigval[e] = m
        semv = {}
        pc = {e: 0 for e in self.ENG}
        progress = True
        while progress:
            progress = False
            for e in self.ENG:
                st = self.streams[e]
                while pc[e] < len(st):
                    ent = st[pc[e]]
                    if ent[0] == "w":
                        sk, idx = ent[1], ent[2]
                        need = 16 * idx if isinstance(sk, tuple) else sigval[sk][idx]
                        if semv.get(sk, 0) < need:
                            break
                    elif ent[0] == "o":
                        if ent[2] in self.sig[e]:
                            semv[e] = semv.get(e, 0) + 1
                    else:
                        semv[ent[2]] = semv.get(ent[2], 0) + 16
                    pc[e] += 1
                    progress = True
        stuck = {e: (pc[e], len(self.streams[e])) for e in self.ENG if pc[e] < len(self.streams[e])}
        if stuck:
            for e, (p, n) in stuck.items():
                print("STUCK", e, p, n, self.streams[e][p][:3], semv)
            raise RuntimeError("semaphore program deadlocks: %r" % (stuck,))
        return {e: len(self.streams[e]) for e in self.ENG}, {k: v for k, v in semv.items()}

    def emit(self, nc, es):
        sems = {e: es.enter_context(nc.semaphore("s_" + e)) for e in self.ENG}
        dsems = {sk: es.enter_context(nc.semaphore("d_" + sk[1])) for sk in self.dcount}
        sigval = {}
        for e in self.ENG:
            m = {}
            c = 0
            ss = self.sig[e]
            for i in range(1, self.nops[e] + 1):
                if i in ss:
                    c += 1
                    m[i] = c
            sigval[e] = m
        engobj = {"pe": "tensor", "act": "scalar", "dve": "vector", "pool": "gpsimd", "sp": "sync"}
        block = es.enter_context(nc.Block())
        for e in self.ENG:
            stream = self.streams[e]
            if not stream:
                continue

            def body(eng, stream=stream, e=e):
                for ent in stream:
                    if ent[0] == "w":
                        sk, idx = ent[1], ent[2]
                        if isinstance(sk, tuple):
                            eng.wait_ge(dsems[sk], 16 * idx)
                        else:
                            eng.wait_ge(sems[sk], sigval[sk][idx])
                    elif ent[0] == "o":
                        ins = ent[1](eng)
                        if ent[2] in self.sig[e]:
                            ins.then_inc(sems[e], 1)
                    else:
                        ins = ent[1](eng)
                        ins.then_inc(dsems[ent[2]], 16)

            getattr(block, engobj[e])(body)


def _kmaj(w):
    K, N = w.shape
    return np.ascontiguousarray(w.reshape(K // 128, 128, N).transpose(1, 0, 2))


def _cols(v):
    return np.ascontiguousarray(v.reshape(-1, 128).T)


RET_H = 4
RET_DK = 256
RET_DV = 512
OFF_RQ, OFF_RK, OFF_RV, OFF_RG = 0, 1024, 2048, 4096
OFF_DQ, OFF_DK, OFF_DV, OFF_MQ, OFF_GATE = 6144, 7168, 8192, 9216, 10240


def _prep_weights(inp):
    out = {}
    for tag in ("ffn1", "ffn2"):
        wg = inp[tag + "_w_gate"][0]
        wu = inp[tag + "_w_up"][0]
        wd = inp[tag + "_w_down"][0]
        gu = np.empty((NG, 128, 2, 8, 256), np.float32)
        dn = np.empty((NG, 128, 2, 1024), np.float32)
        for g in range(NG):
            gu[g, :, 0] = _kmaj(wg[:, g * 256:(g + 1) * 256])
            gu[g, :, 1] = _kmaj(wu[:, g * 256:(g + 1) * 256])
            dn[g] = _kmaj(wd[g * 256:(g + 1) * 256, :])
        out[tag + "_gu"] = gu.reshape(NG, 128, 4096)
        out[tag + "_dn"] = dn.reshape(NG, 128, 2048)
        g4 = np.empty((5, 128, 8, 512), np.float32)
        u4 = np.empty((5, 128, 8, 512), np.float32)
        d4 = np.empty((5, 128, 4, 1024), np.float32)
        for g in range(5):
            g4[g] = _kmaj(wg[:, g * 512:(g + 1) * 512])
            u4[g] = _kmaj(wu[:, g * 512:(g + 1) * 512])
            d4[g] = _kmaj(wd[g * 512:(g + 1) * 512, :])
        out[tag + "_g4"] = g4.reshape(5, 128, 4096)
        out[tag + "_u4"] = u4.reshape(5, 128, 4096)
        out[tag + "_d4"] = d4.reshape(5, 128, 4096)
        out[tag + "_gu"] = np.ascontiguousarray(out[tag + "_gu"][10:11])
        out[tag + "_dn"] = np.ascontiguousarray(out[tag + "_dn"][10:11])
    w_in = inp["w_in"][0]
    rqk = np.empty((RET_H, 128, 8, 4, 128), np.float32)
    rv = np.empty((RET_H, 128, 8, 512), np.float32)
    rg = np.empty((RET_H, 128, 8, 512), np.float32)
    for h in range(RET_H):
        q = w_in[:, OFF_RQ + h * 256: OFF_RQ + (h + 1) * 256]
        k = w_in[:, OFF_RK + h * 256: OFF_RK + (h + 1) * 256]
        rqk[h, :, :, 0] = _kmaj(q[:, 0::2])
        rqk[h, :, :, 1] = _kmaj(q[:, 1::2])
        rqk[h, :, :, 2] = _kmaj(k[:, 0::2])
        rqk[h, :, :, 3] = _kmaj(k[:, 1::2])
        rv[h] = _kmaj(w_in[:, OFF_RV + h * 512: OFF_RV + (h + 1) * 512])
        rg[h] = _kmaj(w_in[:, OFF_RG + h * 512: OFF_RG + (h + 1) * 512])
    out["ret_qk"] = rqk.reshape(RET_H, 128, 4096)
    out["ret_v"] = rv.reshape(RET_H, 128, 4096)
    out["ret_g"] = rg.reshape(RET_H, 128, 4096)
    dqk = np.empty((4, 128, 8, 512), np.float32)
    dv = np.empty((4, 128, 8, 256), np.float32)
    for g in range(4):
        dqk[g, :, :, 0:256] = _kmaj(w_in[:, OFF_DQ + g * 256: OFF_DQ + (g + 1) * 256])
        dqk[g, :, :, 256:512] = _kmaj(w_in[:, OFF_DK + g * 256: OFF_DK + (g + 1) * 256])
        dv[g] = _kmaj(w_in[:, OFF_DV + g * 256: OFF_DV + (g + 1) * 256])
    out["diff_qk"] = dqk.reshape(4, 128, 4096)
    out["diff_v"] = dv.reshape(4, 128, 2048)
    mq = np.empty((2, 128, 8, 512), np.float32)
    for g in range(2):
        mq[g] = _kmaj(w_in[:, OFF_MQ + g * 512: OFF_MQ + (g + 1) * 512])
    out["mem_q"] = mq.reshape(2, 128, 4096)
    mkv = inp["mem_w_kv"][0]
    kv = np.empty((4, 128, 8, 512), np.float32)
    for g in range(4):
        kv[g] = _kmaj(mkv[:, g * 512:(g + 1) * 512])
    out["mem_kv"] = kv.reshape(4, 128, 4096)
    wo_list = [inp["ret_w_o"][0][0:1024], inp["ret_w_o"][0][1024:2048], inp["diff_w_o"][0], inp["mem_w_o"][0]]
    gidx = [0, 0, 1, 2]
    ap_w = np.empty((4, 8, 128, 2, 8, 128), np.float32)
    for a in range(4):
        for dc in range(8):
            ap_w[a, dc, :, 0] = _kmaj(wo_list[a][:, dc * 128:(dc + 1) * 128])
            gc = OFF_GATE + gidx[a] * 1024 + dc * 128
            ap_w[a, dc, :, 1] = _kmaj(w_in[:, gc: gc + 128])
    out["app_w"] = ap_w.reshape(4, 8, 128, 2048)
    wout = inp["w_out"][0]
    wo2 = np.empty((2, 2, 128, 4, 512), np.float32)
    for half in range(2):
        for nh in range(2):
            wo2[half, nh] = _kmaj(wout[half * 512:(half + 1) * 512, nh * 512:(nh + 1) * 512])
    out["w_out"] = wo2.reshape(2, 2, 128, 2048)
    cols = [
        _cols(inp["ffn1_norm"][0]), _cols(inp["mix_norm"][0]), _cols(inp["ffn2_norm"][0]),
        _cols(inp["mem_norm"][0]), _cols(inp["b_gate"][0]),
        _cols(inp["diff_subln"][0]),
    ]
    out["cols"] = np.ascontiguousarray(np.concatenate(cols, axis=1))
    out["fin"] = np.ascontiguousarray(np.broadcast_to(inp["final_norm"][0][None, :], (128, 1024)))
    rep = np.concatenate([
        inp["diff_q_norm"][0], inp["diff_k_norm"][0],
        inp["mem_q_norm"][0], inp["mem_k_norm"][0],
        inp["diff_lambda_q1"][0], inp["diff_lambda_k1"][0], inp["diff_lambda_q2"][0], inp["diff_lambda_k2"][0],
    ])
    out["rep"] = np.ascontiguousarray(np.broadcast_to(rep[None, :], (128, rep.shape[0])))
    return out


COL_FFN1, COL_MIX, COL_FFN2, COL_MEMN, COL_BG, COL_SUBLN = 0, 8, 16, 24, 32, 56
NCOLS = 57
REP_DQN, REP_DKN, REP_MQN, REP_MKN, REP_LAM = 0, 64, 128, 384, 640
NREP = 640 + 256


def _prep_consts():
    c = {}
    c["ident"] = np.eye(128, dtype=np.float32)
    gam = [1.0 - 2.0 ** (-5.0 - h) for h in range(RET_H)]
    i = np.arange(128, dtype=np.float64)
    dm = np.empty((RET_H, 128, 128), np.float64)
    cc = np.empty((128, 3 * RET_H), np.float64)
    for h, g in enumerate(gam):
        lg = math.log(g)
        cI = i[None, :]
        eI = i[:, None]
        mask = (np.floor(eI / 64) <= np.floor(cI / 64))
        dm[h] = np.exp(lg * (np.abs(cI - eI) - (cI + 1.0))) * mask
        cc[:, h] = np.exp(lg * (i + 1.0))
        cc[:, RET_H + h] = np.exp(2 * lg * (i + 1.0)) / 512.0
        cc[:, 2 * RET_H + h] = np.exp(lg * (127.0 - i))
    c["dmask"] = np.ascontiguousarray(dm.transpose(1, 0, 2)).astype(np.float32)
    c["rdec"] = cc.astype(np.float32)
    ret_inv = (1.0 / (np.float32(10000.0) ** np.linspace(0.0, 1.0, 128, dtype=np.float32))).astype(np.float32)
    rope_inv = (1.0 / (np.float32(500000.0) ** (np.arange(0, 16, 2, dtype=np.float32) / np.float32(16)))).astype(np.float32)
    c["ret_inv"] = ret_inv.reshape(128, 1)
    c["rope_inv"] = np.ascontiguousarray(np.broadcast_to(rope_inv[None, :], (128, 8))).astype(np.float32)
    c["gamma128"] = [g ** 128 for g in gam]
    return c


def build_program(wshapes, stage):
    nc = bass.Bass("TRN2", target_bir_lowering=False)
    P = Prog()
    dr = {}

    def din(name, shape, dt=F32):
        dr[name] = nc.dram_tensor(name, list(shape), dt, kind="ExternalInput").ap()
        return dr[name]

    x_d = din("x", [S, D])
    mem_d = din("mem", [256, D])
    posr_d = din("pos_rep", [128, S], I32)
    post_d = din("pos_tok", [128, NT], I32)
    for k, shp in wshapes.items():
        din(k, shp)
    y_d = nc.dram_tensor("y", [S, D], F32, kind="ExternalOutput").ap()
    gamma128 = _prep_consts()["gamma128"]

    es = ExitStack()
    with es:
        def sb(name, shape, dt=F32):
            return es.enter_context(nc.sbuf_tensor("sb_" + name, list(shape), dt))

        X = sb("X", [128, NT, D])
        HT = sb("HT", [128, 8, S], BF16)
        slots = [sb(f"slot{i}", [128, SLOT], BF16) for i in range(NSLOT)]
        ident = sb("ident", [128, 128], BF16)
        ones = sb("ones", [128, 128], BF16)
        cols = sb("cols", [128, NCOLS])
        rep = sb("rep", [128, NREP])
        ss16 = sb("ss16", [128, NT])
        rs16 = sb("rs16", [128, NT])
        ARENA_BYTES = 72 * 1024 + 512
        arena = sb("arena", [128, ARENA_BYTES // 4])
        ps = [es.enter_context(nc.psum_tensor(f"ps{i}", [128, 512], F32)) for i in range(8)]
        psb = [p.bitcast(BF16) for p in ps]

        arena_b = arena.bitcast(BF16)

        def carve_f32(off_bytes, shape):
            n = int(np.prod(shape[1:]))
            o = off_bytes // 4
            ap = arena[:, o:o + n]
            if len(shape) == 3:
                ap = ap.rearrange("p (a b) -> p a b", a=shape[1])
            elif len(shape) == 4:
                ap = ap.rearrange("p (a b c) -> p a b c", a=shape[1], b=shape[2])
            return ap

        def carve_bf(off_bytes, shape):
            n = int(np.prod(shape[1:]))
            o = off_bytes // 2
            ap = arena_b[:, o:o + n]
            if len(shape) == 3:
                ap = ap.rearrange("p (a b) -> p a b", a=shape[1])
            elif len(shape) == 4:
                ap = ap.rearrange("p (a b c) -> p a b c", a=shape[1], b=shape[2])
            return ap

        def mm(out, lhsT, rhs, start, stop, r, w):
            P.op("pe", lambda e: e.matmul(out, lhsT, rhs, start=start, stop=stop), r=r, w=w)

        def tr(out, in_, r, w):
            P.op("pe", lambda e: e.transpose(out, in_, ident[:]), r=list(r) + ["ident"], w=w)

        def act(out, in_, func, r, w, bias=0.0, scale=1.0, accum_out=None, eng="act"):
            if accum_out is not None:
                P.op("act", lambda e: e.activation(out, in_, func, bias=bias, scale=scale, accum_out=accum_out), r=r, w=w)
            else:
                P.op("act", lambda e: e.activation(out, in_, func, bias=bias, scale=scale), r=r, w=w)

        def tt(out, in0, in1, op, r, w, eng="dve"):
            P.op(eng, lambda e: e.tensor_tensor(out, in0, in1, op), r=r, w=w)

        def tsc(out, in0, s1, s2, op0, op1, r, w, eng="dve"):
            if op1 is None:
                P.op(eng, lambda e: e.tensor_scalar(out, in0, s1, None, op0), r=r, w=w)
            else:
                P.op(eng, lambda e: e.tensor_scalar(out, in0, s1, s2, op0, op1), r=r, w=w)

        def stt(out, in0, scalar, in1, op0, op1, r, w):
            P.op("dve", lambda e: e.scalar_tensor_tensor(out, in0, scalar, in1, op0, op1), r=r, w=w)

        def cp(out, in_, r, w, eng="dve"):
            if eng == "act":
                P.op("act", lambda e: e.activation(out, in_, AF.Copy), r=r, w=w)
            else:
                P.op(eng, lambda e: e.tensor_copy(out, in_), r=r, w=w)

        def recip(out, in_, r, w):
            P.op("dve", lambda e: e.reciprocal(out, in_), r=r, w=w)

        def memset(ap, val, w, eng="dve"):
            P.op(eng, lambda e: e.memset(ap, val), r=(), w=w)

        slot_ctr = [0]
        slots_static = list(slots)
        cur_slots = list(slots)

        def load_slot(parts):
            i = slot_ctr[0] % len(cur_slots)
            slot_ctr[0] += 1
            slots = list(cur_slots)
            off = 0
            for (ap, n) in parts:
                o = off
                P.dma("pool", f"slot{i}", lambda e, ap=ap, o=o, n=n, st=cur_slots[i]: e.dma_start(out=st[:, o:o + n], in_=ap, max_dma_last_dim=2048),
                      r=(), w=[("slot", i)])
                off += n
            return slots[i], ("slot", i)

        def const_load(dst, src, key, sem="c"):
            P.dma("sp", sem, lambda e: e.dma_start(out=dst, in_=src), r=(), w=[key])

        P.dma("pool", "c0", lambda e: e.dma_start(out=ident[:], in_=dr["ident"]), r=(), w=["ident"])
        const_load(cols[:], dr["cols"], "cols", "c1")
        const_load(rep[:], dr["rep"], "rep", "c2")
        for q in range(4):
            P.dma("sp", f"x{q}", lambda e, q=q: e.dma_start(
                out=X[:, q * 4:(q + 1) * 4, :],
                in_=x_d[q * 512:(q + 1) * 512, :].rearrange("(t p) d -> p t d", p=128)),
                r=(), w=[("X", t) for t in range(q * 4, q * 4 + 4)])
        memset(ones[:], 1.0, w=["ones"])

        def norm_to_HT(gcol, junk, xn):
            for t in range(NT):
                act(junk, X[:, t, :], AF.Square, r=[("X", t)], w=["junk", ("ss", t)], accum_out=ss16[:, t:t + 1])
            tsc(rs16[:], ss16[:], 1.0 / D, EPS, ALU.mult, ALU.add, r=[("ss", t) for t in range(NT)], w=["rs16"])
            act(rs16[:], rs16[:], AF.Sqrt, r=["rs16"], w=["rs16"])
            recip(rs16[:], rs16[:], r=["rs16"], w=["rs16"])
            for t in range(NT):
                xb = xn[t % 2]
                act(xb, X[:, t, :], AF.Copy, r=[("X", t), "rs16"], w=[("xn", t % 2)], scale=rs16[:, t:t + 1])
                bank = 6 + (t % 2)
                for k in range(8):
                    tr(psb[bank][:, k * 128:(k + 1) * 128], xb[:, k * 128:(k + 1) * 128],
                       r=[("xn", t % 2)], w=[("ps", bank)])
                tt(HT[:, :, t * 128:(t + 1) * 128],
                   psb[bank][:, 0:1024].rearrange("p (k c) -> p k c", k=8),
                   cols[:, gcol:gcol + 8].to_broadcast([128, 8, 128]), ALU.mult,
                   r=[("ps", bank), "cols"], w=[("HT", t // 4)])

        def ffn(tag, gcol):
            junk = carve_bf(0, [128, 1024])
            xn = [carve_bf(2048, [128, 1024]), carve_bf(4096, [128, 1024])]
            AT = [carve_bf(8192 + i * 16384, [128, 4, S]) for i in range(2)]
            SG = [carve_f32(40960 + i * 2048, [128, 512]) for i in range(2)]
            xs = [arena_b[:, (45056 + i * 8192) // 2:(45056 + i * 8192) // 2 + SLOT] for i in range(3)]
            assert 45056 + 3 * 8192 <= ARENA_BYTES
            norm_to_HT(gcol, junk, xn)
            P.barrier()
            cur_slots[:] = list(slots_static) + xs
            slot_ctr[0] = 0
            cnt = 0
            ocnt = 0
            for g in range(6):
                nch = 4 if g < 5 else 2
                if nch == 4:
                    wg_, kg_ = load_slot([(dr[tag + "_g4"][g], 4096)])
                    wu_, ku_ = load_slot([(dr[tag + "_u4"][g], 4096)])
                    wd_, kd_ = load_slot([(dr[tag + "_d4"][g], 4096)])
                    wg_v = wg_[:, 0:4096].rearrange("p (k c) -> p k c", k=8)
                    wu_v = wu_[:, 0:4096].rearrange("p (k c) -> p k c", k=8)
                    wd_v = wd_[:, 0:4096].rearrange("p (c n) -> p c n", c=4)
                else:
                    wgu, kg_ = load_slot([(dr[tag + "_gu"][0], 4096)])
                    ku_ = kg_
                    wd_, kd_ = load_slot([(dr[tag + "_dn"][0], 2048)])
                    wgu_v = wgu[:, 0:4096].rearrange("p (a k c) -> p a k c", a=2, k=8)
                    wg_v = wgu_v[:, 0, :, :]
                    wu_v = wgu_v[:, 1, :, :]
                    wd_v = wd_[:, 0:2048].rearrange("p (c n) -> p c n", c=2)
                at = AT[g % 2]
                for tb in range(NB):
                    for c in range(nch):
                        bg = (cnt % 2)
                        bu = 2 + (cnt % 2)
                        for k in range(8):
                            mm(ps[bg][:, :], wg_v[:, k, c * 128:(c + 1) * 128], HT[:, k, tb * 512:(tb + 1) * 512],
                               k == 0, k == 7, r=[kg_, ("HT", tb)], w=[("ps", bg)])
                        for k in range(8):
                            mm(ps[bu][:, :], wu_v[:, k, c * 128:(c + 1) * 128], HT[:, k, tb * 512:(tb + 1) * 512],
                               k == 0, k == 7, r=[ku_, ("HT", tb)], w=[("ps", bu)])
                        sg = SG[cnt % 2]
                        act(sg, ps[bg][:, :], AF.Silu, r=[("ps", bg)], w=[("SG", cnt % 2)])
                        tt(at[:, c, tb * 512:(tb + 1) * 512], sg, ps[bu][:, :], ALU.mult,
                           r=[("SG", cnt % 2), ("ps", bu)], w=[("AT", g % 2, tb)])
                        cnt += 1
                    for t4 in range(4):
                        t = tb * 4 + t4
                        for nh in range(2):
                            bo = 4 + (ocnt % 3)
                            ocnt += 1
                            for c in range(nch):
                                mm(ps[bo][:, :], at[:, c, t * 128:(t + 1) * 128], wd_v[:, c, nh * 512:(nh + 1) * 512],
                                   c == 0, c == nch - 1, r=[("AT", g % 2, tb), kd_], w=[("ps", bo)])
                            stt(X[:, t, nh * 512:(nh + 1) * 512], ps[bo][:, :], 0.5, X[:, t, nh * 512:(nh + 1) * 512],
                                ALU.mult, ALU.add, r=[("ps", bo), ("X", t)], w=[("X", t)])
            P.barrier()
            cur_slots[:] = list(slots_static)
            slot_ctr[0] = 0

        BT_OFF = 0
        RO_ = 32768

        def mid_bcast(ap2, reps):
            a = ap2.ap
            return bass.AP(ap2.tensor, ap2.offset, [list(a[0]), [0, reps], list(a[1])])

        def range_sin(dst, src, shift, T1, T2, T2i, keys_r, key_w):
            INV = 1.0 / TWO_PI
            C1 = 6.28125
            C2 = TWO_PI - C1
            tsc(T1, src, INV, shift * INV + 0.5, ALU.mult, ALU.add, r=keys_r, w=["rsT1"])
            cp(T2i, T1, r=["rsT1"], w=["rsT2"])
            cp(T1, T2i, r=["rsT2"], w=["rsT1"])
            stt(T2, T1, -C1, src, ALU.mult, ALU.add, r=["rsT1"] + keys_r, w=["rsT2"])
            stt(T2, T1, -C2, T2, ALU.mult, ALU.add, r=["rsT1", "rsT2"], w=["rsT2"])
            if shift != 0.0:
                tsc(T2, T2, shift, None, ALU.add, None, r=["rsT2"], w=["rsT2"])
            tsc(T1, T2, math.pi, -1e30, ALU.add, ALU.mult, r=["rsT2"], w=["rsT1"])
            tsc(T1, T1, 0.0, 1.0, ALU.max, ALU.min, r=["rsT1"], w=["rsT1"])
            stt(T2, T1, TWO_PI, T2, ALU.mult, ALU.add, r=["rsT1", "rsT2"], w=["rsT2"])
            tsc(T1, T2, -math.pi, 1e30, ALU.add, ALU.mult, r=["rsT2"], w=["rsT1"])
            tsc(T1, T1, 0.0, 1.0, ALU.max, ALU.min, r=["rsT1"], w=["rsT1"])
            stt(T2, T1, -TWO_PI, T2, ALU.mult, ALU.add, r=["rsT1", "rsT2"], w=["rsT2"])
            tsc(T2, T2, math.pi - 1e-6, -(math.pi - 1e-6), ALU.min, ALU.max, r=["rsT2"], w=["rsT2"])
            act(dst, T2, AF.Sin, r=["rsT2"], w=[key_w])

        small = sb("small", [128, 136])
        rdec = sb("rdec", [128, 12])
        retinv = sb("retinv", [128, 1])
        ropeinv = sb("ropeinv", [128, 8])
        dmask = sb("dmask", [128, RET_H, 128])
        P.dma("sp", "c5", lambda e: e.dma_start(out=retinv[:], in_=dr["ret_inv"]), r=(), w=["retinv"])
        P.dma("sp", "c6", lambda e: e.dma_start(out=rdec[:], in_=dr["rdec"]), r=(), w=["rdec"])
        P.dma("sp", "c7", lambda e: e.dma_start(out=dmask[:].rearrange("p h c -> p (h c)"), in_=dr["dmask"]), r=(), w=["dmask"])
        P.dma("sp", "c9", lambda e: e.dma_start(out=ropeinv[:], in_=dr["rope_inv"]), r=(), w=["ropeinv"])

        def apply_branch(a, gidx):
            MTh = carve_bf(RO_ + 16384, [128, 4, S])
            SGa = [carve_f32(RO_ + 32768 + i * 2048, [128, 512]) for i in range(2)]
            BT = carve_bf(BT_OFF, [128, 8, S])
            cnt = 0
            ocnt = 0
            for half in range(2):
                for dcl in range(4):
                    dc = half * 4 + dcl
                    w, kw = load_slot([(dr["app_w"][a, dc], 2048)])
                    wv = w[:, 0:2048].rearrange("p (a k c) -> p a k c", a=2, k=8)
                    for tb in range(NB):
                        bp = cnt % 2
                        bgt = 2 + cnt % 2
                        for k in range(8):
                            mm(ps[bp][:, :], wv[:, 0, k, :], BT[:, k, tb * 512:(tb + 1) * 512], k == 0, k == 7,
                               r=[kw, ("BT", tb)], w=[("ps", bp)])
                        for k in range(8):
                            mm(ps[bgt][:, :], wv[:, 1, k, :], HT[:, k, tb * 512:(tb + 1) * 512], k == 0, k == 7,
                               r=[kw, ("HT", tb)], w=[("ps", bgt)])
                        sg = SGa[cnt % 2]
                        bcol = COL_BG + gidx * 8 + dc
                        act(sg, ps[bgt][:, :], AF.Sigmoid, r=[("ps", bgt), "cols"], w=[("SGa", cnt % 2)],
                            bias=cols[:, bcol:bcol + 1])
                        tt(MTh[:, dcl, tb * 512:(tb + 1) * 512], sg, ps[bp][:, :], ALU.mult,
                           r=[("SGa", cnt % 2), ("ps", bp)], w=[("MT", tb)])
                        cnt += 1
                for nh in range(2):
                    w, kw = load_slot([(dr["w_out"][half, nh], 2048)])
                    wv = w[:, 0:2048].rearrange("p (k n) -> p k n", k=4)
                    for t in range(NT):
                        bo = 4 + ocnt % 2
                        ocnt += 1
                        for k in range(4):
                            mm(ps[bo][:, :], MTh[:, k, t * 128:(t + 1) * 128], wv[:, k, :], k == 0, k == 3,
                               r=[("MT", t // 4), kw], w=[("ps", bo)])
                        tt(X[:, t, nh * 512:(nh + 1) * 512], X[:, t, nh * 512:(nh + 1) * 512], ps[bo][:, :], ALU.add,
                           r=[("ps", bo), ("X", t)], w=[("X", t)])

        def ret_tables():
            TABC = carve_f32(RO_, [128, S])
            TABS = carve_f32(RO_ + 8192, [128, S])
            T1 = carve_f32(RO_ + 16384, [128, S])
            T2 = carve_f32(RO_ + 24576, [128, S])
            T2i = T2.bitcast(I32)
            ANG = carve_f32(RO_ + 32768, [128, S])
            ANGi = ANG.bitcast(I32)
            P.dma("sp", "c4", lambda e: e.dma_start(out=ANGi, in_=posr_d), r=(), w=["ANGi"])
            cp(T1, ANGi, r=["ANGi"], w=["posf"])
            tsc(ANG, T1, retinv[:, 0:1], None, ALU.mult, None, r=["posf", "retinv", "ANGi"], w=["ANG"])
            range_sin(TABS, ANG, 0.0, T1, T2, T2i, ["ANG"], "TAB")
            range_sin(TABC, ANG, math.pi / 2, T1, T2, T2i, ["ANG"], "TAB")

        def lam_compute():
            memset(small[:, 6:7], EPS, w=["epsc"])
            pr = small[:, 64:128]
            for i, (a_, b_) in enumerate(((0, 64), (128, 192))):
                tt(pr, rep[:, REP_LAM + a_:REP_LAM + a_ + 64], rep[:, REP_LAM + b_:REP_LAM + b_ + 64], ALU.mult,
                   r=["rep"], w=["lam_pr"])
                P.op("dve", lambda e, i=i: e.reduce_sum(small[:, 1 + i:2 + i], pr, AX.X), r=["lam_pr"], w=[("lam_s", i)])
            act(small[:, 1:3], small[:, 1:3], AF.Exp, r=[("lam_s", 0), ("lam_s", 1)], w=["lam_e"])
            tt(small[:, 3:4], small[:, 2:3], small[:, 1:2], ALU.subtract, r=["lam_e"], w=["lam_d"])
            tsc(small[:, 0:1], small[:, 3:4], -0.2, None, ALU.add, None, r=["lam_d"], w=["neglam"])

        def ret_head(h, bt_base):
            BT = carve_bf(BT_OFF, [128, 8, S])
            TABC = carve_f32(RO_, [128, S])
            TABS = carve_f32(RO_ + 8192, [128, S])
            o = RO_ + 16384
            QT = carve_bf(o, [128, 2, 512]); o += 2048
            KT = carve_bf(o, [128, 2, 512]); o += 2048
            KTOK = carve_bf(o, [128, 4, 256]); o += 2048
            VTOK = carve_bf(o, [128, 4, 512]); o += 4096
            SGt = carve_bf(o, [128, 4, 512]); o += 4096
            Sf = carve_f32(o, [128, 2, 512]); o += 4096
            Sbf = carve_bf(o, [128, 2, 512]); o += 2048
            SCT = carve_bf(o, [128, 128]); o += 256
            RO = [carve_bf(o, [128, 512]), carve_bf(o + 1024, [128, 512])]; o += 2048
            T1 = carve_f32(o, [128, 512]); o += 2048
            T2 = ps[7][:, :]
            assert o <= ARENA_BYTES, o
            wqk, kqk = load_slot([(dr["ret_qk"][h], 4096)])
            wv, kv_ = load_slot([(dr["ret_v"][h], 4096)])
            wg, kg = load_slot([(dr["ret_g"][h], 4096)])
            wqk_v = wqk[:, :].rearrange("p (k j c) -> p k j c", k=8, j=4)
            wv_v = wv[:, :].rearrange("p (k n) -> p k n", k=8)
            wg_v = wg[:, :].rearrange("p (k n) -> p k n", k=8)
            g128 = float(gamma128[h])
            pending = [None]

            def flush_ro(ri, t, tb):
                for fc in range(4):
                    tr(psb[7][:, fc * 128:(fc + 1) * 128], RO[ri][:, fc * 128:(fc + 1) * 128], r=[("RO", ri)], w=[("ps", 7)])
                cp(BT[:, bt_base:bt_base + 4, t * 128:(t + 1) * 128],
                   psb[7][:, 0:512].rearrange("p (f c) -> p f c", f=4), r=[("ps", 7)], w=[("BT", tb)], eng="act")
            for tb in range(NB):
                tbs = slice(tb * 512, (tb + 1) * 512)
                for (j0, b0) in ((0, 0), (2, 4)):
                    for jj in range(2):
                        for k in range(8):
                            mm(ps[b0 + jj][:, :], wqk_v[:, k, j0 + jj, :], HT[:, k, tbs], k == 0, k == 7,
                               r=[kqk, ("HT", tb)], w=[("ps", b0 + jj)])
                for ci in range(4):
                    t = tb * 4 + ci
                    ts_ = slice(t * 128, (t + 1) * 128)
                    bv = 2 if ci % 2 == 0 else 6
                    for k in range(8):
                        mm(ps[bv][:, :], HT[:, k, ts_], wv_v[:, k, :], k == 0, k == 7, r=[kv_, ("HT", tb)], w=[("ps", bv)])
                    act(VTOK[:, ci, :], ps[bv][:, :], AF.Copy, r=[("ps", bv)], w=[("VTOK", ci)])
                C = TABC[:, tbs]
                Sn = TABS[:, tbs]
                for (b0, dst, scale, key) in ((0, QT, 1.0, "QT"), (4, KT, 1.0 / 16.0, "KT")):
                    pe_, po_ = ps[b0], ps[b0 + 1]
                    stt(T1, pe_[:, :], scale, C, ALU.mult, ALU.mult, r=[("ps", b0), "TAB"], w=["T1"])
                    stt(T2, po_[:, :], scale, Sn, ALU.mult, ALU.mult, r=[("ps", b0 + 1), "TAB"], w=[("ps", 7)])
                    tt(dst[:, 0, :], T1, T2, ALU.subtract, r=["T1", ("ps", 7)], w=[key])
                    stt(T1, po_[:, :], scale, C, ALU.mult, ALU.mult, r=[("ps", b0 + 1), "TAB"], w=["T1"])
                    stt(T2, pe_[:, :], scale, Sn, ALU.mult, ALU.mult, r=[("ps", b0), "TAB"], w=[("ps", 7)])
                    tt(dst[:, 1, :], T1, T2, ALU.add, r=["T1", ("ps", 7)], w=[key])
                for ci in range(4):
                    t = tb * 4 + ci
                    ts_ = slice(t * 128, (t + 1) * 128)
                    bg_ = 3 if ci % 2 == 0 else 6
                    for k in range(8):
                        mm(ps[bg_][:, :], HT[:, k, ts_], wg_v[:, k, :], k == 0, k == 7, r=[kg, ("HT", tb)], w=[("ps", bg_)])
                    act(SGt[:, ci, :], ps[bg_][:, :], AF.Silu, r=[("ps", bg_)], w=[("SGt", ci)])
                for ci in range(4):
                    cs = slice(ci * 128, (ci + 1) * 128)
                    for c in range(2):
                        tr(psb[4][:, c * 128:(c + 1) * 128], KT[:, c, cs], r=["KT"], w=[("ps", 4)])
                    tsc(KTOK[:, ci, :], psb[4][:, 0:256], rdec[:, 8 + h:9 + h], None, ALU.mult, None,
                        r=[("ps", 4), "rdec"], w=[("KTOK", ci)])
                def post(n, ci, ob):
                    pp = n % 2
                    rss = small[:, 26 + 2 * pp:27 + 2 * pp]
                    rf = small[:, 27 + 2 * pp:28 + 2 * pp]
                    act(T1, ps[ob][:, :], AF.Square, r=[("ps", ob)], w=["T1", ("rss", pp)], accum_out=rss)
                    tt(rf, rss, rdec[:, 4 + h:5 + h], ALU.mult, r=[("rss", pp), "rdec"], w=[("rf", pp)])
                    tsc(rf, rf, EPS, None, ALU.add, None, r=[("rf", pp)], w=[("rf", pp)])
                    act(rf, rf, AF.Sqrt, r=[("rf", pp)], w=[("rf", pp)])
                    recip(rf, rf, r=[("rf", pp)], w=[("rf", pp)])
                    tt(rf, rf, rdec[:, h:h + 1], ALU.mult, r=[("rf", pp), "rdec"], w=[("rf", pp)])
                    stt(RO[pp], ps[ob][:, :], rf, SGt[:, ci, :], ALU.mult, ALU.mult,
                        r=[("ps", ob), ("rf", pp), ("SGt", ci)], w=[("RO", pp)])

                prev_post = None
                for ci in range(4):
                    n = tb * 4 + ci
                    t = n
                    cs = slice(ci * 128, (ci + 1) * 128)
                    ob = 6 if n % 2 == 0 else 2
                    for c in range(2):
                        mm(ps[5][:, 0:128], KT[:, c, cs], QT[:, c, cs], c == 0, c == 1, r=["KT", "QT"], w=[("ps", 5)])
                    tt(SCT, ps[5][:, 0:128], dmask[:, h, :], ALU.mult, r=[("ps", 5), "dmask"], w=["SCT"])
                    if n < 15:
                        for c in range(2):
                            mm(ps[c][:, :], KTOK[:, ci, c * 128:(c + 1) * 128], VTOK[:, ci, :], True, True,
                               r=[("KTOK", ci), ("VTOK", ci)], w=[("ps", c)])
                    mm(ps[ob][:, :], SCT, VTOK[:, ci, :], True, n == 0, r=["SCT", ("VTOK", ci)], w=[("ps", ob)])
                    if n > 0:
                        for c in range(2):
                            mm(ps[ob][:, :], QT[:, c, cs], Sbf[:, c, :], False, c == 1, r=["QT", ("Sbf", c)], w=[("ps", ob)])
                    if n < 15:
                        for c in range(2):
                            if n == 0:
                                cp(Sf[:, c, :], ps[c][:, :], r=[("ps", c)], w=[("Sf", c)])
                            else:
                                stt(Sf[:, c, :], Sf[:, c, :], g128, ps[c][:, :], ALU.mult, ALU.add,
                                    r=[("ps", c), ("Sf", c)], w=[("Sf", c)])
                            cp(Sbf[:, c, :], Sf[:, c, :], r=[("Sf", c)], w=[("Sbf", c)], eng="act")
                    if pending[0] is not None:
                        flush_ro(*pending[0])
                        pending[0] = None
                    if prev_post is not None:
                        post(*prev_post)
                        pending[0] = (prev_post[0] % 2, prev_post[0], tb)
                    prev_post = (n, ci, ob)
                if pending[0] is not None:
                    flush_ro(*pending[0])
                    pending[0] = None
                post(*prev_post)
                pending[0] = (prev_post[0] % 2, prev_post[0], tb)
            flush_ro(*pending[0])

        def diff_tables():
            DS8 = carve_f32(RO_ + 29696, [128, 16, 8])
            DC8 = carve_f32(RO_ + 30208, [128, 16, 8])
            DSS = carve_f32(RO_ + 28672, [128, 16, 16])
            DCC = carve_f32(RO_ + 33792, [128, 16, 16])
            A = carve_f32(RO_ + 34816, [128, 16, 8])
            T1 = carve_f32(RO_ + 34816 + 512, [128, 16, 8])
            T2 = carve_f32(RO_ + 34816 + 1024, [128, 16, 8])
            T2i = T2.bitcast(I32)
            PI_ = carve_f32(RO_ + 34816 + 1536, [128, 16]).bitcast(I32)
            PF = carve_f32(RO_ + 34816 + 1600, [128, 16])
            P.dma("sp", "c8", lambda e: e.dma_start(out=PI_, in_=post_d), r=(), w=["PI"])
            cp(PF, PI_, r=["PI"], w=["PF"])
            tt(A, PF.to_broadcast([128, 16, 8]),
               mid_bcast(ropeinv[:, 0:8], 16), ALU.mult, r=["PF", "ropeinv"], w=["DA"])
            range_sin(DS8, A, 0.0, T1, T2, T2i, ["DA"], "DT8")
            range_sin(DC8, A, math.pi / 2, T1, T2, T2i, ["DA"], "DT8")
            cp(DCC[:, :, 0:8], DC8, r=["DT8"], w=["DTAB"])
            cp(DCC[:, :, 8:16], DC8, r=["DT8"], w=["DTAB"])
            cp(DSS[:, :, 8:16], DS8, r=["DT8"], w=["DTAB"])
            tsc(DSS[:, :, 0:8], DS8, -1.0, None, ALU.mult, None, r=["DT8"], w=["DTAB"])

        def diff_group(gi):
            BT = carve_bf(BT_OFF, [128, 8, S])
            QTd = carve_bf(RO_, [128, 2, S])
            KTd = carve_bf(RO_ + 8192, [128, 2, S])
            VTd = carve_bf(RO_ + 16384, [128, 16, 256])
            PT = [[carve_bf(RO_ + 24576 + (b * 2 + r_) * 1024, [128, 512]) for r_ in range(2)] for b in range(2)]
            DSS = carve_f32(RO_ + 28672, [128, 16, 16])
            SQ = carve_f32(RO_ + 30720, [128, 512])
            QKb = carve_bf(RO_ + 32768, [128, 512])
            DCC = carve_f32(RO_ + 33792, [128, 16, 16])
            A0 = carve_f32(RO_ + 34816, [128, 512])
            A1 = carve_f32(RO_ + 36864, [128, 512])
            RD = carve_f32(RO_ + 38912, [128, 512])
            QKf = A1
            XR = RD[:, 0:128].rearrange("p (g d) -> p g d", g=8)
            RA = RD[:, 128:256].rearrange("p (g d) -> p g d", g=8)
            RB = RD[:, 256:384].rearrange("p (g d) -> p g d", g=8)
            assert RO_ + 40960 <= ARENA_BYTES
            SQb = SQ.bitcast(BF16)[:, 0:512]
            wqk, kqk = load_slot([(dr["diff_qk"][gi], 4096)])
            wv, kv_ = load_slot([(dr["diff_v"][gi], 2048)])
            wqk_v = wqk[:, :].rearrange("p (k n) -> p k n", k=8)
            wv_v = wv[:, 0:2048].rearrange("p (k n) -> p k n", k=8)
            QKf3 = QKf.rearrange("p (g d) -> p g d", g=8)
            QKb3 = QKb.rearrange("p (g d) -> p g d", g=8)
            gain_ap = bass.AP(rep[:, 0:1].tensor, REP_DQN, [[NREP, 128], [64, 2], [0, 4], [1, 64]])
            gain16_ap = bass.AP(rep[:, 0:1].tensor, REP_DQN, [[NREP, 128], [64, 2], [0, 4], [1, 16]])

            def dproj(t):
                ts_ = slice(t * 128, (t + 1) * 128)
                bq = t % 2
                bv = 2 + t % 2
                for k in range(8):
                    mm(ps[bq][:, :], HT[:, k, ts_], wqk_v[:, k, :], k == 0, k == 7, r=[kqk, ("HT", t // 4)], w=[("ps", bq)])
                for k in range(8):
                    mm(ps[bv][:, 0:256], HT[:, k, ts_], wv_v[:, k, :], k == 0, k == 7, r=[kv_, ("HT", t // 4)], w=[("ps", bv)])

            def actpre(t):
                bq = t % 2
                bv = 2 + t % 2
                act(VTd[:, t, :], ps[bv][:, 0:256], AF.Copy, r=[("ps", bv)], w=[("VTd", t)])
                act(SQ, ps[bq][:, :], AF.Square, r=[("ps", bq)], w=["SQ"])

            dproj(0)
            actpre(0)
            for t in range(NT):
                ts_ = slice(t * 128, (t + 1) * 128)
                bq = t % 2
                btr = 4 + t % 2
                if t + 1 < NT:
                    dproj(t + 1)
                P.op("dve", lambda e: e.reduce_sum(small[:, 8:16], SQ.rearrange("p (g d) -> p g d", g=8), AX.X),
                     r=["SQ"], w=["ss8"])
                tsc(small[:, 8:16], small[:, 8:16], 1.0 / 64.0, EPS, ALU.mult, ALU.add, r=["ss8"], w=["ss8"])
                act(small[:, 8:16], small[:, 8:16], AF.Sqrt, r=["ss8"], w=["ss8"])
                recip(small[:, 8:16], small[:, 8:16], r=["ss8"], w=["ss8"])
                tt(QKf3, ps[bq][:, :].rearrange("p (g d) -> p g d", g=8), small[:, 8:16].to_broadcast([128, 8, 64]), ALU.mult,
                   r=[("ps", bq), "ss8"], w=["A1"])
                if t + 1 < NT:
                    actpre(t + 1)
                tt(QKb.rearrange("p (a b d) -> p a b d", a=2, b=4), QKf.rearrange("p (a b d) -> p a b d", a=2, b=4), gain_ap,
                   ALU.mult, r=["A1", "rep"], w=["QKb"])
                tt(XR.rearrange("p (a b) d -> p a b d", a=2), QKf.rearrange("p (a b d) -> p a b d", a=2, b=4)[:, :, :, 0:16], gain16_ap,
                   ALU.mult, r=["A1", "rep"], w=["RD"])
                ccb = mid_bcast(DCC[:, t, :], 8)
                tt(RA, XR, ccb, ALU.mult, r=["RD", "DTAB"], w=["RD"])
                tt(RB[:, :, 0:8], XR[:, :, 8:16], mid_bcast(DSS[:, t, 0:8], 8), ALU.mult, r=["RD", "DTAB"], w=["RD"])
                tt(RB[:, :, 8:16], XR[:, :, 0:8], mid_bcast(DSS[:, t, 8:16], 8), ALU.mult, r=["RD", "DTAB"], w=["RD"])
                tt(QKb3[:, :, 0:16], RA, RB, ALU.add, r=["RD", "QKb"], w=["QKb"])
                for j in range(4):
                    tr(psb[btr][:, j * 128:(j + 1) * 128], QKb[:, j * 128:(j + 1) * 128], r=["QKb"], w=[("ps", btr)])
                cp(QTd[:, :, ts_], psb[btr][:, 0:256].rearrange("p (h c) -> p h c", h=2), r=[("ps", btr)], w=[("QTd", t // 4)], eng="act")
                cp(KTd[:, :, ts_], psb[btr][:, 256:512].rearrange("p (h c) -> p h c", h=2), r=[("ps", btr)], w=[("KTd", t // 4)], eng="act")
            steps = []
            for hh in range(2):
                for qb in range(NB):
                    nkt = 4 * (qb + 1)
                    for kt in range(nkt):
                        steps.append((hh, qb, kt, nkt))

            def emit_scores(i):
                hh, qb, kt, nkt = steps[i]
                c0 = max(0, kt - 4 * qb) * 128
                b = i % 2
                for r_ in range(2):
                    pr = slice(r_ * 64, (r_ + 1) * 64)
                    sbank = b * 2 + r_
                    mm(ps[sbank][:, c0:512], KTd[pr, hh, kt * 128:(kt + 1) * 128],
                       QTd[pr, hh, qb * 512 + c0:(qb + 1) * 512], True, True,
                       r=[("KTd", kt // 4), ("QTd", qb)], w=[("ps", sbank)])

            def emit_exp(i):
                hh, qb, kt, nkt = steps[i]
                c0 = max(0, kt - 4 * qb) * 128
                b = i % 2
                for r_ in range(2):
                    sbank = b * 2 + r_
                    pt = PT[b][r_]
                    act(pt[:, c0:512], ps[sbank][:, c0:512], AF.Exp, r=[("ps", sbank)], w=[("PT", b, r_)], scale=0.125)
                    if kt >= 4 * qb:
                        memset(pt[64:128, c0:c0 + 64], 0.0, w=[("PT", b, r_)], eng="dve")

            def emit_pv(i):
                hh, qb, kt, nkt = steps[i]
                c0 = max(0, kt - 4 * qb) * 128
                b = i % 2
                for r_ in range(2):
                    pt = PT[b][r_]
                    mm(ps[4 + r_][:, c0:512], VTd[:, kt, hh * 128:(hh + 1) * 128], pt[:, c0:512], kt == 0, kt == nkt - 1,
                       r=[("VTd", kt), ("PT", b, r_)], w=[("ps", 4 + r_)])
                    mm(ps[6 + r_][:, c0:512], ones[:, :], pt[:, c0:512], kt == 0, kt == nkt - 1,
                       r=["ones", ("PT", b, r_)], w=[("ps", 6 + r_)])

            def finalize(hh, qb):
                h = 2 * gi + hh
                qs = slice(qb * 512, (qb + 1) * 512)
                RDb = SQ
                SQq = QKb
                act(RD, ps[6][:, :], AF.Ln, r=[("ps", 6)], w=["RD"])
                act(RD, RD, AF.Exp, r=["RD"], w=["RD"], scale=-1.0)
                act(RDb, ps[7][:, :], AF.Ln, r=[("ps", 7)], w=["SQ"])
                act(RDb, RDb, AF.Exp, r=["SQ"], w=["SQ"], scale=-1.0)
                tt(A0, ps[4][:, :], RD, ALU.mult, r=[("ps", 4), "RD"], w=["A0"])
                tt(A1, ps[5][:, :], RDb, ALU.mult, r=[("ps", 5), "SQ"], w=["A1"])
                stt(A0, A1, small[:, 0:1], A0, ALU.mult, ALU.add, r=["A0", "A1", "neglam"], w=["A0"])
                act(SQq, A0, AF.Square, r=["A0"], w=["QKb"])
                mm(ps[6][:, :], ones[:, :], SQq, True, True, r=["ones", "QKb"], w=[("ps", 6)])
                act(RD, ps[6][:, :], AF.Ln, r=[("ps", 6), "epsc"], w=["RD"], scale=1.0 / 128.0, bias=small[:, 6:7])
                act(RD, RD, AF.Exp, r=["RD"], w=["RD"], scale=-0.5)
                tt(A0, A0, RD, ALU.mult, r=["A0", "RD"], w=["A0"])
                tsc(BT[:, h, qs], A0, cols[:, COL_SUBLN:COL_SUBLN + 1], 0.8, ALU.mult, ALU.mult,
                    r=["A0", "cols"], w=[("BT", qb)])

            emit_scores(0)
            for i in range(len(steps)):
                if i + 1 < len(steps):
                    emit_scores(i + 1)
                emit_exp(i)
                emit_pv(i)
                hh, qb, kt, nkt = steps[i]
                if kt == nkt - 1:
                    finalize(hh, qb)

        def group_rms_pre(psrc, G, d, SQ, keyp, par=0):
            n = G * d
            act(SQ[:, par * 512:par * 512 + n], psrc, AF.Square, r=[keyp], w=[("SQ", par)])

        def group_rms_main(psrc, G, d, gain_off, WKf, SQ, QBb, keyp, par=0, mid=None):
            n = G * d
            o = par * 512
            c0 = 16 + 2 * par
            sm = small[:, c0:c0 + G]
            P.op("dve", lambda e: e.reduce_sum(sm, SQ[:, o:o + n].rearrange("p (g d) -> p g d", g=G), AX.X),
                 r=[("SQ", par)], w=[("ssg", par)])
            tsc(sm, sm, 1.0 / d, EPS, ALU.mult, ALU.add, r=[("ssg", par)], w=[("ssg", par)])
            act(sm, sm, AF.Sqrt, r=[("ssg", par)], w=[("ssg", par)])
            recip(sm, sm, r=[("ssg", par)], w=[("ssg", par)])
            tt(WKf[:, o:o + n].rearrange("p (g d) -> p g d", g=G), psrc.rearrange("p (g d) -> p g d", g=G),
               sm.to_broadcast([128, G, d]), ALU.mult, r=[keyp, ("ssg", par)], w=[("WKf", par)])
            if mid is not None:
                mid()
            tt(QBb[:, o:o + n].rearrange("p (g d) -> p g d", g=G), WKf[:, o:o + n].rearrange("p (g d) -> p g d", g=G),
               mid_bcast(rep[:, gain_off:gain_off + d], G), ALU.mult, r=[("WKf", par), "rep"], w=[("QBb", par)])

        def group_rms(psrc, G, d, gain_off, WKf, SQ, QBb, keyp):
            group_rms_pre(psrc, G, d, SQ, keyp, 0)
            group_rms_main(psrc, G, d, gain_off, WKf, SQ, QBb, keyp, 0)

        def mem_phase():
            BT = carve_bf(BT_OFF, [128, 8, S])
            MQT = carve_bf(RO_, [128, 4, S])
            MKT = carve_bf(RO_ + 16384, [128, 8, 256])
            MV = carve_bf(RO_ + 20480, [128, 2, 1024])
            MNT = carve_bf(RO_ + 24576, [128, 8, 256])
            PTm = [[carve_bf(RO_ + 28672 + m * 1024, [128, 512]) for m in range(2)] for b in range(2)]
            WKf = carve_f32(RO_ + 30720, [128, 1024])
            SQ = carve_f32(RO_ + 34816, [128, 1024])
            QBb = carve_bf(RO_ + 38912, [128, 1024])
            assert RO_ + 40960 <= ARENA_BYTES
            RD = SQ[:, 0:512]
            for mt in range(2):
                P.dma("sp", "mem", lambda e, mt=mt: e.dma_start(out=WKf, in_=mem_d[mt * 128:(mt + 1) * 128, :]), r=(), w=[("WKf", 0), ("WKf", 1)])
                act(SQ, WKf, AF.Square, r=[("WKf", 0), ("WKf", 1)], w=[("SQ", 0), ("SQ", 1), "mss"], accum_out=small[:, 24:25])
                tsc(small[:, 24:25], small[:, 24:25], 1.0 / D, EPS, ALU.mult, ALU.add, r=["mss"], w=["mss"])
                act(small[:, 24:25], small[:, 24:25], AF.Sqrt, r=["mss"], w=["mss"])
                recip(small[:, 24:25], small[:, 24:25], r=["mss"], w=["mss"])
                act(QBb, WKf, AF.Copy, r=[("WKf", 0), ("WKf", 1), "mss"], w=[("QBb", 0), ("QBb", 1)], scale=small[:, 24:25])
                for k in range(8):
                    tr(psb[0][:, k * 128:(k + 1) * 128], QBb[:, k * 128:(k + 1) * 128], r=[("QBb", 0), ("QBb", 1)], w=[("ps", 0)])
                tt(MNT[:, :, mt * 128:(mt + 1) * 128], psb[0][:, 0:1024].rearrange("p (k c) -> p k c", k=8),
                   cols[:, COL_MEMN:COL_MEMN + 8].to_broadcast([128, 8, 128]), ALU.mult, r=[("ps", 0), "cols"], w=["MNT"])
            for g in range(4):
                w, kw = load_slot([(dr["mem_kv"][g], 4096)])
                wv = w[:, :].rearrange("p (k n) -> p k n", k=8)
                for mt in range(2):
                    for k in range(8):
                        mm(ps[1][:, :], MNT[:, k, mt * 128:(mt + 1) * 128], wv[:, k, :], k == 0, k == 7, r=["MNT", kw], w=[("ps", 1)])
                    if g < 2:
                        group_rms(ps[1][:, :], 2, 256, REP_MKN, WKf, SQ, QBb, ("ps", 1))
                        for j in range(4):
                            tr(psb[2][:, j * 128:(j + 1) * 128], QBb[:, j * 128:(j + 1) * 128], r=[("QBb", 0)], w=[("ps", 2)])
                        cp(MKT[:, g * 4:(g + 1) * 4, mt * 128:(mt + 1) * 128], psb[2][:, 0:512].rearrange("p (j c) -> p j c", j=4),
                           r=[("ps", 2)], w=["MKT"], eng="act")
                    else:
                        act(MV[:, mt, (g - 2) * 512:(g - 1) * 512], ps[1][:, :], AF.Copy, r=[("ps", 1)], w=["MV"])
            pcnt = 0
            for half in range(2):
                w, kw = load_slot([(dr["mem_q"][half], 4096)])
                wv = w[:, :].rearrange("p (k n) -> p k n", k=8)
                def mproj(t):
                    ts_ = slice(t * 128, (t + 1) * 128)
                    bq = t % 2
                    for k in range(8):
                        mm(ps[bq][:, :], HT[:, k, ts_], wv[:, k, :], k == 0, k == 7, r=[("HT", t // 4), kw], w=[("ps", bq)])

                mproj(0)
                group_rms_pre(ps[0][:, :], 2, 256, SQ, ("ps", 0), 0)
                for t in range(NT):
                    ts_ = slice(t * 128, (t + 1) * 128)
                    bq = t % 2
                    par = t % 2
                    btr = 2 + t % 2
                    if t + 1 < NT:
                        mproj(t + 1)

                    def pre_next(t=t):
                        if t + 1 < NT:
                            group_rms_pre(ps[(t + 1) % 2][:, :], 2, 256, SQ, ("ps", (t + 1) % 2), (t + 1) % 2)

                    group_rms_main(ps[bq][:, :], 2, 256, REP_MQN, WKf, SQ, QBb, ("ps", bq), par, mid=pre_next)
                    for j in range(4):
                        tr(psb[btr][:, j * 128:(j + 1) * 128], QBb[:, par * 512 + j * 128:par * 512 + (j + 1) * 128],
                           r=[("QBb", par)], w=[("ps", btr)])
                    cp(MQT[:, :, ts_], psb[btr][:, 0:512].rearrange("p (j c) -> p j c", j=4), r=[("ps", btr)], w=[("MQT", t // 4)], eng="act")
                for hh in range(2):
                    h = half * 2 + hh
                    for qb in range(NB):
                        qs = slice(qb * 512, (qb + 1) * 512)
                        b = pcnt % 2
                        pcnt += 1
                        for mt in range(2):
                            for c in range(2):
                                mm(ps[3 + mt][:, :], MKT[:, h * 2 + c, mt * 128:(mt + 1) * 128], MQT[:, hh * 2 + c, qs], c == 0, c == 1,
                                   r=["MKT", ("MQT", qb)], w=[("ps", 3 + mt)])
                            act(PTm[b][mt], ps[3 + mt][:, :], AF.Exp, r=[("ps", 3 + mt)], w=[("PTm", 0, mt)], scale=1.0 / 16.0)
                        for c in range(2):
                            for mt in range(2):
                                mm(ps[5 + c][:, :], MV[:, mt, h * 256 + c * 128:h * 256 + (c + 1) * 128], PTm[b][mt], mt == 0, mt == 1,
                                   r=["MV", ("PTm", 0, mt)], w=[("ps", 5 + c)])
                        for mt in range(2):
                            mm(ps[7][:, :], ones[:, :], PTm[b][mt], mt == 0, mt == 1, r=["ones", ("PTm", 0, mt)], w=[("ps", 7)])
                        act(RD, ps[7][:, :], AF.Ln, r=[("ps", 7)], w=[("SQ", 0)])
                        act(RD, RD, AF.Exp, r=[("SQ", 0)], w=[("SQ", 0)], scale=-1.0)
                        for c in range(2):
                            tt(BT[:, h * 2 + c, qs], ps[5 + c][:, :], RD, ALU.mult, r=[("ps", 5 + c), ("SQ", 0)], w=[("BT", qb)])

        def mixer():
            junk = carve_bf(0, [128, 1024])
            xn = [carve_bf(2048, [128, 1024]), carve_bf(4096, [128, 1024])]
            norm_to_HT(COL_MIX, junk, xn)
            P.barrier()
            do_ret = stage in (2, 21) or stage >= 3
            do_diff = stage in (2, 22) or stage >= 3
            do_mem = stage in (2, 23) or stage >= 3
            if stage >= 20:
                do_ret, do_diff, do_mem = stage == 21, stage == 22, stage == 23
            if do_ret:
                ret_tables()
                P.barrier()
                for pair in range(2):
                    for hh in range(2):
                        ret_head(pair * 2 + hh, hh * 4)
                    P.barrier()
                    apply_branch(pair, 0)
                    P.barrier()
            if do_diff:
                lam_compute()
                diff_tables()
                P.barrier()
                for gi in range(4):
                    diff_group(gi)
                P.barrier()
                apply_branch(2, 1)
                P.barrier()
            if do_mem:
                mem_phase()
                P.barrier()
                apply_branch(3, 2)
                P.barrier()

        def final_out(do_norm=True):
            junk = carve_bf(0, [128, 1024])
            OB = [carve_f32(4096 + i * 4096, [128, 1024]) for i in range(2)]
            FIN = carve_f32(12288, [128, 1024])
            if do_norm:
                P.dma("sp", "c3", lambda e: e.dma_start(out=FIN, in_=dr["fin"]), r=(), w=["FIN"])
                for t in range(NT):
                    act(junk, X[:, t, :], AF.Square, r=[("X", t)], w=["junk", ("ss", t)], accum_out=ss16[:, t:t + 1])
                tsc(rs16[:], ss16[:], 1.0 / D, EPS, ALU.mult, ALU.add, r=[("ss", t) for t in range(NT)], w=["rs16"])
                act(rs16[:], rs16[:], AF.Sqrt, r=["rs16"], w=["rs16"])
                recip(rs16[:], rs16[:], r=["rs16"], w=["rs16"])
            for t in range(NT):
                ob = OB[t % 2]
                if do_norm:
                    stt(ob, X[:, t, :], rs16[:, t:t + 1], FIN, ALU.mult, ALU.mult,
                        r=[("X", t), "rs16", "FIN"], w=[("OB", t % 2)])
                else:
                    cp(ob, X[:, t, :], r=[("X", t)], w=[("OB", t % 2)])
                P.dma("sp", f"y{t % 2}", lambda e, ob=ob, t=t: e.dma_start(out=y_d[t * 128:(t + 1) * 128, :], in_=ob),
                      r=[("OB", t % 2)], w=[("Y", t)])

        ffn("ffn1", COL_FFN1)
        P.barrier()
        if stage >= 2:
            mixer()
            P.barrier()
        if stage >= 3 and stage < 20:
            ffn("ffn2", COL_FFN2)
            P.barrier()
        final_out(do_norm=(stage >= 3 and stage < 20))
        P.barrier()
        print('PROG check:', P.check())
        P.emit(nc, es)
    return nc


_CACHE = {}


def kernel(**inputs):
    inp = {k: np.asarray(v) for k, v in inputs.items()}
    B = inp["x"].shape[0]
    W = _prep_weights(inp)
    C = _prep_consts()
    shared = dict(W)
    shared["ident"] = C["ident"]
    shared["dmask"] = C["dmask"].reshape(128, RET_H * 128)
    shared["rdec"] = C["rdec"]
    shared["ret_inv"] = C["ret_inv"]
    shared["rope_inv"] = C["rope_inv"]
    wshapes = {k: v.shape for k, v in shared.items()}
    key = ("nc", STAGE)
    if key not in _CACHE:
        _CACHE[key] = build_program(wshapes, STAGE)
    nc = _CACHE[key]
    in_maps = []
    for b in range(B):
        m = dict(shared)
        m["x"] = np.ascontiguousarray(inp["x"][b])
        m["mem"] = np.ascontiguousarray(inp["mem"][b])
        pos = inp["positions"][b].astype(np.int32)
        m["pos_rep"] = np.ascontiguousarray(np.broadcast_to(pos[None, :], (128, S)))
        m["pos_tok"] = np.ascontiguousarray(pos.reshape(NT, 128).T)
        in_maps.append(m)
    res = run_bass_kernel_spmd(nc, in_maps, core_ids=list(range(B)))
    out = np.stack([np.asarray(r["y"]) for r in res.results], axis=0)
    return out.astype(np.float32, copy=False)
```

```python
import os
import math
import numpy as np
from contextlib import ExitStack
import concourse.bass as bass
import concourse.mybir as mybir
from concourse.bass_utils import run_bass_kernel_spmd

F32 = mybir.dt.float32
BF16 = mybir.dt.bfloat16
I32 = mybir.dt.int32
AF = mybir.ActivationFunctionType
ALU = mybir.AluOpType
AX = mybir.AxisListType

S = 2048
D = 1024
NT = 16
NB = 4
DFF = 2816
NG = 11
EPS = 1e-6
SLOT = 4096
NSLOT = 4
TWO_PI = 2.0 * math.pi

STAGE = int(os.environ.get("MK_STAGE", "3"))


class Prog:
    ENG = ("pe", "act", "dve", "pool", "sp")

    def __init__(self):
        self.streams = {e: [] for e in self.ENG}
        self.nops = {e: 0 for e in self.ENG}
        self.known = {e: {} for e in self.ENG}
        self.res = {}
        self.sig = {e: set() for e in self.ENG}
        self.dcount = {}

    def _deps(self, eng, reads, writes):
        need = {}

        def add(c):
            sk, idx, clock = c
            if sk == "pe" and eng == "pe":
                return
            if self.known[eng].get(sk, 0) >= idx:
                return
            cur = need.get(sk)
            if cur is None or cur[0] < idx:
                need[sk] = (idx, clock)

        for k in reads:
            st = self.res.get(k)
            if st is not None and st[0] is not None:
                add(st[0])
        for k in writes:
            st = self.res.get(k)
            if st is not None:
                if st[0] is not None:
                    add(st[0])
                for sk, (idx, clock) in st[1].items():
                    add((sk, idx, clock))
        kn = self.known[eng]
        for sk, (idx, clock) in need.items():
            if kn.get(sk, 0) >= idx:
                continue
            self.streams[eng].append(("w", sk, idx))
            if sk in self.sig:
                self.sig[sk].add(idx)
            for a, b in clock.items():
                if kn.get(a, 0) < b:
                    kn[a] = b
            kn[sk] = max(kn.get(sk, 0), idx)

    def _record(self, comp, reads, writes):
        sk, idx, clock = comp
        for k in writes:
            self.res[k] = [comp, {}]
        for k in reads:
            st = self.res.get(k)
            if st is None:
                st = [None, {}]
                self.res[k] = st
            cur = st[1].get(sk)
            if cur is None or cur[0] < idx:
                st[1][sk] = (idx, clock)

    def op(self, eng, fn, r=(), w=()):
        self._deps(eng, r, w)
        self.nops[eng] += 1
        idx = self.nops[eng]
        self.streams[eng].append(("o", fn, idx))
        clock = dict(self.known[eng])
        clock[eng] = idx
        self._record((eng, idx, clock), r, w)

    def dma(self, issuer, sem, fn, r=(), w=()):
        self._deps(issuer, r, w)
        sk = ("d", sem)
        self.dcount[sk] = self.dcount.get(sk, 0) + 1
        idx = self.dcount[sk]
        self.streams[issuer].append(("d", fn, sk))
        clock = dict(self.known[issuer])
        clock[sk] = idx
        self._record((sk, idx, clock), r, w)

    def barrier(self):
        pend = {}
        for st in self.res.values():
            if st[0] is not None:
                sk, idx, _ = st[0]
                pend[sk] = max(pend.get(sk, 0), idx)
            for sk, (idx, _) in st[1].items():
                pend[sk] = max(pend.get(sk, 0), idx)
        for e in self.ENG:
            kn = self.known[e]
            for sk, idx in pend.items():
                if kn.get(sk, 0) >= idx:
                    continue
                if sk == e and e == "pe":
                    pass
                self.streams[e].append(("w", sk, idx))
                if sk in self.sig:
                    self.sig[sk].add(idx)
                kn[sk] = idx
        self.res = {}

    def check(self):
        sigval = {}
        for e in self.ENG:
            m = {}
            c = 0
            for i in range(1, self.nops[e] + 1):
                if i in self.sig[e]:
                    c += 1
                    m[i] = c
            sigval[e] = m
        semv = {}
        pc = {e: 0 for e in self.ENG}
        progress = True
        while progress:
            progress = False
            for e in self.ENG:
                st = self.streams[e]
                while pc[e] < len(st):
                    ent = st[pc[e]]
                    if ent[0] == "w":
                        sk, idx = ent[1], ent[2]
                        need = 16 * idx if isinstance(sk, tuple) else sigval[sk][idx]
                        if semv.get(sk, 0) < need:
                            break
                    elif ent[0] == "o":
                        if ent[2] in self.sig[e]:
                            semv[e] = semv.get(e, 0) + 1
                    else:
                        semv[ent[2]] = semv.get(ent[2], 0) + 16
                    pc[e] += 1
                    progress = True
        stuck = {e: (pc[e], len(self.streams[e])) for e in self.ENG if pc[e] < len(self.streams[e])}
        if stuck:
            for e, (p, n) in stuck.items():
                print("STUCK", e, p, n, self.streams[e][p][:3], semv)
            raise RuntimeError("semaphore program deadlocks: %r" % (stuck,))
        return {e: len(self.streams[e]) for e in self.ENG}, {k: v for k, v in semv.items()}

    def emit(self, nc, es):
        sems = {e: es.enter_context(nc.semaphore("s_" + e)) for e in self.ENG}
        dsems = {sk: es.enter_context(nc.semaphore("d_" + sk[1])) for sk in self.dcount}
        sigval = {}
        for e in self.ENG:
            m = {}
            c = 0
            ss = self.sig[e]
            for i in range(1, self.nops[e] + 1):
                if i in ss:
                    c += 1
                    m[i] = c
            sigval[e] = m
        engobj = {"pe": "tensor", "act": "scalar", "dve": "vector", "pool": "gpsimd", "sp": "sync"}
        block = es.enter_context(nc.Block())
        for e in self.ENG:
            stream = self.streams[e]
            if not stream:
                continue

            def body(eng, stream=stream, e=e):
                for ent in stream:
                    if ent[0] == "w":
                        sk, idx = ent[1], ent[2]
                        if isinstance(sk, tuple):
                            eng.wait_ge(dsems[sk], 16 * idx)
                        else:
                            eng.wait_ge(sems[sk], sigval[sk][idx])
                    elif ent[0] == "o":
                        ins = ent[1](eng)
                        if ent[2] in self.sig[e]:
                            ins.then_inc(sems[e], 1)
                    else:
                        ins = ent[1](eng)
                        ins.then_inc(dsems[ent[2]], 16)

            getattr(block, engobj[e])(body)


def _kmaj(w):
    K, N = w.shape
    return np.ascontiguousarray(w.reshape(K // 128, 128, N).transpose(1, 0, 2))


def _cols(v):
    return np.ascontiguousarray(v.reshape(-1, 128).T)


RET_H = 4
RET_DK = 256
RET_DV = 512
OFF_RQ, OFF_RK, OFF_RV, OFF_RG = 0, 1024, 2048, 4096
OFF_DQ, OFF_DK, OFF_DV, OFF_MQ, OFF_GATE = 6144, 7168, 8192, 9216, 10240


def _prep_weights(inp):
    out = {}
    for tag in ("ffn1", "ffn2"):
        wg = inp[tag + "_w_gate"][0]
        wu = inp[tag + "_w_up"][0]
        wd = inp[tag + "_w_down"][0]
        gu = np.empty((NG, 128, 2, 8, 256), np.float32)
        dn = np.empty((NG, 128, 2, 1024), np.float32)
        for g in range(NG):
            gu[g, :, 0] = _kmaj(wg[:, g * 256:(g + 1) * 256])
            gu[g, :, 1] = _kmaj(wu[:, g * 256:(g + 1) * 256])
            dn[g] = _kmaj(wd[g * 256:(g + 1) * 256, :])
        out[tag + "_gu"] = gu.reshape(NG, 128, 4096)
        out[tag + "_dn"] = dn.reshape(NG, 128, 2048)
        g4 = np.empty((5, 128, 8, 512), np.float32)
        u4 = np.empty((5, 128, 8, 512), np.float32)
        d4 = np.empty((5, 128, 4, 1024), np.float32)
        for g in range(5):
            g4[g] = _kmaj(wg[:, g * 512:(g + 1) * 512])
            u4[g] = _kmaj(wu[:, g * 512:(g + 1) * 512])
            d4[g] = _kmaj(wd[g * 512:(g + 1) * 512, :])
        out[tag + "_g4"] = g4.reshape(5, 128, 4096)
        out[tag + "_u4"] = u4.reshape(5, 128, 4096)
        out[tag + "_d4"] = d4.reshape(5, 128, 4096)
        out[tag + "_gu"] = np.ascontiguousarray(out[tag + "_gu"][10:11])
        out[tag + "_dn"] = np.ascontiguousarray(out[tag + "_dn"][10:11])
    w_in = inp["w_in"][0]
    rqk = np.empty((RET_H, 128, 8, 4, 128), np.float32)
    rv = np.empty((RET_H, 128, 8, 512), np.float32)
    rg = np.empty((RET_H, 128, 8, 512), np.float32)
    for h in range(RET_H):
        q = w_in[:, OFF_RQ + h * 256: OFF_RQ + (h + 1) * 256]
        k = w_in[:, OFF_RK + h * 256: OFF_RK + (h + 1) * 256]
        rqk[h, :, :, 0] = _kmaj(q[:, 0::2])
        rqk[h, :, :, 1] = _kmaj(q[:, 1::2])
        rqk[h, :, :, 2] = _kmaj(k[:, 0::2])
        rqk[h, :, :, 3] = _kmaj(k[:, 1::2])
        rv[h] = _kmaj(w_in[:, OFF_RV + h * 512: OFF_RV + (h + 1) * 512])
        rg[h] = _kmaj(w_in[:, OFF_RG + h * 512: OFF_RG + (h + 1) * 512])
    out["ret_qk"] = rqk.reshape(RET_H, 128, 4096)
    out["ret_v"] = rv.reshape(RET_H, 128, 4096)
    out["ret_g"] = rg.reshape(RET_H, 128, 4096)
    dqk = np.empty((4, 128, 8, 512), np.float32)
    dv = np.empty((4, 128, 8, 256), np.float32)
    for g in range(4):
        dqk[g, :, :, 0:256] = _kmaj(w_in[:, OFF_DQ + g * 256: OFF_DQ + (g + 1) * 256])
        dqk[g, :, :, 256:512] = _kmaj(w_in[:, OFF_DK + g * 256: OFF_DK + (g + 1) * 256])
        dv[g] = _kmaj(w_in[:, OFF_DV + g * 256: OFF_DV + (g + 1) * 256])
    out["diff_qk"] = dqk.reshape(4, 128, 4096)
    out["diff_v"] = dv.reshape(4, 128, 2048)
    mq = np.empty((2, 128, 8, 512), np.float32)
    for g in range(2):
        mq[g] = _kmaj(w_in[:, OFF_MQ + g * 512: OFF_MQ + (g + 1) * 512])
    out["mem_q"] = mq.reshape(2, 128, 4096)
    mkv = inp["mem_w_kv"][0]
    kv = np.empty((4, 128, 8, 512), np.float32)
    for g in range(4):
        kv[g] = _kmaj(mkv[:, g * 512:(g + 1) * 512])
    out["mem_kv"] = kv.reshape(4, 128, 4096)
    wo_list = [inp["ret_w_o"][0][0:1024], inp["ret_w_o"][0][1024:2048], inp["diff_w_o"][0], inp["mem_w_o"][0]]
    gidx = [0, 0, 1, 2]
    ap_w = np.empty((4, 8, 128, 2, 8, 128), np.float32)
    for a in range(4):
        for dc in range(8):
            ap_w[a, dc, :, 0] = _kmaj(wo_list[a][:, dc * 128:(dc + 1) * 128])
            gc = OFF_GATE + gidx[a] * 1024 + dc * 128
            ap_w[a, dc, :, 1] = _kmaj(w_in[:, gc: gc + 128])
    out["app_w"] = ap_w.reshape(4, 8, 128, 2048)
    wout = inp["w_out"][0]
    wo2 = np.empty((2, 2, 128, 4, 512), np.float32)
    for half in range(2):
        for nh in range(2):
            wo2[half, nh] = _kmaj(wout[half * 512:(half + 1) * 512, nh * 512:(nh + 1) * 512])
    out["w_out"] = wo2.reshape(2, 2, 128, 2048)
    cols = [
        _cols(inp["ffn1_norm"][0]), _cols(inp["mix_norm"][0]), _cols(inp["ffn2_norm"][0]),
        _cols(inp["mem_norm"][0]), _cols(inp["b_gate"][0]),
        _cols(inp["diff_subln"][0]),
    ]
    out["cols"] = np.ascontiguousarray(np.concatenate(cols, axis=1))
    out["fin"] = np.ascontiguousarray(np.broadcast_to(inp["final_norm"][0][None, :], (128, 1024)))
    rep = np.concatenate([
        inp["diff_q_norm"][0], inp["diff_k_norm"][0],
        inp["mem_q_norm"][0], inp["mem_k_norm"][0],
        inp["diff_lambda_q1"][0], inp["diff_lambda_k1"][0], inp["diff_lambda_q2"][0], inp["diff_lambda_k2"][0],
    ])
    out["rep"] = np.ascontiguousarray(np.broadcast_to(rep[None, :], (128, rep.shape[0])))
    return out


COL_FFN1, COL_MIX, COL_FFN2, COL_MEMN, COL_BG, COL_SUBLN = 0, 8, 16, 24, 32, 56
NCOLS = 57
REP_DQN, REP_DKN, REP_MQN, REP_MKN, REP_LAM = 0, 64, 128, 384, 640
NREP = 640 + 256


def _prep_consts():
    c = {}
    c["ident"] = np.eye(128, dtype=np.float32)
    gam = [1.0 - 2.0 ** (-5.0 - h) for h in range(RET_H)]
    i = np.arange(128, dtype=np.float64)
    dm = np.empty((RET_H, 128, 128), np.float64)
    cc = np.empty((128, 3 * RET_H), np.float64)
    for h, g in enumerate(gam):
        lg = math.log(g)
        cI = i[None, :]
        eI = i[:, None]
        mask = (np.floor(eI / 64) <= np.floor(cI / 64))
        dm[h] = np.exp(lg * (np.abs(cI - eI) - (cI + 1.0))) * mask
        cc[:, h] = np.exp(lg * (i + 1.0))
        cc[:, RET_H + h] = np.exp(2 * lg * (i + 1.0)) / 512.0
        cc[:, 2 * RET_H + h] = np.exp(lg * (127.0 - i))
    c["dmask"] = np.ascontiguousarray(dm.transpose(1, 0, 2)).astype(np.float32)
    c["rdec"] = cc.astype(np.float32)
    ret_inv = (1.0 / (np.float32(10000.0) ** np.linspace(0.0, 1.0, 128, dtype=np.float32))).astype(np.float32)
    rope_inv = (1.0 / (np.float32(500000.0) ** (np.arange(0, 16, 2, dtype=np.float32) / np.float32(16)))).astype(np.float32)
    c["ret_inv"] = ret_inv.reshape(128, 1)
    c["rope_inv"] = np.ascontiguousarray(np.broadcast_to(rope_inv[None, :], (128, 8))).astype(np.float32)
    c["gamma128"] = [g ** 128 for g in gam]
    return c


def build_program(wshapes, stage):
    nc = bass.Bass("TRN2", target_bir_lowering=False)
    P = Prog()
    dr = {}

    def din(name, shape, dt=F32):
        dr[name] = nc.dram_tensor(name, list(shape), dt, kind="ExternalInput").ap()
        return dr[name]

    x_d = din("x", [S, D])
    mem_d = din("mem", [256, D])
    posr_d = din("pos_rep", [128, S], I32)
    post_d = din("pos_tok", [128, NT], I32)
    for k, shp in wshapes.items():
        din(k, shp)
    y_d = nc.dram_tensor("y", [S, D], F32, kind="ExternalOutput").ap()
    gamma128 = _prep_consts()["gamma128"]

    es = ExitStack()
    with es:
        def sb(name, shape, dt=F32):
            return es.enter_context(nc.sbuf_tensor("sb_" + name, list(shape), dt))

        X = sb("X", [128, NT, D])
        HT = sb("HT", [128, 8, S], BF16)
        slots = [sb(f"slot{i}", [128, SLOT], BF16) for i in range(NSLOT)]
        ident = sb("ident", [128, 128], BF16)
        ones = sb("ones", [128, 128], BF16)
        cols = sb("cols", [128, NCOLS])
        rep = sb("rep", [128, NREP])
        ss16 = sb("ss16", [128, NT])
        rs16 = sb("rs16", [128, NT])
        ARENA_BYTES = 72 * 1024 + 512
        arena = sb("arena", [128, ARENA_BYTES // 4])
        ps = [es.enter_context(nc.psum_tensor(f"ps{i}", [128, 512], F32)) for i in range(8)]
        psb = [p.bitcast(BF16) for p in ps]

        arena_b = arena.bitcast(BF16)

        def carve_f32(off_bytes, shape):
            n = int(np.prod(shape[1:]))
            o = off_bytes // 4
            ap = arena[:, o:o + n]
            if len(shape) == 3:
                ap = ap.rearrange("p (a b) -> p a b", a=shape[1])
            elif len(shape) == 4:
                ap = ap.rearrange("p (a b c) -> p a b c", a=shape[1], b=shape[2])
            return ap

        def carve_bf(off_bytes, shape):
            n = int(np.prod(shape[1:]))
            o = off_bytes // 2
            ap = arena_b[:, o:o + n]
            if len(shape) == 3:
                ap = ap.rearrange("p (a b) -> p a b", a=shape[1])
            elif len(shape) == 4:
                ap = ap.rearrange("p (a b c) -> p a b c", a=shape[1], b=shape[2])
            return ap

        def mm(out, lhsT, rhs, start, stop, r, w):
            P.op("pe", lambda e: e.matmul(out, lhsT, rhs, start=start, stop=stop), r=r, w=w)

        def tr(out, in_, r, w):
            P.op("pe", lambda e: e.transpose(out, in_, ident[:]), r=list(r) + ["ident"], w=w)

        def act(out, in_, func, r, w, bias=0.0, scale=1.0, accum_out=None, eng="act"):
            if accum_out is not None:
                P.op("act", lambda e: e.activation(out, in_, func, bias=bias, scale=scale, accum_out=accum_out), r=r, w=w)
            else:
                P.op("act", lambda e: e.activation(out, in_, func, bias=bias, scale=scale), r=r, w=w)

        def tt(out, in0, in1, op, r, w, eng="dve"):
            P.op(eng, lambda e: e.tensor_tensor(out, in0, in1, op), r=r, w=w)

        def tsc(out, in0, s1, s2, op0, op1, r, w, eng="dve"):
            if op1 is None:
                P.op(eng, lambda e: e.tensor_scalar(out, in0, s1, None, op0), r=r, w=w)
            else:
                P.op(eng, lambda e: e.tensor_scalar(out, in0, s1, s2, op0, op1), r=r, w=w)

        def stt(out, in0, scalar, in1, op0, op1, r, w):
            P.op("dve", lambda e: e.scalar_tensor_tensor(out, in0, scalar, in1, op0, op1), r=r, w=w)

        def cp(out, in_, r, w, eng="dve"):
            if eng == "act":
                P.op("act", lambda e: e.activation(out, in_, AF.Copy), r=r, w=w)
            else:
                P.op(eng, lambda e: e.tensor_copy(out, in_), r=r, w=w)

        def recip(out, in_, r, w):
            P.op("dve", lambda e: e.reciprocal(out, in_), r=r, w=w)

        def memset(ap, val, w, eng="dve"):
            P.op(eng, lambda e: e.memset(ap, val), r=(), w=w)

        slot_ctr = [0]
        slots_static = list(slots)
        cur_slots = list(slots)

        def load_slot(parts):
            i = slot_ctr[0] % len(cur_slots)
            slot_ctr[0] += 1
            slots = list(cur_slots)
            off = 0
            for (ap, n) in parts:
                o = off
                P.dma("pool", f"slot{i}", lambda e, ap=ap, o=o, n=n, st=cur_slots[i]: e.dma_start(out=st[:, o:o + n], in_=ap, max_dma_last_dim=2048),
                      r=(), w=[("slot", i)])
                off += n
            return slots[i], ("slot", i)

        def const_load(dst, src, key, sem="c"):
            P.dma("sp", sem, lambda e: e.dma_start(out=dst, in_=src), r=(), w=[key])

        P.dma("pool", "c0", lambda e: e.dma_start(out=ident[:], in_=dr["ident"]), r=(), w=["ident"])
        const_load(cols[:], dr["cols"], "cols", "c1")
        const_load(rep[:], dr["rep"], "rep", "c2")
        for q in range(4):
            P.dma("sp", f"x{q}", lambda e, q=q: e.dma_start(
                out=X[:, q * 4:(q + 1) * 4, :],
                in_=x_d[q * 512:(q + 1) * 512, :].rearrange("(t p) d -> p t d", p=128)),
                r=(), w=[("X", t) for t in range(q * 4, q * 4 + 4)])
        memset(ones[:], 1.0, w=["ones"])

        def norm_to_HT(gcol, junk, xn):
            for t in range(NT):
                act(junk, X[:, t, :], AF.Square, r=[("X", t)], w=["junk", ("ss", t)], accum_out=ss16[:, t:t + 1])
            tsc(rs16[:], ss16[:], 1.0 / D, EPS, ALU.mult, ALU.add, r=[("ss", t) for t in range(NT)], w=["rs16"])
            act(rs16[:], rs16[:], AF.Sqrt, r=["rs16"], w=["rs16"])
            recip(rs16[:], rs16[:], r=["rs16"], w=["rs16"])
            for t in range(NT):
                xb = xn[t % 2]
                act(xb, X[:, t, :], AF.Copy, r=[("X", t), "rs16"], w=[("xn", t % 2)], scale=rs16[:, t:t + 1])
                bank = 6 + (t % 2)
                for k in range(8):
                    tr(psb[bank][:, k * 128:(k + 1) * 128], xb[:, k * 128:(k + 1) * 128],
                       r=[("xn", t % 2)], w=[("ps", bank)])
                tt(HT[:, :, t * 128:(t + 1) * 128],
                   psb[bank][:, 0:1024].rearrange("p (k c) -> p k c", k=8),
                   cols[:, gcol:gcol + 8].to_broadcast([128, 8, 128]), ALU.mult,
                   r=[("ps", bank), "cols"], w=[("HT", t // 4)])

        def ffn(tag, gcol):
            junk = carve_bf(0, [128, 1024])
            xn = [carve_bf(2048, [128, 1024]), carve_bf(4096, [128, 1024])]
            AT = [carve_bf(8192 + i * 16384, [128, 4, S]) for i in range(2)]
            SG = [carve_f32(40960 + i * 2048, [128, 512]) for i in range(2)]
            xs = [arena_b[:, (45056 + i * 8192) // 2:(45056 + i * 8192) // 2 + SLOT] for i in range(3)]
            assert 45056 + 3 * 8192 <= ARENA_BYTES
            cur_slots[:] = list(slots_static) + xs
            slot_ctr[0] = 0
            norm_to_HT(gcol, junk, xn)
            slot_ctr[0] = 0
            cnt = 0
            ocnt = 0
            for g in range(6):
                nch = 4 if g < 5 else 2
                if nch == 4:
                    wg_, kg_ = load_slot([(dr[tag + "_g4"][g], 4096)])
                    wu_, ku_ = load_slot([(dr[tag + "_u4"][g], 4096)])
                    wd_, kd_ = load_slot([(dr[tag + "_d4"][g], 4096)])
                    wg_v = wg_[:, 0:4096].rearrange("p (k c) -> p k c", k=8)
                    wu_v = wu_[:, 0:4096].rearrange("p (k c) -> p k c", k=8)
                    wd_v = wd_[:, 0:4096].rearrange("p (c n) -> p c n", c=4)
                else:
                    wgu, kg_ = load_slot([(dr[tag + "_gu"][0], 4096)])
                    ku_ = kg_
                    wd_, kd_ = load_slot([(dr[tag + "_dn"][0], 2048)])
                    wgu_v = wgu[:, 0:4096].rearrange("p (a k c) -> p a k c", a=2, k=8)
                    wg_v = wgu_v[:, 0, :, :]
                    wu_v = wgu_v[:, 1, :, :]
                    wd_v = wd_[:, 0:2048].rearrange("p (c n) -> p c n", c=2)
                at = AT[g % 2]
                for tb in range(NB):
                    for c in range(nch):
                        bg = (cnt % 2)
                        bu = 2 + (cnt % 2)
                        for k in range(8):
                            mm(ps[bg][:, :], wg_v[:, k, c * 128:(c + 1) * 128], HT[:, k, tb * 512:(tb + 1) * 512],
                               k == 0, k == 7, r=[kg_, ("HT", tb)], w=[("ps", bg)])
                        for k in range(8):
                            mm(ps[bu][:, :], wu_v[:, k, c * 128:(c + 1) * 128], HT[:, k, tb * 512:(tb + 1) * 512],
                               k == 0, k == 7, r=[ku_, ("HT", tb)], w=[("ps", bu)])
                        sg = SG[cnt % 2]
                        act(sg, ps[bg][:, :], AF.Silu, r=[("ps", bg)], w=[("SG", cnt % 2)])
                        tt(at[:, c, tb * 512:(tb + 1) * 512], sg, ps[bu][:, :], ALU.mult,
                           r=[("SG", cnt % 2), ("ps", bu)], w=[("AT", g % 2, tb)])
                        cnt += 1
                    for t4 in range(4):
                        t = tb * 4 + t4
                        for nh in range(2):
                            bo = 4 + (ocnt % 3)
                            ocnt += 1
                            for c in range(nch):
                                mm(ps[bo][:, :], at[:, c, t * 128:(t + 1) * 128], wd_v[:, c, nh * 512:(nh + 1) * 512],
                                   c == 0, c == nch - 1, r=[("AT", g % 2, tb), kd_], w=[("ps", bo)])
                            stt(X[:, t, nh * 512:(nh + 1) * 512], ps[bo][:, :], 0.5, X[:, t, nh * 512:(nh + 1) * 512],
                                ALU.mult, ALU.add, r=[("ps", bo), ("X", t)], w=[("X", t)])
            P.barrier()
            cur_slots[:] = list(slots_static)
            slot_ctr[0] = 0

        BT_OFF = 0
        RO_ = 32768

        def mid_bcast(ap2, reps):
            a = ap2.ap
            return bass.AP(ap2.tensor, ap2.offset, [list(a[0]), [0, reps], list(a[1])])

        def range_sin(dst, src, shift, T1, T2, T2i, keys_r, key_w):
            INV = 1.0 / TWO_PI
            C1 = 6.28125
            C2 = TWO_PI - C1
            tsc(T1, src, INV, shift * INV + 0.5, ALU.mult, ALU.add, r=keys_r, w=["rsT1"])
            cp(T2i, T1, r=["rsT1"], w=["rsT2"])
            cp(T1, T2i, r=["rsT2"], w=["rsT1"])
            stt(T2, T1, -C1, src, ALU.mult, ALU.add, r=["rsT1"] + keys_r, w=["rsT2"])
            stt(T2, T1, -C2, T2, ALU.mult, ALU.add, r=["rsT1", "rsT2"], w=["rsT2"])
            if shift != 0.0:
                tsc(T2, T2, shift, None, ALU.add, None, r=["rsT2"], w=["rsT2"])
            tsc(T1, T2, math.pi, -1e30, ALU.add, ALU.mult, r=["rsT2"], w=["rsT1"])
            tsc(T1, T1, 0.0, 1.0, ALU.max, ALU.min, r=["rsT1"], w=["rsT1"])
            stt(T2, T1, TWO_PI, T2, ALU.mult, ALU.add, r=["rsT1", "rsT2"], w=["rsT2"])
            tsc(T1, T2, -math.pi, 1e30, ALU.add, ALU.mult, r=["rsT2"], w=["rsT1"])
            tsc(T1, T1, 0.0, 1.0, ALU.max, ALU.min, r=["rsT1"], w=["rsT1"])
            stt(T2, T1, -TWO_PI, T2, ALU.mult, ALU.add, r=["rsT1", "rsT2"], w=["rsT2"])
            tsc(T2, T2, math.pi - 1e-6, -(math.pi - 1e-6), ALU.min, ALU.max, r=["rsT2"], w=["rsT2"])
            act(dst, T2, AF.Sin, r=["rsT2"], w=[key_w])

        small = sb("small", [128, 136])
        rdec = sb("rdec", [128, 12])
        retinv = sb("retinv", [128, 1])
        ropeinv = sb("ropeinv", [128, 8])
        dmask = sb("dmask", [128, RET_H, 128])
        P.dma("sp", "c5", lambda e: e.dma_start(out=retinv[:], in_=dr["ret_inv"]), r=(), w=["retinv"])
        P.dma("sp", "c6", lambda e: e.dma_start(out=rdec[:], in_=dr["rdec"]), r=(), w=["rdec"])
        P.dma("sp", "c7", lambda e: e.dma_start(out=dmask[:].rearrange("p h c -> p (h c)"), in_=dr["dmask"]), r=(), w=["dmask"])
        P.dma("sp", "c9", lambda e: e.dma_start(out=ropeinv[:], in_=dr["rope_inv"]), r=(), w=["ropeinv"])

        def apply_branch(a, gidx):
            MTh = carve_bf(RO_ + 16384, [128, 4, S])
            SGa = [carve_f32(RO_ + 32768 + i * 2048, [128, 512]) for i in range(2)]
            BT = carve_bf(BT_OFF, [128, 8, S])
            cnt = 0
            ocnt = 0
            for half in range(2):
                for dcl in range(4):
                    dc = half * 4 + dcl
                    w, kw = load_slot([(dr["app_w"][a, dc], 2048)])
                    wv = w[:, 0:2048].rearrange("p (a k c) -> p a k c", a=2, k=8)
                    for tb in range(NB):
                        bp = cnt % 2
                        bgt = 2 + cnt % 2
                        for k in range(8):
                            mm(ps[bp][:, :], wv[:, 0, k, :], BT[:, k, tb * 512:(tb + 1) * 512], k == 0, k == 7,
                               r=[kw, ("BT", tb)], w=[("ps", bp)])
                        for k in range(8):
                            mm(ps[bgt][:, :], wv[:, 1, k, :], HT[:, k, tb * 512:(tb + 1) * 512], k == 0, k == 7,
                               r=[kw, ("HT", tb)], w=[("ps", bgt)])
                        sg = SGa[cnt % 2]
                        bcol = COL_BG + gidx * 8 + dc
                        act(sg, ps[bgt][:, :], AF.Sigmoid, r=[("ps", bgt), "cols"], w=[("SGa", cnt % 2)],
                            bias=cols[:, bcol:bcol + 1])
                        tt(MTh[:, dcl, tb * 512:(tb + 1) * 512], sg, ps[bp][:, :], ALU.mult,
                           r=[("SGa", cnt % 2), ("ps", bp)], w=[("MT", tb)])
                        cnt += 1
                for nh in range(2):
                    w, kw = load_slot([(dr["w_out"][half, nh], 2048)])
                    wv = w[:, 0:2048].rearrange("p (k n) -> p k n", k=4)
                    for t in range(NT):
                        bo = 4 + ocnt % 2
                        ocnt += 1
                        for k in range(4):
                            mm(ps[bo][:, :], MTh[:, k, t * 128:(t + 1) * 128], wv[:, k, :], k == 0, k == 3,
                               r=[("MT", t // 4), kw], w=[("ps", bo)])
                        tt(X[:, t, nh * 512:(nh + 1) * 512], X[:, t, nh * 512:(nh + 1) * 512], ps[bo][:, :], ALU.add,
                           r=[("ps", bo), ("X", t)], w=[("X", t)])

        def ret_tables():
            TABC = carve_f32(RO_, [128, S])
            TABS = carve_f32(RO_ + 8192, [128, S])
            T1 = carve_f32(RO_ + 16384, [128, S])
            T2 = carve_f32(RO_ + 24576, [128, S])
            T2i = T2.bitcast(I32)
            ANG = carve_f32(RO_ + 32768, [128, S])
            ANGi = ANG.bitcast(I32)
            P.dma("sp", "c4", lambda e: e.dma_start(out=ANGi, in_=posr_d), r=(), w=["ANGi"])
            cp(T1, ANGi, r=["ANGi"], w=["posf"])
            tsc(ANG, T1, retinv[:, 0:1], None, ALU.mult, None, r=["posf", "retinv", "ANGi"], w=["ANG"])
            range_sin(TABS, ANG, 0.0, T1, T2, T2i, ["ANG"], "TAB")
            range_sin(TABC, ANG, math.pi / 2, T1, T2, T2i, ["ANG"], "TAB")

        def lam_compute():
            memset(small[:, 6:7], EPS, w=["epsc"])
            pr = small[:, 64:128]
            for i, (a_, b_) in enumerate(((0, 64), (128, 192))):
                tt(pr, rep[:, REP_LAM + a_:REP_LAM + a_ + 64], rep[:, REP_LAM + b_:REP_LAM + b_ + 64], ALU.mult,
                   r=["rep"], w=["lam_pr"])
                P.op("dve", lambda e, i=i: e.reduce_sum(small[:, 1 + i:2 + i], pr, AX.X), r=["lam_pr"], w=[("lam_s", i)])
            act(small[:, 1:3], small[:, 1:3], AF.Exp, r=[("lam_s", 0), ("lam_s", 1)], w=["lam_e"])
            tt(small[:, 3:4], small[:, 2:3], small[:, 1:2], ALU.subtract, r=["lam_e"], w=["lam_d"])
            tsc(small[:, 0:1], small[:, 3:4], -0.2, None, ALU.add, None, r=["lam_d"], w=["neglam"])

        def ret_head(h, bt_base):
            BT = carve_bf(BT_OFF, [128, 8, S])
            TABC = carve_f32(RO_, [128, S])
            TABS = carve_f32(RO_ + 8192, [128, S])
            o = RO_ + 16384
            QT = carve_bf(o, [128, 2, 512]); o += 2048
            KT = carve_bf(o, [128, 2, 512]); o += 2048
            KTOK = carve_bf(o, [128, 4, 256]); o += 2048
            VTOK = carve_bf(o, [128, 4, 512]); o += 4096
            SGt = carve_bf(o, [128, 4, 512]); o += 4096
            Sf = carve_f32(o, [128, 2, 512]); o += 4096
            Sbf = carve_bf(o, [128, 2, 512]); o += 2048
            SCT = carve_bf(o, [128, 128]); o += 256
            RO = [carve_bf(o, [128, 512]), carve_bf(o + 1024, [128, 512])]; o += 2048
            T1 = carve_f32(o, [128, 512]); o += 2048
            T2 = ps[7][:, :]
            assert o <= ARENA_BYTES, o
            wqk, kqk = load_slot([(dr["ret_qk"][h], 4096)])
            wv, kv_ = load_slot([(dr["ret_v"][h], 4096)])
            wg, kg = load_slot([(dr["ret_g"][h], 4096)])
            wqk_v = wqk[:, :].rearrange("p (k j c) -> p k j c", k=8, j=4)
            wv_v = wv[:, :].rearrange("p (k n) -> p k n", k=8)
            wg_v = wg[:, :].rearrange("p (k n) -> p k n", k=8)
            g128 = float(gamma128[h])
            pending = [None]

            def flush_ro(ri, t, tb):
                for fc in range(4):
                    tr(psb[7][:, fc * 128:(fc + 1) * 128], RO[ri][:, fc * 128:(fc + 1) * 128], r=[("RO", ri)], w=[("ps", 7)])
                cp(BT[:, bt_base:bt_base + 4, t * 128:(t + 1) * 128],
                   psb[7][:, 0:512].rearrange("p (f c) -> p f c", f=4), r=[("ps", 7)], w=[("BT", tb)], eng="act")
            for tb in range(NB):
                tbs = slice(tb * 512, (tb + 1) * 512)
                for (j0, b0) in ((0, 0), (2, 4)):
                    for jj in range(2):
                        for k in range(8):
                            mm(ps[b0 + jj][:, :], wqk_v[:, k, j0 + jj, :], HT[:, k, tbs], k == 0, k == 7,
                               r=[kqk, ("HT", tb)], w=[("ps", b0 + jj)])
                for ci in range(4):
                    t = tb * 4 + ci
                    ts_ = slice(t * 128, (t + 1) * 128)
                    bv = 2 if ci % 2 == 0 else 6
                    for k in range(8):
                        mm(ps[bv][:, :], HT[:, k, ts_], wv_v[:, k, :], k == 0, k == 7, r=[kv_, ("HT", tb)], w=[("ps", bv)])
                    act(VTOK[:, ci, :], ps[bv][:, :], AF.Copy, r=[("ps", bv)], w=[("VTOK", ci)])
                C = TABC[:, tbs]
                Sn = TABS[:, tbs]
                for (b0, dst, scale, key) in ((0, QT, 1.0, "QT"), (4, KT, 1.0 / 16.0, "KT")):
                    pe_, po_ = ps[b0], ps[b0 + 1]
                    stt(T1, pe_[:, :], scale, C, ALU.mult, ALU.mult, r=[("ps", b0), "TAB"], w=["T1"])
                    stt(T2, po_[:, :], scale, Sn, ALU.mult, ALU.mult, r=[("ps", b0 + 1), "TAB"], w=[("ps", 7)])
                    tt(dst[:, 0, :], T1, T2, ALU.subtract, r=["T1", ("ps", 7)], w=[key])
                    stt(T1, po_[:, :], scale, C, ALU.mult, ALU.mult, r=[("ps", b0 + 1), "TAB"], w=["T1"])
                    stt(T2, pe_[:, :], scale, Sn, ALU.mult, ALU.mult, r=[("ps", b0), "TAB"], w=[("ps", 7)])
                    tt(dst[:, 1, :], T1, T2, ALU.add, r=["T1", ("ps", 7)], w=[key])
                for ci in range(4):
                    t = tb * 4 + ci
                    ts_ = slice(t * 128, (t + 1) * 128)
                    bg_ = 3 if ci % 2 == 0 else 6
                    for k in range(8):
                        mm(ps[bg_][:, :], HT[:, k, ts_], wg_v[:, k, :], k == 0, k == 7, r=[kg, ("HT", tb)], w=[("ps", bg_)])
                    act(SGt[:, ci, :], ps[bg_][:, :], AF.Silu, r=[("ps", bg_)], w=[("SGt", ci)])
                for ci in range(4):
                    cs = slice(ci * 128, (ci + 1) * 128)
                    for c in range(2):
                        tr(psb[4][:, c * 128:(c + 1) * 128], KT[:, c, cs], r=["KT"], w=[("ps", 4)])
                    tsc(KTOK[:, ci, :], psb[4][:, 0:256], rdec[:, 8 + h:9 + h], None, ALU.mult, None,
                        r=[("ps", 4), "rdec"], w=[("KTOK", ci)])
                for ci in range(4):
                    n = tb * 4 + ci
                    t = n
                    cs = slice(ci * 128, (ci + 1) * 128)
                    ts_ = slice(t * 128, (t + 1) * 128)
                    rob = RO[n % 2]
                    for c in range(2):
                        mm(ps[5][:, 0:128], KT[:, c, cs], QT[:, c, cs], c == 0, c == 1, r=["KT", "QT"], w=[("ps", 5)])
                    tt(SCT, ps[5][:, 0:128], dmask[:, h, :], ALU.mult, r=[("ps", 5), "dmask"], w=["SCT"])
                    if n < 15:
                        for c in range(2):
                            mm(ps[c][:, :], KTOK[:, ci, c * 128:(c + 1) * 128], VTOK[:, ci, :], True, True,
                               r=[("KTOK", ci), ("VTOK", ci)], w=[("ps", c)])
                    mm(ps[6][:, :], SCT, VTOK[:, ci, :], True, n == 0, r=["SCT", ("VTOK", ci)], w=[("ps", 6)])
                    if n > 0:
                        for c in range(2):
                            mm(ps[6][:, :], QT[:, c, cs], Sbf[:, c, :], False, c == 1, r=["QT", ("Sbf", c)], w=[("ps", 6)])
                    if n < 15:
                        for c in range(2):
                            if n == 0:
                                cp(Sf[:, c, :], ps[c][:, :], r=[("ps", c)], w=[("Sf", c)])
                            else:
                                stt(Sf[:, c, :], Sf[:, c, :], g128, ps[c][:, :], ALU.mult, ALU.add,
                                    r=[("ps", c), ("Sf", c)], w=[("Sf", c)])
                            cp(Sbf[:, c, :], Sf[:, c, :], r=[("Sf", c)], w=[("Sbf", c)], eng="act")
                    act(T1, ps[6][:, :], AF.Square, r=[("ps", 6)], w=["T1", "rss"], accum_out=small[:, 4:5])
                    tt(small[:, 5:6], small[:, 4:5], rdec[:, 4 + h:5 + h], ALU.mult, r=["rss", "rdec"], w=["rf"])
                    tsc(small[:, 5:6], small[:, 5:6], EPS, None, ALU.add, None, r=["rf"], w=["rf"])
                    act(small[:, 5:6], small[:, 5:6], AF.Sqrt, r=["rf"], w=["rf"])
                    recip(small[:, 5:6], small[:, 5:6], r=["rf"], w=["rf"])
                    tt(small[:, 5:6], small[:, 5:6], rdec[:, h:h + 1], ALU.mult, r=["rf", "rdec"], w=["rf"])
                    stt(rob, ps[6][:, :], small[:, 5:6], SGt[:, ci, :], ALU.mult, ALU.mult,
                        r=[("ps", 6), "rf", ("SGt", ci)], w=[("RO", n % 2)])
                    if pending[0] is not None:
                        flush_ro(*pending[0])
                    pending[0] = (n % 2, t, tb)
            flush_ro(*pending[0])

        def diff_tables():
            DS8 = carve_f32(RO_ + 29696, [128, 16, 8])
            DC8 = carve_f32(RO_ + 30208, [128, 16, 8])
            DSS = carve_f32(RO_ + 28672, [128, 16, 16])
            DCC = carve_f32(RO_ + 33792, [128, 16, 16])
            A = carve_f32(RO_ + 34816, [128, 16, 8])
            T1 = carve_f32(RO_ + 34816 + 512, [128, 16, 8])
            T2 = carve_f32(RO_ + 34816 + 1024, [128, 16, 8])
            T2i = T2.bitcast(I32)
            PI_ = carve_f32(RO_ + 34816 + 1536, [128, 16]).bitcast(I32)
            PF = carve_f32(RO_ + 34816 + 1600, [128, 16])
            P.dma("sp", "c8", lambda e: e.dma_start(out=PI_, in_=post_d), r=(), w=["PI"])
            cp(PF, PI_, r=["PI"], w=["PF"])
            tt(A, PF.to_broadcast([128, 16, 8]),
               mid_bcast(ropeinv[:, 0:8], 16), ALU.mult, r=["PF", "ropeinv"], w=["DA"])
            range_sin(DS8, A, 0.0, T1, T2, T2i, ["DA"], "DT8")
            range_sin(DC8, A, math.pi / 2, T1, T2, T2i, ["DA"], "DT8")
            cp(DCC[:, :, 0:8], DC8, r=["DT8"], w=["DTAB"])
            cp(DCC[:, :, 8:16], DC8, r=["DT8"], w=["DTAB"])
            cp(DSS[:, :, 8:16], DS8, r=["DT8"], w=["DTAB"])
            tsc(DSS[:, :, 0:8], DS8, -1.0, None, ALU.mult, None, r=["DT8"], w=["DTAB"])

        def diff_group(gi):
            BT = carve_bf(BT_OFF, [128, 8, S])
            QTd = carve_bf(RO_, [128, 2, S])
            KTd = carve_bf(RO_ + 8192, [128, 2, S])
            VTd = carve_bf(RO_ + 16384, [128, 16, 256])
            PT = [[carve_bf(RO_ + 24576 + (b * 2 + r_) * 1024, [128, 512]) for r_ in range(2)] for b in range(2)]
            DSS = carve_f32(RO_ + 28672, [128, 16, 16])
            SQ = carve_f32(RO_ + 30720, [128, 512])
            QKb = carve_bf(RO_ + 32768, [128, 512])
            DCC = carve_f32(RO_ + 33792, [128, 16, 16])
            A0 = carve_f32(RO_ + 34816, [128, 512])
            A1 = carve_f32(RO_ + 36864, [128, 512])
            RD = carve_f32(RO_ + 38912, [128, 512])
            QKf = A1
            XR = RD[:, 0:128].rearrange("p (g d) -> p g d", g=8)
            RA = RD[:, 128:256].rearrange("p (g d) -> p g d", g=8)
            RB = RD[:, 256:384].rearrange("p (g d) -> p g d", g=8)
            assert RO_ + 40960 <= ARENA_BYTES
            SQb = SQ.bitcast(BF16)[:, 0:512]
            wqk, kqk = load_slot([(dr["diff_qk"][gi], 4096)])
            wv, kv_ = load_slot([(dr["diff_v"][gi], 2048)])
            wqk_v = wqk[:, :].rearrange("p (k n) -> p k n", k=8)
            wv_v = wv[:, 0:2048].rearrange("p (k n) -> p k n", k=8)
            QKf3 = QKf.rearrange("p (g d) -> p g d", g=8)
            QKb3 = QKb.rearrange("p (g d) -> p g d", g=8)
            gain_ap = bass.AP(rep[:, 0:1].tensor, REP_DQN, [[NREP, 128], [64, 2], [0, 4], [1, 64]])
            gain16_ap = bass.AP(rep[:, 0:1].tensor, REP_DQN, [[NREP, 128], [64, 2], [0, 4], [1, 16]])

            def dproj(t):
                ts_ = slice(t * 128, (t + 1) * 128)
                bq = t % 2
                bv = 2 + t % 2
                for k in range(8):
                    mm(ps[bq][:, :], HT[:, k, ts_], wqk_v[:, k, :], k == 0, k == 7, r=[kqk, ("HT", t // 4)], w=[("ps", bq)])
                for k in range(8):
                    mm(ps[bv][:, 0:256], HT[:, k, ts_], wv_v[:, k, :], k == 0, k == 7, r=[kv_, ("HT", t // 4)], w=[("ps", bv)])

            def actpre(t):
                bq = t % 2
                bv = 2 + t % 2
                act(VTd[:, t, :], ps[bv][:, 0:256], AF.Copy, r=[("ps", bv)], w=[("VTd", t)])
                act(SQ, ps[bq][:, :], AF.Square, r=[("ps", bq)], w=["SQ"])

            dproj(0)
            actpre(0)
            for t in range(NT):
                ts_ = slice(t * 128, (t + 1) * 128)
                bq = t % 2
                btr = 4 + t % 2
                if t + 1 < NT:
                    dproj(t + 1)
                P.op("dve", lambda e: e.reduce_sum(small[:, 8:16], SQ.rearrange("p (g d) -> p g d", g=8), AX.X),
                     r=["SQ"], w=["ss8"])
                tsc(small[:, 8:16], small[:, 8:16], 1.0 / 64.0, EPS, ALU.mult, ALU.add, r=["ss8"], w=["ss8"])
                act(small[:, 8:16], small[:, 8:16], AF.Sqrt, r=["ss8"], w=["ss8"])
                recip(small[:, 8:16], small[:, 8:16], r=["ss8"], w=["ss8"])
                tt(QKf3, ps[bq][:, :].rearrange("p (g d) -> p g d", g=8), small[:, 8:16].to_broadcast([128, 8, 64]), ALU.mult,
                   r=[("ps", bq), "ss8"], w=["A1"])
                if t + 1 < NT:
                    actpre(t + 1)
                tt(QKb.rearrange("p (a b d) -> p a b d", a=2, b=4), QKf.rearrange("p (a b d) -> p a b d", a=2, b=4), gain_ap,
                   ALU.mult, r=["A1", "rep"], w=["QKb"])
                tt(XR.rearrange("p (a b) d -> p a b d", a=2), QKf.rearrange("p (a b d) -> p a b d", a=2, b=4)[:, :, :, 0:16], gain16_ap,
                   ALU.mult, r=["A1", "rep"], w=["RD"])
                ccb = mid_bcast(DCC[:, t, :], 8)
                tt(RA, XR, ccb, ALU.mult, r=["RD", "DTAB"], w=["RD"])
                tt(RB[:, :, 0:8], XR[:, :, 8:16], mid_bcast(DSS[:, t, 0:8], 8), ALU.mult, r=["RD", "DTAB"], w=["RD"])
                tt(RB[:, :, 8:16], XR[:, :, 0:8], mid_bcast(DSS[:, t, 8:16], 8), ALU.mult, r=["RD", "DTAB"], w=["RD"])
                tt(QKb3[:, :, 0:16], RA, RB, ALU.add, r=["RD", "QKb"], w=["QKb"])
                for j in range(4):
                    tr(psb[btr][:, j * 128:(j + 1) * 128], QKb[:, j * 128:(j + 1) * 128], r=["QKb"], w=[("ps", btr)])
                cp(QTd[:, :, ts_], psb[btr][:, 0:256].rearrange("p (h c) -> p h c", h=2), r=[("ps", btr)], w=[("QTd", t // 4)], eng="act")
                cp(KTd[:, :, ts_], psb[btr][:, 256:512].rearrange("p (h c) -> p h c", h=2), r=[("ps", btr)], w=[("KTd", t // 4)], eng="act")
            steps = []
            for hh in range(2):
                for qb in range(NB):
                    nkt = 4 * (qb + 1)
                    for kt in range(nkt):
                        steps.append((hh, qb, kt, nkt))

            def emit_scores(i):
                hh, qb, kt, nkt = steps[i]
                c0 = max(0, kt - 4 * qb) * 128
                b = i % 2
                for r_ in range(2):
                    pr = slice(r_ * 64, (r_ + 1) * 64)
                    sbank = b * 2 + r_
                    mm(ps[sbank][:, c0:512], KTd[pr, hh, kt * 128:(kt + 1) * 128],
                       QTd[pr, hh, qb * 512 + c0:(qb + 1) * 512], True, True,
                       r=[("KTd", kt // 4), ("QTd", qb)], w=[("ps", sbank)])

            def emit_exp(i):
                hh, qb, kt, nkt = steps[i]
                c0 = max(0, kt - 4 * qb) * 128
                b = i % 2
                for r_ in range(2):
                    sbank = b * 2 + r_
                    pt = PT[b][r_]
                    act(pt[:, c0:512], ps[sbank][:, c0:512], AF.Exp, r=[("ps", sbank)], w=[("PT", b, r_)], scale=0.125)
                    if kt >= 4 * qb:
                        memset(pt[64:128, c0:c0 + 64], 0.0, w=[("PT", b, r_)], eng="dve")

            def emit_pv(i):
                hh, qb, kt, nkt = steps[i]
                c0 = max(0, kt - 4 * qb) * 128
                b = i % 2
                for r_ in range(2):
                    pt = PT[b][r_]
                    mm(ps[4 + r_][:, c0:512], VTd[:, kt, hh * 128:(hh + 1) * 128], pt[:, c0:512], kt == 0, kt == nkt - 1,
                       r=[("VTd", kt), ("PT", b, r_)], w=[("ps", 4 + r_)])
                    mm(ps[6 + r_][:, c0:512], ones[:, :], pt[:, c0:512], kt == 0, kt == nkt - 1,
                       r=["ones", ("PT", b, r_)], w=[("ps", 6 + r_)])

            def finalize(hh, qb):
                h = 2 * gi + hh
                qs = slice(qb * 512, (qb + 1) * 512)
                RDb = SQ
                SQq = QKb
                act(RD, ps[6][:, :], AF.Ln, r=[("ps", 6)], w=["RD"])
                act(RD, RD, AF.Exp, r=["RD"], w=["RD"], scale=-1.0)
                act(RDb, ps[7][:, :], AF.Ln, r=[("ps", 7)], w=["SQ"])
                act(RDb, RDb, AF.Exp, r=["SQ"], w=["SQ"], scale=-1.0)
                tt(A0, ps[4][:, :], RD, ALU.mult, r=[("ps", 4), "RD"], w=["A0"])
                tt(A1, ps[5][:, :], RDb, ALU.mult, r=[("ps", 5), "SQ"], w=["A1"])
                stt(A0, A1, small[:, 0:1], A0, ALU.mult, ALU.add, r=["A0", "A1", "neglam"], w=["A0"])
                act(SQq, A0, AF.Square, r=["A0"], w=["QKb"])
                mm(ps[6][:, :], ones[:, :], SQq, True, True, r=["ones", "QKb"], w=[("ps", 6)])
                act(RD, ps[6][:, :], AF.Ln, r=[("ps", 6), "epsc"], w=["RD"], scale=1.0 / 128.0, bias=small[:, 6:7])
                act(RD, RD, AF.Exp, r=["RD"], w=["RD"], scale=-0.5)
                tt(A0, A0, RD, ALU.mult, r=["A0", "RD"], w=["A0"])
                tsc(BT[:, h, qs], A0, cols[:, COL_SUBLN:COL_SUBLN + 1], 0.8, ALU.mult, ALU.mult,
                    r=["A0", "cols"], w=[("BT", qb)])

            emit_scores(0)
            for i in range(len(steps)):
                if i + 1 < len(steps):
                    emit_scores(i + 1)
                emit_exp(i)
                emit_pv(i)
                hh, qb, kt, nkt = steps[i]
                if kt == nkt - 1:
                    finalize(hh, qb)

        def group_rms_pre(psrc, G, d, SQ, keyp, par=0):
            n = G * d
            act(SQ[:, par * 512:par * 512 + n], psrc, AF.Square, r=[keyp], w=[("SQ", par)])

        def group_rms_main(psrc, G, d, gain_off, WKf, SQ, QBb, keyp, par=0, mid=None):
            n = G * d
            o = par * 512
            c0 = 16 + 2 * par
            sm = small[:, c0:c0 + G]
            P.op("dve", lambda e: e.reduce_sum(sm, SQ[:, o:o + n].rearrange("p (g d) -> p g d", g=G), AX.X),
                 r=[("SQ", par)], w=[("ssg", par)])
            tsc(sm, sm, 1.0 / d, EPS, ALU.mult, ALU.add, r=[("ssg", par)], w=[("ssg", par)])
            act(sm, sm, AF.Sqrt, r=[("ssg", par)], w=[("ssg", par)])
            recip(sm, sm, r=[("ssg", par)], w=[("ssg", par)])
            tt(WKf[:, o:o + n].rearrange("p (g d) -> p g d", g=G), psrc.rearrange("p (g d) -> p g d", g=G),
               sm.to_broadcast([128, G, d]), ALU.mult, r=[keyp, ("ssg", par)], w=[("WKf", par)])
            if mid is not None:
                mid()
            tt(QBb[:, o:o + n].rearrange("p (g d) -> p g d", g=G), WKf[:, o:o + n].rearrange("p (g d) -> p g d", g=G),
               mid_bcast(rep[:, gain_off:gain_off + d], G), ALU.mult, r=[("WKf", par), "rep"], w=[("QBb", par)])

        def group_rms(psrc, G, d, gain_off, WKf, SQ, QBb, keyp):
            group_rms_pre(psrc, G, d, SQ, keyp, 0)
            group_rms_main(psrc, G, d, gain_off, WKf, SQ, QBb, keyp, 0)

        def mem_phase():
            BT = carve_bf(BT_OFF, [128, 8, S])
            MQT = carve_bf(RO_, [128, 4, S])
            MKT = carve_bf(RO_ + 16384, [128, 8, 256])
            MV = carve_bf(RO_ + 20480, [128, 2, 1024])
            MNT = carve_bf(RO_ + 24576, [128, 8, 256])
            PTm = [[carve_bf(RO_ + 28672 + m * 1024, [128, 512]) for m in range(2)] for b in range(2)]
            WKf = carve_f32(RO_ + 30720, [128, 1024])
            SQ = carve_f32(RO_ + 34816, [128, 1024])
            QBb = carve_bf(RO_ + 38912, [128, 1024])
            assert RO_ + 40960 <= ARENA_BYTES
            RD = SQ[:, 0:512]
            for mt in range(2):
                P.dma("sp", "mem", lambda e, mt=mt: e.dma_start(out=WKf, in_=mem_d[mt * 128:(mt + 1) * 128, :]), r=(), w=[("WKf", 0), ("WKf", 1)])
                act(SQ, WKf, AF.Square, r=[("WKf", 0), ("WKf", 1)], w=[("SQ", 0), ("SQ", 1), "mss"], accum_out=small[:, 24:25])
                tsc(small[:, 24:25], small[:, 24:25], 1.0 / D, EPS, ALU.mult, ALU.add, r=["mss"], w=["mss"])
                act(small[:, 24:25], small[:, 24:25], AF.Sqrt, r=["mss"], w=["mss"])
                recip(small[:, 24:25], small[:, 24:25], r=["mss"], w=["mss"])
                act(QBb, WKf, AF.Copy, r=[("WKf", 0), ("WKf", 1), "mss"], w=[("QBb", 0), ("QBb", 1)], scale=small[:, 24:25])
                for k in range(8):
                    tr(psb[0][:, k * 128:(k + 1) * 128], QBb[:, k * 128:(k + 1) * 128], r=[("QBb", 0), ("QBb", 1)], w=[("ps", 0)])
                tt(MNT[:, :, mt * 128:(mt + 1) * 128], psb[0][:, 0:1024].rearrange("p (k c) -> p k c", k=8),
                   cols[:, COL_MEMN:COL_MEMN + 8].to_broadcast([128, 8, 128]), ALU.mult, r=[("ps", 0), "cols"], w=["MNT"])
            for g in range(4):
                w, kw = load_slot([(dr["mem_kv"][g], 4096)])
                wv = w[:, :].rearrange("p (k n) -> p k n", k=8)
                for mt in range(2):
                    for k in range(8):
                        mm(ps[1][:, :], MNT[:, k, mt * 128:(mt + 1) * 128], wv[:, k, :], k == 0, k == 7, r=["MNT", kw], w=[("ps", 1)])
                    if g < 2:
                        group_rms(ps[1][:, :], 2, 256, REP_MKN, WKf, SQ, QBb, ("ps", 1))
                        for j in range(4):
                            tr(psb[2][:, j * 128:(j + 1) * 128], QBb[:, j * 128:(j + 1) * 128], r=[("QBb", 0)], w=[("ps", 2)])
                        cp(MKT[:, g * 4:(g + 1) * 4, mt * 128:(mt + 1) * 128], psb[2][:, 0:512].rearrange("p (j c) -> p j c", j=4),
                           r=[("ps", 2)], w=["MKT"], eng="act")
                    else:
                        act(MV[:, mt, (g - 2) * 512:(g - 1) * 512], ps[1][:, :], AF.Copy, r=[("ps", 1)], w=["MV"])
            pcnt = 0
            for half in range(2):
                w, kw = load_slot([(dr["mem_q"][half], 4096)])
                wv = w[:, :].rearrange("p (k n) -> p k n", k=8)
                def mproj(t):
                    ts_ = slice(t * 128, (t + 1) * 128)
                    bq = t % 2
                    for k in range(8):
                        mm(ps[bq][:, :], HT[:, k, ts_], wv[:, k, :], k == 0, k == 7, r=[("HT", t // 4), kw], w=[("ps", bq)])

                mproj(0)
                group_rms_pre(ps[0][:, :], 2, 256, SQ, ("ps", 0), 0)
                for t in range(NT):
                    ts_ = slice(t * 128, (t + 1) * 128)
                    bq = t % 2
                    par = t % 2
                    btr = 2 + t % 2
                    if t + 1 < NT:
                        mproj(t + 1)

                    def pre_next(t=t):
                        if t + 1 < NT:
                            group_rms_pre(ps[(t + 1) % 2][:, :], 2, 256, SQ, ("ps", (t + 1) % 2), (t + 1) % 2)

                    group_rms_main(ps[bq][:, :], 2, 256, REP_MQN, WKf, SQ, QBb, ("ps", bq), par, mid=pre_next)
                    for j in range(4):
                        tr(psb[btr][:, j * 128:(j + 1) * 128], QBb[:, par * 512 + j * 128:par * 512 + (j + 1) * 128],
                           r=[("QBb", par)], w=[("ps", btr)])
                    cp(MQT[:, :, ts_], psb[btr][:, 0:512].rearrange("p (j c) -> p j c", j=4), r=[("ps", btr)], w=[("MQT", t // 4)], eng="act")
                for hh in range(2):
                    h = half * 2 + hh
                    for qb in range(NB):
                        qs = slice(qb * 512, (qb + 1) * 512)
                        b = pcnt % 2
                        pcnt += 1
                        for mt in range(2):
                            for c in range(2):
                                mm(ps[3 + mt][:, :], MKT[:, h * 2 + c, mt * 128:(mt + 1) * 128], MQT[:, hh * 2 + c, qs], c == 0, c == 1,
                                   r=["MKT", ("MQT", qb)], w=[("ps", 3 + mt)])
                            act(PTm[b][mt], ps[3 + mt][:, :], AF.Exp, r=[("ps", 3 + mt)], w=[("PTm", 0, mt)], scale=1.0 / 16.0)
                        for c in range(2):
                            for mt in range(2):
                                mm(ps[5 + c][:, :], MV[:, mt, h * 256 + c * 128:h * 256 + (c + 1) * 128], PTm[b][mt], mt == 0, mt == 1,
                                   r=["MV", ("PTm", 0, mt)], w=[("ps", 5 + c)])
                        for mt in range(2):
                            mm(ps[7][:, :], ones[:, :], PTm[b][mt], mt == 0, mt == 1, r=["ones", ("PTm", 0, mt)], w=[("ps", 7)])
                        act(RD, ps[7][:, :], AF.Ln, r=[("ps", 7)], w=[("SQ", 0)])
                        act(RD, RD, AF.Exp, r=[("SQ", 0)], w=[("SQ", 0)], scale=-1.0)
                        for c in range(2):
                            tt(BT[:, h * 2 + c, qs], ps[5 + c][:, :], RD, ALU.mult, r=[("ps", 5 + c), ("SQ", 0)], w=[("BT", qb)])

        def mixer():
            junk = carve_bf(0, [128, 1024])
            xn = [carve_bf(2048, [128, 1024]), carve_bf(4096, [128, 1024])]
            do_ret = stage in (2, 21) or stage >= 3
            do_diff = stage in (2, 22) or stage >= 3
            do_mem = stage in (2, 23) or stage >= 3
            if stage >= 20:
                do_ret, do_diff, do_mem = stage == 21, stage == 22, stage == 23
            if do_ret:
                ret_tables()
            norm_to_HT(COL_MIX, junk, xn)
            P.barrier()
            if do_ret:
                for pair in range(2):
                    for hh in range(2):
                        ret_head(pair * 2 + hh, hh * 4)
                    P.barrier()
                    apply_branch(pair, 0)
                    P.barrier()
            if do_diff:
                lam_compute()
                diff_tables()
                P.barrier()
                for gi in range(4):
                    diff_group(gi)
                P.barrier()
                apply_branch(2, 1)
                P.barrier()
            if do_mem:
                mem_phase()
                P.barrier()
                apply_branch(3, 2)
                P.barrier()

        def final_out(do_norm=True):
            junk = carve_bf(0, [128, 1024])
            OB = [carve_f32(4096 + i * 4096, [128, 1024]) for i in range(2)]
            FIN = carve_f32(12288, [128, 1024])
            if do_norm:
                P.dma("sp", "c3", lambda e: e.dma_start(out=FIN, in_=dr["fin"]), r=(), w=["FIN"])
                for t in range(NT):
                    act(junk, X[:, t, :], AF.Square, r=[("X", t)], w=["junk", ("ss", t)], accum_out=ss16[:, t:t + 1])
                tsc(rs16[:], ss16[:], 1.0 / D, EPS, ALU.mult, ALU.add, r=[("ss", t) for t in range(NT)], w=["rs16"])
                act(rs16[:], rs16[:], AF.Sqrt, r=["rs16"], w=["rs16"])
                recip(rs16[:], rs16[:], r=["rs16"], w=["rs16"])
            for t in range(NT):
                ob = OB[t % 2]
                if do_norm:
                    stt(ob, X[:, t, :], rs16[:, t:t + 1], FIN, ALU.mult, ALU.mult,
                        r=[("X", t), "rs16", "FIN"], w=[("OB", t % 2)])
                else:
                    cp(ob, X[:, t, :], r=[("X", t)], w=[("OB", t % 2)])
                P.dma("sp", f"y{t % 2}", lambda e, ob=ob, t=t: e.dma_start(out=y_d[t * 128:(t + 1) * 128, :], in_=ob),
                      r=[("OB", t % 2)], w=[("Y", t)])

        ffn("ffn1", COL_FFN1)
        P.barrier()
        if stage >= 2:
            mixer()
            P.barrier()
        if stage >= 3 and stage < 20:
            ffn("ffn2", COL_FFN2)
            P.barrier()
        final_out(do_norm=(stage >= 3 and stage < 20))
        P.barrier()
        print('PROG check:', P.check())
        P.emit(nc, es)
    return nc


_CACHE = {}


def kernel(**inputs):
    inp = {k: np.asarray(v) for k, v in inputs.items()}
    B = inp["x"].shape[0]
    W = _prep_weights(inp)
    C = _prep_consts()
    shared = dict(W)
    shared["ident"] = C["ident"]
    shared["dmask"] = C["dmask"].reshape(128, RET_H * 128)
    shared["rdec"] = C["rdec"]
    shared["ret_inv"] = C["ret_inv"]
    shared["rope_inv"] = C["rope_inv"]
    wshapes = {k: v.shape for k, v in shared.items()}
    key = ("nc", STAGE)
    if key not in _CACHE:
        _CACHE[key] = build_program(wshapes, STAGE)
    nc = _CACHE[key]
    in_maps = []
    for b in range(B):
        m = dict(shared)
        m["x"] = np.ascontiguousarray(inp["x"][b])
        m["mem"] = np.ascontiguousarray(inp["mem"][b])
        pos = inp["positions"][b].astype(np.int32)
        m["pos_rep"] = np.ascontiguousarray(np.broadcast_to(pos[None, :], (128, S)))
        m["pos_tok"] = np.ascontiguousarray(pos.reshape(NT, 128).T)
        in_maps.append(m)
    res = run_bass_kernel_spmd(nc, in_maps, core_ids=list(range(B)))
    out = np.stack([np.asarray(r["y"]) for r in res.results], axis=0)
    return out.astype(np.float32, copy=False)
```

```python
import os
import math
import numpy as np
from contextlib import ExitStack
import concourse.bass as bass
import concourse.mybir as mybir
from concourse.bass_utils import run_bass_kernel_spmd

F32 = mybir.dt.float32
BF16 = mybir.dt.bfloat16
I32 = mybir.dt.int32
AF = mybir.ActivationFunctionType
ALU = mybir.AluOpType
AX = mybir.AxisListType

S = 2048
D = 1024
NT = 16
NB = 4
DFF = 2816
NG = 11
EPS = 1e-6
SLOT = 4096
NSLOT = 4
TWO_PI = 2.0 * math.pi

STAGE = int(os.environ.get("MK_STAGE", "3"))


class Prog:
    ENG = ("pe", "act", "dve", "pool", "sp")

    def __init__(self):
        self.streams = {e: [] for e in self.ENG}
        self.nops = {e: 0 for e in self.ENG}
        self.known = {e: {} for e in self.ENG}
        self.res = {}
        self.sig = {e: set() for e in self.ENG}
        self.dcount = {}

    def _deps(self, eng, reads, writes):
        need = {}

        def add(c):
            sk, idx, clock = c
            if sk == "pe" and eng == "pe":
                return
            if self.known[eng].get(sk, 0) >= idx:
                return
            cur = need.get(sk)
            if cur is None or cur[0] < idx:
                need[sk] = (idx, clock)

        for k in reads:
            st = self.res.get(k)
            if st is not None and st[0] is not None:
                add(st[0])
        for k in writes:
            st = self.res.get(k)
            if st is not None:
                if st[0] is not None:
                    add(st[0])
                for sk, (idx, clock) in st[1].items():
                    add((sk, idx, clock))
        kn = self.known[eng]
        for sk, (idx, clock) in need.items():
            if kn.get(sk, 0) >= idx:
                continue
            self.streams[eng].append(("w", sk, idx))
            if sk in self.sig:
                self.sig[sk].add(idx)
            for a, b in clock.items():
                if kn.get(a, 0) < b:
                    kn[a] = b
            kn[sk] = max(kn.get(sk, 0), idx)

    def _record(self, comp, reads, writes):
        sk, idx, clock = comp
        for k in writes:
            self.res[k] = [comp, {}]
        for k in reads:
            st = self.res.get(k)
            if st is None:
                st = [None, {}]
                self.res[k] = st
            cur = st[1].get(sk)
            if cur is None or cur[0] < idx:
                st[1][sk] = (idx, clock)

    def op(self, eng, fn, r=(), w=()):
        self._deps(eng, r, w)
        self.nops[eng] += 1
        idx = self.nops[eng]
        self.streams[eng].append(("o", fn, idx))
        clock = dict(self.known[eng])
        clock[eng] = idx
        self._record((eng, idx, clock), r, w)

    def dma(self, issuer, sem, fn, r=(), w=()):
        self._deps(issuer, r, w)
        sk = ("d", sem)
        self.dcount[sk] = self.dcount.get(sk, 0) + 1
        idx = self.dcount[sk]
        self.streams[issuer].append(("d", fn, sk))
        clock = dict(self.known[issuer])
        clock[sk] = idx
        self._record((sk, idx, clock), r, w)

    def barrier(self):
        pend = {}
        for st in self.res.values():
            if st[0] is not None:
                sk, idx, _ = st[0]
                pend[sk] = max(pend.get(sk, 0), idx)
            for sk, (idx, _) in st[1].items():
                pend[sk] = max(pend.get(sk, 0), idx)
        for e in self.ENG:
            kn = self.known[e]
            for sk, idx in pend.items():
                if kn.get(sk, 0) >= idx:
                    continue
                if sk == e and e == "pe":
                    pass
                self.streams[e].append(("w", sk, idx))
                if sk in self.sig:
                    self.sig[sk].add(idx)
                kn[sk] = idx
        self.res = {}

    def check(self):
        sigval = {}
        for e in self.ENG:
            m = {}
            c = 0
            for i in range(1, self.nops[e] + 1):
                if i in self.sig[e]:
                    c += 1
                    m[i] = c
            sigval[e] = m
        semv = {}
        pc = {e: 0 for e in self.ENG}
        progress = True
        while progress:
            progress = False
            for e in self.ENG:
                st = self.streams[e]
                while pc[e] < len(st):
                    ent = st[pc[e]]
                    if ent[0] == "w":
                        sk, idx = ent[1], ent[2]
                        need = 16 * idx if isinstance(sk, tuple) else sigval[sk][idx]
                        if semv.get(sk, 0) < need:
                            break
                    elif ent[0] == "o":
                        if ent[2] in self.sig[e]:
                            semv[e] = semv.get(e, 0) + 1
                    else:
                        semv[ent[2]] = semv.get(ent[2], 0) + 16
                    pc[e] += 1
                    progress = True
        stuck = {e: (pc[e], len(self.streams[e])) for e in self.ENG if pc[e] < len(self.streams[e])}
        if stuck:
            for e, (p, n) in stuck.items():
                print("STUCK", e, p, n, self.streams[e][p][:3], semv)
            raise RuntimeError("semaphore program deadlocks: %r" % (stuck,))
        return {e: len(self.streams[e]) for e in self.ENG}, {k: v for k, v in semv.items()}

    def emit(self, nc, es):
        sems = {e: es.enter_context(nc.semaphore("s_" + e)) for e in self.ENG}
        dsems = {sk: es.enter_context(nc.semaphore("d_" + sk[1])) for sk in self.dcount}
        sigval = {}
        for e in self.ENG:
            m = {}
            c = 0
            ss = self.sig[e]
            for i in range(1, self.nops[e] + 1):
                if i in ss:
                    c += 1
                    m[i] = c
            sigval[e] = m
        engobj = {"pe": "tensor", "act": "scalar", "dve": "vector", "pool": "gpsimd", "sp": "sync"}
        block = es.enter_context(nc.Block())
        for e in self.ENG:
            stream = self.streams[e]
            if not stream:
                continue

            def body(eng, stream=stream, e=e):
                for ent in stream:
                    if ent[0] == "w":
                        sk, idx = ent[1], ent[2]
                        if isinstance(sk, tuple):
                            eng.wait_ge(dsems[sk], 16 * idx)
                        else:
                            eng.wait_ge(sems[sk], sigval[sk][idx])
                    elif ent[0] == "o":
                        ins = ent[1](eng)
                        if ent[2] in self.sig[e]:
                            ins.then_inc(sems[e], 1)
                    else:
                        ins = ent[1](eng)
                        ins.then_inc(dsems[ent[2]], 16)

            getattr(block, engobj[e])(body)


def _kmaj(w):
    K, N = w.shape
    return np.ascontiguousarray(w.reshape(K // 128, 128, N).transpose(1, 0, 2))


def _cols(v):
    return np.ascontiguousarray(v.reshape(-1, 128).T)


RET_H = 4
RET_DK = 256
RET_DV = 512
OFF_RQ, OFF_RK, OFF_RV, OFF_RG = 0, 1024, 2048, 4096
OFF_DQ, OFF_DK, OFF_DV, OFF_MQ, OFF_GATE = 6144, 7168, 8192, 9216, 10240


def _prep_weights(inp):
    out = {}
    for tag in ("ffn1", "ffn2"):
        wg = inp[tag + "_w_gate"][0]
        wu = inp[tag + "_w_up"][0]
        wd = inp[tag + "_w_down"][0]
        gu = np.empty((NG, 128, 2, 8, 256), np.float32)
        dn = np.empty((NG, 128, 2, 1024), np.float32)
        for g in range(NG):
            gu[g, :, 0] = _kmaj(wg[:, g * 256:(g + 1) * 256])
            gu[g, :, 1] = _kmaj(wu[:, g * 256:(g + 1) * 256])
            dn[g] = _kmaj(wd[g * 256:(g + 1) * 256, :])
        out[tag + "_gu"] = gu.reshape(NG, 128, 4096)
        out[tag + "_dn"] = dn.reshape(NG, 128, 2048)
        g4 = np.empty((5, 128, 8, 512), np.float32)
        u4 = np.empty((5, 128, 8, 512), np.float32)
        d4 = np.empty((5, 128, 4, 1024), np.float32)
        for g in range(5):
            g4[g] = _kmaj(wg[:, g * 512:(g + 1) * 512])
            u4[g] = _kmaj(wu[:, g * 512:(g + 1) * 512])
            d4[g] = _kmaj(wd[g * 512:(g + 1) * 512, :])
        out[tag + "_g4"] = g4.reshape(5, 128, 4096)
        out[tag + "_u4"] = u4.reshape(5, 128, 4096)
        out[tag + "_d4"] = d4.reshape(5, 128, 4096)
        out[tag + "_gu"] = np.ascontiguousarray(out[tag + "_gu"][10:11])
        out[tag + "_dn"] = np.ascontiguousarray(out[tag + "_dn"][10:11])
    w_in = inp["w_in"][0]
    rqk = np.empty((RET_H, 128, 8, 4, 128), np.float32)
    rv = np.empty((RET_H, 128, 8, 512), np.float32)
    rg = np.empty((RET_H, 128, 8, 512), np.float32)
    for h in range(RET_H):
        q = w_in[:, OFF_RQ + h * 256: OFF_RQ + (h + 1) * 256]
        k = w_in[:, OFF_RK + h * 256: OFF_RK + (h + 1) * 256]
        rqk[h, :, :, 0] = _kmaj(q[:, 0::2])
        rqk[h, :, :, 1] = _kmaj(q[:, 1::2])
        rqk[h, :, :, 2] = _kmaj(k[:, 0::2])
        rqk[h, :, :, 3] = _kmaj(k[:, 1::2])
        rv[h] = _kmaj(w_in[:, OFF_RV + h * 512: OFF_RV + (h + 1) * 512])
        rg[h] = _kmaj(w_in[:, OFF_RG + h * 512: OFF_RG + (h + 1) * 512])
    out["ret_qk"] = rqk.reshape(RET_H, 128, 4096)
    out["ret_v"] = rv.reshape(RET_H, 128, 4096)
    out["ret_g"] = rg.reshape(RET_H, 128, 4096)
    dqk = np.empty((4, 128, 8, 512), np.float32)
    dv = np.empty((4, 128, 8, 256), np.float32)
    for g in range(4):
        dqk[g, :, :, 0:256] = _kmaj(w_in[:, OFF_DQ + g * 256: OFF_DQ + (g + 1) * 256])
        dqk[g, :, :, 256:512] = _kmaj(w_in[:, OFF_DK + g * 256: OFF_DK + (g + 1) * 256])
        dv[g] = _kmaj(w_in[:, OFF_DV + g * 256: OFF_DV + (g + 1) * 256])
    out["diff_qk"] = dqk.reshape(4, 128, 4096)
    out["diff_v"] = dv.reshape(4, 128, 2048)
    mq = np.empty((2, 128, 8, 512), np.float32)
    for g in range(2):
        mq[g] = _kmaj(w_in[:, OFF_MQ + g * 512: OFF_MQ + (g + 1) * 512])
    out["mem_q"] = mq.reshape(2, 128, 4096)
    mkv = inp["mem_w_kv"][0]
    kv = np.empty((4, 128, 8, 512), np.float32)
    for g in range(4):
        kv[g] = _kmaj(mkv[:, g * 512:(g + 1) * 512])
    out["mem_kv"] = kv.reshape(4, 128, 4096)
    wo_list = [inp["ret_w_o"][0][0:1024], inp["ret_w_o"][0][1024:2048], inp["diff_w_o"][0], inp["mem_w_o"][0]]
    gidx = [0, 0, 1, 2]
    ap_w = np.empty((4, 8, 128, 2, 8, 128), np.float32)
    for a in range(4):
        for dc in range(8):
            ap_w[a, dc, :, 0] = _kmaj(wo_list[a][:, dc * 128:(dc + 1) * 128])
            gc = OFF_GATE + gidx[a] * 1024 + dc * 128
            ap_w[a, dc, :, 1] = _kmaj(w_in[:, gc: gc + 128])
    out["app_w"] = ap_w.reshape(4, 8, 128, 2048)
    wout = inp["w_out"][0]
    wo2 = np.empty((2, 2, 128, 4, 512), np.float32)
    for half in range(2):
        for nh in range(2):
            wo2[half, nh] = _kmaj(wout[half * 512:(half + 1) * 512, nh * 512:(nh + 1) * 512])
    out["w_out"] = wo2.reshape(2, 2, 128, 2048)
    cols = [
        _cols(inp["ffn1_norm"][0]), _cols(inp["mix_norm"][0]), _cols(inp["ffn2_norm"][0]),
        _cols(inp["mem_norm"][0]), _cols(inp["b_gate"][0]),
        _cols(inp["diff_subln"][0]),
    ]
    out["cols"] = np.ascontiguousarray(np.concatenate(cols, axis=1))
    out["fin"] = np.ascontiguousarray(np.broadcast_to(inp["final_norm"][0][None, :], (128, 1024)))
    rep = np.concatenate([
        inp["diff_q_norm"][0], inp["diff_k_norm"][0],
        inp["mem_q_norm"][0], inp["mem_k_norm"][0],
        inp["diff_lambda_q1"][0], inp["diff_lambda_k1"][0], inp["diff_lambda_q2"][0], inp["diff_lambda_k2"][0],
    ])
    out["rep"] = np.ascontiguousarray(np.broadcast_to(rep[None, :], (128, rep.shape[0])))
    return out


COL_FFN1, COL_MIX, COL_FFN2, COL_MEMN, COL_BG, COL_SUBLN = 0, 8, 16, 24, 32, 56
NCOLS = 57
REP_DQN, REP_DKN, REP_MQN, REP_MKN, REP_LAM = 0, 64, 128, 384, 640
NREP = 640 + 256


def _prep_consts():
    c = {}
    c["ident"] = np.eye(128, dtype=np.float32)
    gam = [1.0 - 2.0 ** (-5.0 - h) for h in range(RET_H)]
    i = np.arange(128, dtype=np.float64)
    dm = np.empty((RET_H, 128, 128), np.float64)
    cc = np.empty((128, 3 * RET_H), np.float64)
    for h, g in enumerate(gam):
        lg = math.log(g)
        cI = i[None, :]
        eI = i[:, None]
        mask = (np.floor(eI / 64) <= np.floor(cI / 64))
        dm[h] = np.exp(lg * (np.abs(cI - eI) - (cI + 1.0))) * mask
        cc[:, h] = np.exp(lg * (i + 1.0))
        cc[:, RET_H + h] = np.exp(2 * lg * (i + 1.0)) / 512.0
        cc[:, 2 * RET_H + h] = np.exp(lg * (127.0 - i))
    c["dmask"] = np.ascontiguousarray(dm.transpose(1, 0, 2)).astype(np.float32)
    c["rdec"] = cc.astype(np.float32)
    ret_inv = (1.0 / (np.float32(10000.0) ** np.linspace(0.0, 1.0, 128, dtype=np.float32))).astype(np.float32)
    rope_inv = (1.0 / (np.float32(500000.0) ** (np.arange(0, 16, 2, dtype=np.float32) / np.float32(16)))).astype(np.float32)
    c["ret_inv"] = ret_inv.reshape(128, 1)
    c["rope_inv"] = np.ascontiguousarray(np.broadcast_to(rope_inv[None, :], (128, 8))).astype(np.float32)
    c["gamma128"] = [g ** 128 for g in gam]
    return c


def build_program(wshapes, stage):
    nc = bass.Bass("TRN2", target_bir_lowering=False)
    P = Prog()
    dr = {}

    def din(name, shape, dt=F32):
        dr[name] = nc.dram_tensor(name, list(shape), dt, kind="ExternalInput").ap()
        return dr[name]

    x_d = din("x", [S, D])
    mem_d = din("mem", [256, D])
    posr_d = din("pos_rep", [128, S], I32)
    post_d = din("pos_tok", [128, NT], I32)
    for k, shp in wshapes.items():
        din(k, shp)
    y_d = nc.dram_tensor("y", [S, D], F32, kind="ExternalOutput").ap()
    gamma128 = _prep_consts()["gamma128"]

    es = ExitStack()
    with es:
        def sb(name, shape, dt=F32):
            return es.enter_context(nc.sbuf_tensor("sb_" + name, list(shape), dt))

        X = sb("X", [128, NT, D])
        HT = sb("HT", [128, 8, S], BF16)
        slots = [sb(f"slot{i}", [128, SLOT], BF16) for i in range(NSLOT)]
        ident = sb("ident", [128, 128], BF16)
        ones = sb("ones", [128, 128], BF16)
        cols = sb("cols", [128, NCOLS])
        rep = sb("rep", [128, NREP])
        ss16 = sb("ss16", [128, NT])
        rs16 = sb("rs16", [128, NT])
        ARENA_BYTES = 72 * 1024 + 512
        arena = sb("arena", [128, ARENA_BYTES // 4])
        ps = [es.enter_context(nc.psum_tensor(f"ps{i}", [128, 512], F32)) for i in range(8)]
        psb = [p.bitcast(BF16) for p in ps]

        arena_b = arena.bitcast(BF16)

        def carve_f32(off_bytes, shape):
            n = int(np.prod(shape[1:]))
            o = off_bytes // 4
            ap = arena[:, o:o + n]
            if len(shape) == 3:
                ap = ap.rearrange("p (a b) -> p a b", a=shape[1])
            elif len(shape) == 4:
                ap = ap.rearrange("p (a b c) -> p a b c", a=shape[1], b=shape[2])
            return ap

        def carve_bf(off_bytes, shape):
            n = int(np.prod(shape[1:]))
            o = off_bytes // 2
            ap = arena_b[:, o:o + n]
            if len(shape) == 3:
                ap = ap.rearrange("p (a b) -> p a b", a=shape[1])
            elif len(shape) == 4:
                ap = ap.rearrange("p (a b c) -> p a b c", a=shape[1], b=shape[2])
            return ap

        def mm(out, lhsT, rhs, start, stop, r, w):
            P.op("pe", lambda e: e.matmul(out, lhsT, rhs, start=start, stop=stop), r=r, w=w)

        def tr(out, in_, r, w):
            P.op("pe", lambda e: e.transpose(out, in_, ident[:]), r=list(r) + ["ident"], w=w)

        def act(out, in_, func, r, w, bias=0.0, scale=1.0, accum_out=None, eng="act"):
            if accum_out is not None:
                P.op("act", lambda e: e.activation(out, in_, func, bias=bias, scale=scale, accum_out=accum_out), r=r, w=w)
            else:
                P.op("act", lambda e: e.activation(out, in_, func, bias=bias, scale=scale), r=r, w=w)

        def tt(out, in0, in1, op, r, w, eng="dve"):
            P.op(eng, lambda e: e.tensor_tensor(out, in0, in1, op), r=r, w=w)

        def tsc(out, in0, s1, s2, op0, op1, r, w, eng="dve"):
            if op1 is None:
                P.op(eng, lambda e: e.tensor_scalar(out, in0, s1, None, op0), r=r, w=w)
            else:
                P.op(eng, lambda e: e.tensor_scalar(out, in0, s1, s2, op0, op1), r=r, w=w)

        def stt(out, in0, scalar, in1, op0, op1, r, w):
            P.op("dve", lambda e: e.scalar_tensor_tensor(out, in0, scalar, in1, op0, op1), r=r, w=w)

        def cp(out, in_, r, w, eng="dve"):
            if eng == "act":
                P.op("act", lambda e: e.activation(out, in_, AF.Copy), r=r, w=w)
            else:
                P.op(eng, lambda e: e.tensor_copy(out, in_), r=r, w=w)

        def recip(out, in_, r, w):
            P.op("dve", lambda e: e.reciprocal(out, in_), r=r, w=w)

        def memset(ap, val, w, eng="dve"):
            P.op(eng, lambda e: e.memset(ap, val), r=(), w=w)

        slot_ctr = [0]
        slots_static = list(slots)
        cur_slots = list(slots)

        def load_slot(parts):
            i = slot_ctr[0] % len(cur_slots)
            slot_ctr[0] += 1
            slots = list(cur_slots)
            off = 0
            for (ap, n) in parts:
                o = off
                P.dma("pool", f"slot{i}", lambda e, ap=ap, o=o, n=n, st=cur_slots[i]: e.dma_start(out=st[:, o:o + n], in_=ap, max_dma_last_dim=2048),
                      r=(), w=[("slot", i)])
                off += n
            return slots[i], ("slot", i)

        def const_load(dst, src, key, sem="c"):
            P.dma("sp", sem, lambda e: e.dma_start(out=dst, in_=src), r=(), w=[key])

        P.dma("pool", "c0", lambda e: e.dma_start(out=ident[:], in_=dr["ident"]), r=(), w=["ident"])
        const_load(cols[:], dr["cols"], "cols", "c1")
        const_load(rep[:], dr["rep"], "rep", "c2")
        for q in range(4):
            P.dma("sp", f"x{q}", lambda e, q=q: e.dma_start(
                out=X[:, q * 4:(q + 1) * 4, :],
                in_=x_d[q * 512:(q + 1) * 512, :].rearrange("(t p) d -> p t d", p=128)),
                r=(), w=[("X", t) for t in range(q * 4, q * 4 + 4)])
        memset(ones[:], 1.0, w=["ones"])

        def norm_to_HT(gcol, junk, xn):
            for t in range(NT):
                act(junk, X[:, t, :], AF.Square, r=[("X", t)], w=["junk", ("ss", t)], accum_out=ss16[:, t:t + 1])
            tsc(rs16[:], ss16[:], 1.0 / D, EPS, ALU.mult, ALU.add, r=[("ss", t) for t in range(NT)], w=["rs16"])
            act(rs16[:], rs16[:], AF.Sqrt, r=["rs16"], w=["rs16"])
            recip(rs16[:], rs16[:], r=["rs16"], w=["rs16"])
            for t in range(NT):
                xb = xn[t % 2]
                act(xb, X[:, t, :], AF.Copy, r=[("X", t), "rs16"], w=[("xn", t % 2)], scale=rs16[:, t:t + 1])
                bank = 6 + (t % 2)
                for k in range(8):
                    tr(psb[bank][:, k * 128:(k + 1) * 128], xb[:, k * 128:(k + 1) * 128],
                       r=[("xn", t % 2)], w=[("ps", bank)])
                tt(HT[:, :, t * 128:(t + 1) * 128],
                   psb[bank][:, 0:1024].rearrange("p (k c) -> p k c", k=8),
                   cols[:, gcol:gcol + 8].to_broadcast([128, 8, 128]), ALU.mult,
                   r=[("ps", bank), "cols"], w=[("HT", t // 4)])

        def ffn(tag, gcol):
            junk = carve_bf(0, [128, 1024])
            xn = [carve_bf(2048, [128, 1024]), carve_bf(4096, [128, 1024])]
            AT = [carve_bf(8192 + i * 16384, [128, 4, S]) for i in range(2)]
            SG = [carve_f32(40960 + i * 2048, [128, 512]) for i in range(2)]
            xs = [arena_b[:, (45056 + i * 8192) // 2:(45056 + i * 8192) // 2 + SLOT] for i in range(3)]
            assert 45056 + 3 * 8192 <= ARENA_BYTES
            cur_slots[:] = list(slots_static) + xs
            slot_ctr[0] = 0
            norm_to_HT(gcol, junk, xn)
            slot_ctr[0] = 0
            cnt = 0
            ocnt = 0
            for g in range(6):
                nch = 4 if g < 5 else 2
                if nch == 4:
                    wg_, kg_ = load_slot([(dr[tag + "_g4"][g], 4096)])
                    wu_, ku_ = load_slot([(dr[tag + "_u4"][g], 4096)])
                    wd_, kd_ = load_slot([(dr[tag + "_d4"][g], 4096)])
                    wg_v = wg_[:, 0:4096].rearrange("p (k c) -> p k c", k=8)
                    wu_v = wu_[:, 0:4096].rearrange("p (k c) -> p k c", k=8)
                    wd_v = wd_[:, 0:4096].rearrange("p (c n) -> p c n", c=4)
                else:
                    wgu, kg_ = load_slot([(dr[tag + "_gu"][0], 4096)])
                    ku_ = kg_
                    wd_, kd_ = load_slot([(dr[tag + "_dn"][0], 2048)])
                    wgu_v = wgu[:, 0:4096].rearrange("p (a k c) -> p a k c", a=2, k=8)
                    wg_v = wgu_v[:, 0, :, :]
                    wu_v = wgu_v[:, 1, :, :]
                    wd_v = wd_[:, 0:2048].rearrange("p (c n) -> p c n", c=2)
                at = AT[g % 2]
                for tb in range(NB):
                    for c in range(nch):
                        bg = (cnt % 2)
                        bu = 2 + (cnt % 2)
                        for k in range(8):
                            mm(ps[bg][:, :], wg_v[:, k, c * 128:(c + 1) * 128], HT[:, k, tb * 512:(tb + 1) * 512],
                               k == 0, k == 7, r=[kg_, ("HT", tb)], w=[("ps", bg)])
                        for k in range(8):
                            mm(ps[bu][:, :], wu_v[:, k, c * 128:(c + 1) * 128], HT[:, k, tb * 512:(tb + 1) * 512],
                               k == 0, k == 7, r=[ku_, ("HT", tb)], w=[("ps", bu)])
                        sg = SG[cnt % 2]
                        act(sg, ps[bg][:, :], AF.Silu, r=[("ps", bg)], w=[("SG", cnt % 2)])
                        tt(at[:, c, tb * 512:(tb + 1) * 512], sg, ps[bu][:, :], ALU.mult,
                           r=[("SG", cnt % 2), ("ps", bu)], w=[("AT", g % 2, tb)])
                        cnt += 1
                    for t4 in range(4):
                        t = tb * 4 + t4
                        for nh in range(2):
                            bo = 4 + (ocnt % 3)
                            ocnt += 1
                            for c in range(nch):
                                mm(ps[bo][:, :], at[:, c, t * 128:(t + 1) * 128], wd_v[:, c, nh * 512:(nh + 1) * 512],
                                   c == 0, c == nch - 1, r=[("AT", g % 2, tb), kd_], w=[("ps", bo)])
                            stt(X[:, t, nh * 512:(nh + 1) * 512], ps[bo][:, :], 0.5, X[:, t, nh * 512:(nh + 1) * 512],
                                ALU.mult, ALU.add, r=[("ps", bo), ("X", t)], w=[("X", t)])
            P.barrier()
            cur_slots[:] = list(slots_static)
            slot_ctr[0] = 0

        BT_OFF = 0
        RO_ = 32768

        def mid_bcast(ap2, reps):
            a = ap2.ap
            return bass.AP(ap2.tensor, ap2.offset, [list(a[0]), [0, reps], list(a[1])])

        def range_sin(dst, src, shift, T1, T2, T2i, keys_r, key_w):
            INV = 1.0 / TWO_PI
            C1 = 6.28125
            C2 = TWO_PI - C1
            tsc(T1, src, INV, shift * INV + 0.5, ALU.mult, ALU.add, r=keys_r, w=["rsT1"])
            cp(T2i, T1, r=["rsT1"], w=["rsT2"])
            cp(T1, T2i, r=["rsT2"], w=["rsT1"])
            stt(T2, T1, -C1, src, ALU.mult, ALU.add, r=["rsT1"] + keys_r, w=["rsT2"])
            stt(T2, T1, -C2, T2, ALU.mult, ALU.add, r=["rsT1", "rsT2"], w=["rsT2"])
            if shift != 0.0:
                tsc(T2, T2, shift, None, ALU.add, None, r=["rsT2"], w=["rsT2"])
            tsc(T1, T2, math.pi, -1e30, ALU.add, ALU.mult, r=["rsT2"], w=["rsT1"])
            tsc(T1, T1, 0.0, 1.0, ALU.max, ALU.min, r=["rsT1"], w=["rsT1"])
            stt(T2, T1, TWO_PI, T2, ALU.mult, ALU.add, r=["rsT1", "rsT2"], w=["rsT2"])
            tsc(T1, T2, -math.pi, 1e30, ALU.add, ALU.mult, r=["rsT2"], w=["rsT1"])
            tsc(T1, T1, 0.0, 1.0, ALU.max, ALU.min, r=["rsT1"], w=["rsT1"])
            stt(T2, T1, -TWO_PI, T2, ALU.mult, ALU.add, r=["rsT1", "rsT2"], w=["rsT2"])
            tsc(T2, T2, math.pi - 1e-6, -(math.pi - 1e-6), ALU.min, ALU.max, r=["rsT2"], w=["rsT2"])
            act(dst, T2, AF.Sin, r=["rsT2"], w=[key_w])

        small = sb("small", [128, 136])
        rdec = sb("rdec", [128, 12])
        retinv = sb("retinv", [128, 1])
        ropeinv = sb("ropeinv", [128, 8])
        dmask = sb("dmask", [128, RET_H, 128])
        P.dma("sp", "c5", lambda e: e.dma_start(out=retinv[:], in_=dr["ret_inv"]), r=(), w=["retinv"])
        P.dma("sp", "c6", lambda e: e.dma_start(out=rdec[:], in_=dr["rdec"]), r=(), w=["rdec"])
        P.dma("sp", "c7", lambda e: e.dma_start(out=dmask[:].rearrange("p h c -> p (h c)"), in_=dr["dmask"]), r=(), w=["dmask"])
        P.dma("sp", "c9", lambda e: e.dma_start(out=ropeinv[:], in_=dr["rope_inv"]), r=(), w=["ropeinv"])

        def apply_branch(a, gidx):
            MTh = carve_bf(RO_ + 16384, [128, 4, S])
            SGa = [carve_f32(RO_ + 32768 + i * 2048, [128, 512]) for i in range(2)]
            BT = carve_bf(BT_OFF, [128, 8, S])
            cnt = 0
            ocnt = 0
            for half in range(2):
                for dcl in range(4):
                    dc = half * 4 + dcl
                    w, kw = load_slot([(dr["app_w"][a, dc], 2048)])
                    wv = w[:, 0:2048].rearrange("p (a k c) -> p a k c", a=2, k=8)
                    for tb in range(NB):
                        bp = cnt % 2
                        bgt = 2 + cnt % 2
                        for k in range(8):
                            mm(ps[bp][:, :], wv[:, 0, k, :], BT[:, k, tb * 512:(tb + 1) * 512], k == 0, k == 7,
                               r=[kw, ("BT", tb)], w=[("ps", bp)])
                        for k in range(8):
                            mm(ps[bgt][:, :], wv[:, 1, k, :], HT[:, k, tb * 512:(tb + 1) * 512], k == 0, k == 7,
                               r=[kw, ("HT", tb)], w=[("ps", bgt)])
                        sg = SGa[cnt % 2]
                        bcol = COL_BG + gidx * 8 + dc
                        act(sg, ps[bgt][:, :], AF.Sigmoid, r=[("ps", bgt), "cols"], w=[("SGa", cnt % 2)],
                            bias=cols[:, bcol:bcol + 1])
                        tt(MTh[:, dcl, tb * 512:(tb + 1) * 512], sg, ps[bp][:, :], ALU.mult,
                           r=[("SGa", cnt % 2), ("ps", bp)], w=[("MT", tb)])
                        cnt += 1
                for nh in range(2):
                    w, kw = load_slot([(dr["w_out"][half, nh], 2048)])
                    wv = w[:, 0:2048].rearrange("p (k n) -> p k n", k=4)
                    for t in range(NT):
                        bo = 4 + ocnt % 2
                        ocnt += 1
                        for k in range(4):
                            mm(ps[bo][:, :], MTh[:, k, t * 128:(t + 1) * 128], wv[:, k, :], k == 0, k == 3,
                               r=[("MT", t // 4), kw], w=[("ps", bo)])
                        tt(X[:, t, nh * 512:(nh + 1) * 512], X[:, t, nh * 512:(nh + 1) * 512], ps[bo][:, :], ALU.add,
                           r=[("ps", bo), ("X", t)], w=[("X", t)])

        def ret_tables():
            TABC = carve_f32(RO_, [128, S])
            TABS = carve_f32(RO_ + 8192, [128, S])
            T1 = carve_f32(RO_ + 16384, [128, S])
            T2 = carve_f32(RO_ + 24576, [128, S])
            T2i = T2.bitcast(I32)
            ANG = carve_f32(RO_ + 32768, [128, S])
            ANGi = ANG.bitcast(I32)
            P.dma("sp", "c4", lambda e: e.dma_start(out=ANGi, in_=posr_d), r=(), w=["ANGi"])
            cp(T1, ANGi, r=["ANGi"], w=["posf"])
            tsc(ANG, T1, retinv[:, 0:1], None, ALU.mult, None, r=["posf", "retinv", "ANGi"], w=["ANG"])
            range_sin(TABS, ANG, 0.0, T1, T2, T2i, ["ANG"], "TAB")
            range_sin(TABC, ANG, math.pi / 2, T1, T2, T2i, ["ANG"], "TAB")

        def lam_compute():
            memset(small[:, 6:7], EPS, w=["epsc"])
            pr = small[:, 64:128]
            for i, (a_, b_) in enumerate(((0, 64), (128, 192))):
                tt(pr, rep[:, REP_LAM + a_:REP_LAM + a_ + 64], rep[:, REP_LAM + b_:REP_LAM + b_ + 64], ALU.mult,
                   r=["rep"], w=["lam_pr"])
                P.op("dve", lambda e, i=i: e.reduce_sum(small[:, 1 + i:2 + i], pr, AX.X), r=["lam_pr"], w=[("lam_s", i)])
            act(small[:, 1:3], small[:, 1:3], AF.Exp, r=[("lam_s", 0), ("lam_s", 1)], w=["lam_e"])
            tt(small[:, 3:4], small[:, 2:3], small[:, 1:2], ALU.subtract, r=["lam_e"], w=["lam_d"])
            tsc(small[:, 0:1], small[:, 3:4], -0.2, None, ALU.add, None, r=["lam_d"], w=["neglam"])

        def ret_head(h, bt_base):
            BT = carve_bf(BT_OFF, [128, 8, S])
            TABC = carve_f32(RO_, [128, S])
            TABS = carve_f32(RO_ + 8192, [128, S])
            o = RO_ + 16384
            QT = carve_bf(o, [128, 2, 512]); o += 2048
            KT = carve_bf(o, [128, 2, 512]); o += 2048
            KTOK = carve_bf(o, [128, 4, 256]); o += 2048
            VTOK = carve_bf(o, [128, 4, 512]); o += 4096
            SGt = carve_bf(o, [128, 4, 512]); o += 4096
            Sf = carve_f32(o, [128, 2, 512]); o += 4096
            Sbf = carve_bf(o, [128, 2, 512]); o += 2048
            SCT = carve_bf(o, [128, 128]); o += 256
            RO = [carve_bf(o, [128, 512]), carve_bf(o + 1024, [128, 512])]; o += 2048
            T1 = carve_f32(o, [128, 512]); o += 2048
            T2 = ps[7][:, :]
            assert o <= ARENA_BYTES, o
            wqk, kqk = load_slot([(dr["ret_qk"][h], 4096)])
            wv, kv_ = load_slot([(dr["ret_v"][h], 4096)])
            wg, kg = load_slot([(dr["ret_g"][h], 4096)])
            wqk_v = wqk[:, :].rearrange("p (k j c) -> p k j c", k=8, j=4)
            wv_v = wv[:, :].rearrange("p (k n) -> p k n", k=8)
            wg_v = wg[:, :].rearrange("p (k n) -> p k n", k=8)
            g128 = float(gamma128[h])
            pending = [None]

            def flush_ro(ri, t, tb):
                for fc in range(4):
                    tr(psb[7][:, fc * 128:(fc + 1) * 128], RO[ri][:, fc * 128:(fc + 1) * 128], r=[("RO", ri)], w=[("ps", 7)])
                cp(BT[:, bt_base:bt_base + 4, t * 128:(t + 1) * 128],
                   psb[7][:, 0:512].rearrange("p (f c) -> p f c", f=4), r=[("ps", 7)], w=[("BT", tb)], eng="act")
            for tb in range(NB):
                tbs = slice(tb * 512, (tb + 1) * 512)
                for (j0, b0) in ((0, 0), (2, 4)):
                    for jj in range(2):
                        for k in range(8):
                            mm(ps[b0 + jj][:, :], wqk_v[:, k, j0 + jj, :], HT[:, k, tbs], k == 0, k == 7,
                               r=[kqk, ("HT", tb)], w=[("ps", b0 + jj)])
                for ci in range(4):
                    t = tb * 4 + ci
                    ts_ = slice(t * 128, (t + 1) * 128)
                    bv = 2 if ci % 2 == 0 else 6
                    for k in range(8):
                        mm(ps[bv][:, :], HT[:, k, ts_], wv_v[:, k, :], k == 0, k == 7, r=[kv_, ("HT", tb)], w=[("ps", bv)])
                    act(VTOK[:, ci, :], ps[bv][:, :], AF.Copy, r=[("ps", bv)], w=[("VTOK", ci)])
                C = TABC[:, tbs]
                Sn = TABS[:, tbs]
                for (b0, dst, scale, key) in ((0, QT, 1.0, "QT"), (4, KT, 1.0 / 16.0, "KT")):
                    pe_, po_ = ps[b0], ps[b0 + 1]
                    stt(T1, pe_[:, :], scale, C, ALU.mult, ALU.mult, r=[("ps", b0), "TAB"], w=["T1"])
                    stt(T2, po_[:, :], scale, Sn, ALU.mult, ALU.mult, r=[("ps", b0 + 1), "TAB"], w=[("ps", 7)])
                    tt(dst[:, 0, :], T1, T2, ALU.subtract, r=["T1", ("ps", 7)], w=[key])
                    stt(T1, po_[:, :], scale, C, ALU.mult, ALU.mult, r=[("ps", b0 + 1), "TAB"], w=["T1"])
                    stt(T2, pe_[:, :], scale, Sn, ALU.mult, ALU.mult, r=[("ps", b0), "TAB"], w=[("ps", 7)])
                    tt(dst[:, 1, :], T1, T2, ALU.add, r=["T1", ("ps", 7)], w=[key])
                for ci in range(4):
                    t = tb * 4 + ci
                    ts_ = slice(t * 128, (t + 1) * 128)
                    bg_ = 3 if ci % 2 == 0 else 6
                    for k in range(8):
                        mm(ps[bg_][:, :], HT[:, k, ts_], wg_v[:, k, :], k == 0, k == 7, r=[kg, ("HT", tb)], w=[("ps", bg_)])
                    act(SGt[:, ci, :], ps[bg_][:, :], AF.Silu, r=[("ps", bg_)], w=[("SGt", ci)])
                for ci in range(4):
                    cs = slice(ci * 128, (ci + 1) * 128)
                    for c in range(2):
                        tr(psb[4][:, c * 128:(c + 1) * 128], KT[:, c, cs], r=["KT"], w=[("ps", 4)])
                    tsc(KTOK[:, ci, :], psb[4][:, 0:256], rdec[:, 8 + h:9 + h], None, ALU.mult, None,
                        r=[("ps", 4), "rdec"], w=[("KTOK", ci)])
                for ci in range(4):
                    n = tb * 4 + ci
                    t = n
                    cs = slice(ci * 128, (ci + 1) * 128)
                    ts_ = slice(t * 128, (t + 1) * 128)
                    rob = RO[n % 2]
                    for c in range(2):
                        mm(ps[5][:, 0:128], KT[:, c, cs], QT[:, c, cs], c == 0, c == 1, r=["KT", "QT"], w=[("ps", 5)])
                    tt(SCT, ps[5][:, 0:128], dmask[:, h, :], ALU.mult, r=[("ps", 5), "dmask"], w=["SCT"])
                    if n < 15:
                        for c in range(2):
                            mm(ps[c][:, :], KTOK[:, ci, c * 128:(c + 1) * 128], VTOK[:, ci, :], True, True,
                               r=[("KTOK", ci), ("VTOK", ci)], w=[("ps", c)])
                    mm(ps[6][:, :], SCT, VTOK[:, ci, :], True, n == 0, r=["SCT", ("VTOK", ci)], w=[("ps", 6)])
                    if n > 0:
                        for c in range(2):
                            mm(ps[6][:, :], QT[:, c, cs], Sbf[:, c, :], False, c == 1, r=["QT", ("Sbf", c)], w=[("ps", 6)])
                    if n < 15:
                        for c in range(2):
                            if n == 0:
                                cp(Sf[:, c, :], ps[c][:, :], r=[("ps", c)], w=[("Sf", c)])
                            else:
                                stt(Sf[:, c, :], Sf[:, c, :], g128, ps[c][:, :], ALU.mult, ALU.add,
                                    r=[("ps", c), ("Sf", c)], w=[("Sf", c)])
                            cp(Sbf[:, c, :], Sf[:, c, :], r=[("Sf", c)], w=[("Sbf", c)], eng="act")
                    act(T1, ps[6][:, :], AF.Square, r=[("ps", 6)], w=["T1", "rss"], accum_out=small[:, 4:5])
                    tt(small[:, 5:6], small[:, 4:5], rdec[:, 4 + h:5 + h], ALU.mult, r=["rss", "rdec"], w=["rf"])
                    tsc(small[:, 5:6], small[:, 5:6], EPS, None, ALU.add, None, r=["rf"], w=["rf"])
                    act(small[:, 5:6], small[:, 5:6], AF.Sqrt, r=["rf"], w=["rf"])
                    recip(small[:, 5:6], small[:, 5:6], r=["rf"], w=["rf"])
                    tt(small[:, 5:6], small[:, 5:6], rdec[:, h:h + 1], ALU.mult, r=["rf", "rdec"], w=["rf"])
                    stt(rob, ps[6][:, :], small[:, 5:6], SGt[:, ci, :], ALU.mult, ALU.mult,
                        r=[("ps", 6), "rf", ("SGt", ci)], w=[("RO", n % 2)])
                    if pending[0] is not None:
                        flush_ro(*pending[0])
                    pending[0] = (n % 2, t, tb)
            flush_ro(*pending[0])

        def diff_tables():
            DS8 = carve_f32(RO_ + 29696, [128, 16, 8])
            DC8 = carve_f32(RO_ + 30208, [128, 16, 8])
            DSS = carve_f32(RO_ + 28672, [128, 16, 16])
            DCC = carve_f32(RO_ + 33792, [128, 16, 16])
            A = carve_f32(RO_ + 34816, [128, 16, 8])
            T1 = carve_f32(RO_ + 34816 + 512, [128, 16, 8])
            T2 = carve_f32(RO_ + 34816 + 1024, [128, 16, 8])
            T2i = T2.bitcast(I32)
            PI_ = carve_f32(RO_ + 34816 + 1536, [128, 16]).bitcast(I32)
            PF = carve_f32(RO_ + 34816 + 1600, [128, 16])
            P.dma("sp", "c8", lambda e: e.dma_start(out=PI_, in_=post_d), r=(), w=["PI"])
            cp(PF, PI_, r=["PI"], w=["PF"])
            tt(A, PF.to_broadcast([128, 16, 8]),
               mid_bcast(ropeinv[:, 0:8], 16), ALU.mult, r=["PF", "ropeinv"], w=["DA"])
            range_sin(DS8, A, 0.0, T1, T2, T2i, ["DA"], "DT8")
            range_sin(DC8, A, math.pi / 2, T1, T2, T2i, ["DA"], "DT8")
            cp(DCC[:, :, 0:8], DC8, r=["DT8"], w=["DTAB"])
            cp(DCC[:, :, 8:16], DC8, r=["DT8"], w=["DTAB"])
            cp(DSS[:, :, 8:16], DS8, r=["DT8"], w=["DTAB"])
            tsc(DSS[:, :, 0:8], DS8, -1.0, None, ALU.mult, None, r=["DT8"], w=["DTAB"])

        def diff_group(gi):
            BT = carve_bf(BT_OFF, [128, 8, S])
            QTd = carve_bf(RO_, [128, 2, S])
            KTd = carve_bf(RO_ + 8192, [128, 2, S])
            VTd = carve_bf(RO_ + 16384, [128, 16, 256])
            PT = [[carve_bf(RO_ + 24576 + (b * 2 + r_) * 1024, [128, 512]) for r_ in range(2)] for b in range(2)]
            DSS = carve_f32(RO_ + 28672, [128, 16, 16])
            SQ = carve_f32(RO_ + 30720, [128, 512])
            QKb = carve_bf(RO_ + 32768, [128, 512])
            DCC = carve_f32(RO_ + 33792, [128, 16, 16])
            A0 = carve_f32(RO_ + 34816, [128, 512])
            A1 = carve_f32(RO_ + 36864, [128, 512])
            RD = carve_f32(RO_ + 38912, [128, 512])
            QKf = A1
            XR = RD[:, 0:128].rearrange("p (g d) -> p g d", g=8)
            RA = RD[:, 128:256].rearrange("p (g d) -> p g d", g=8)
            RB = RD[:, 256:384].rearrange("p (g d) -> p g d", g=8)
            assert RO_ + 40960 <= ARENA_BYTES
            SQb = SQ.bitcast(BF16)[:, 0:512]
            wqk, kqk = load_slot([(dr["diff_qk"][gi], 4096)])
            wv, kv_ = load_slot([(dr["diff_v"][gi], 2048)])
            wqk_v = wqk[:, :].rearrange("p (k n) -> p k n", k=8)
            wv_v = wv[:, 0:2048].rearrange("p (k n) -> p k n", k=8)
            QKf3 = QKf.rearrange("p (g d) -> p g d", g=8)
            QKb3 = QKb.rearrange("p (g d) -> p g d", g=8)
            gain_ap = bass.AP(rep[:, 0:1].tensor, REP_DQN, [[NREP, 128], [64, 2], [0, 4], [1, 64]])
            gain16_ap = bass.AP(rep[:, 0:1].tensor, REP_DQN, [[NREP, 128], [64, 2], [0, 4], [1, 16]])

            def dproj(t):
                ts_ = slice(t * 128, (t + 1) * 128)
                bq = t % 2
                bv = 2 + t % 2
                for k in range(8):
                    mm(ps[bq][:, :], HT[:, k, ts_], wqk_v[:, k, :], k == 0, k == 7, r=[kqk, ("HT", t // 4)], w=[("ps", bq)])
                for k in range(8):
                    mm(ps[bv][:, 0:256], HT[:, k, ts_], wv_v[:, k, :], k == 0, k == 7, r=[kv_, ("HT", t // 4)], w=[("ps", bv)])

            def actpre(t):
                bq = t % 2
                bv = 2 + t % 2
                act(VTd[:, t, :], ps[bv][:, 0:256], AF.Copy, r=[("ps", bv)], w=[("VTd", t)])
                act(SQ, ps[bq][:, :], AF.Square, r=[("ps", bq)], w=["SQ"])

            dproj(0)
            actpre(0)
            for t in range(NT):
                ts_ = slice(t * 128, (t + 1) * 128)
                bq = t % 2
                btr = 4 + t % 2
                if t + 1 < NT:
                    dproj(t + 1)
                P.op("dve", lambda e: e.reduce_sum(small[:, 8:16], SQ.rearrange("p (g d) -> p g d", g=8), AX.X),
                     r=["SQ"], w=["ss8"])
                tsc(small[:, 8:16], small[:, 8:16], 1.0 / 64.0, EPS, ALU.mult, ALU.add, r=["ss8"], w=["ss8"])
                act(small[:, 8:16], small[:, 8:16], AF.Sqrt, r=["ss8"], w=["ss8"])
                recip(small[:, 8:16], small[:, 8:16], r=["ss8"], w=["ss8"])
                tt(QKf3, ps[bq][:, :].rearrange("p (g d) -> p g d", g=8), small[:, 8:16].to_broadcast([128, 8, 64]), ALU.mult,
                   r=[("ps", bq), "ss8"], w=["A1"])
                if t + 1 < NT:
                    actpre(t + 1)
                tt(QKb.rearrange("p (a b d) -> p a b d", a=2, b=4), QKf.rearrange("p (a b d) -> p a b d", a=2, b=4), gain_ap,
                   ALU.mult, r=["A1", "rep"], w=["QKb"])
                tt(XR.rearrange("p (a b) d -> p a b d", a=2), QKf.rearrange("p (a b d) -> p a b d", a=2, b=4)[:, :, :, 0:16], gain16_ap,
                   ALU.mult, r=["A1", "rep"], w=["RD"])
                ccb = mid_bcast(DCC[:, t, :], 8)
                tt(RA, XR, ccb, ALU.mult, r=["RD", "DTAB"], w=["RD"])
                tt(RB[:, :, 0:8], XR[:, :, 8:16], mid_bcast(DSS[:, t, 0:8], 8), ALU.mult, r=["RD", "DTAB"], w=["RD"])
                tt(RB[:, :, 8:16], XR[:, :, 0:8], mid_bcast(DSS[:, t, 8:16], 8), ALU.mult, r=["RD", "DTAB"], w=["RD"])
                tt(QKb3[:, :, 0:16], RA, RB, ALU.add, r=["RD", "QKb"], w=["QKb"])
                for j in range(4):
                    tr(psb[btr][:, j * 128:(j + 1) * 128], QKb[:, j * 128:(j + 1) * 128], r=["QKb"], w=[("ps", btr)])
                cp(QTd[:, :, ts_], psb[btr][:, 0:256].rearrange("p (h c) -> p h c", h=2), r=[("ps", btr)], w=[("QTd", t // 4)], eng="act")
                cp(KTd[:, :, ts_], psb[btr][:, 256:512].rearrange("p (h c) -> p h c", h=2), r=[("ps", btr)], w=[("KTd", t // 4)], eng="act")
            steps = []
            for hh in range(2):
                for qb in range(NB):
                    nkt = 4 * (qb + 1)
                    for kt in range(nkt):
                        steps.append((hh, qb, kt, nkt))

            def emit_scores(i):
                hh, qb, kt, nkt = steps[i]
                c0 = max(0, kt - 4 * qb) * 128
                b = i % 2
                for r_ in range(2):
                    pr = slice(r_ * 64, (r_ + 1) * 64)
                    sbank = b * 2 + r_
                    mm(ps[sbank][:, c0:512], KTd[pr, hh, kt * 128:(kt + 1) * 128],
                       QTd[pr, hh, qb * 512 + c0:(qb + 1) * 512], True, True,
                       r=[("KTd", kt // 4), ("QTd", qb)], w=[("ps", sbank)])

            def emit_exp(i):
                hh, qb, kt, nkt = steps[i]
                c0 = max(0, kt - 4 * qb) * 128
                b = i % 2
                for r_ in range(2):
                    sbank = b * 2 + r_
                    pt = PT[b][r_]
                    act(pt[:, c0:512], ps[sbank][:, c0:512], AF.Exp, r=[("ps", sbank)], w=[("PT", b, r_)], scale=0.125)
                    if kt >= 4 * qb:
                        memset(pt[64:128, c0:c0 + 64], 0.0, w=[("PT", b, r_)], eng="dve")

            def emit_pv(i):
                hh, qb, kt, nkt = steps[i]
                c0 = max(0, kt - 4 * qb) * 128
                b = i % 2
                for r_ in range(2):
                    pt = PT[b][r_]
                    mm(ps[4 + r_][:, c0:512], VTd[:, kt, hh * 128:(hh + 1) * 128], pt[:, c0:512], kt == 0, kt == nkt - 1,
                       r=[("VTd", kt), ("PT", b, r_)], w=[("ps", 4 + r_)])
                    mm(ps[6 + r_][:, c0:512], ones[:, :], pt[:, c0:512], kt == 0, kt == nkt - 1,
                       r=["ones", ("PT", b, r_)], w=[("ps", 6 + r_)])

            def finalize(hh, qb):
                h = 2 * gi + hh
                qs = slice(qb * 512, (qb + 1) * 512)
                RDb = SQ
                SQq = QKb
                act(RD, ps[6][:, :], AF.Ln, r=[("ps", 6)], w=["RD"])
                act(RD, RD, AF.Exp, r=["RD"], w=["RD"], scale=-1.0)
                act(RDb, ps[7][:, :], AF.Ln, r=[("ps", 7)], w=["SQ"])
                act(RDb, RDb, AF.Exp, r=["SQ"], w=["SQ"], scale=-1.0)
                tt(A0, ps[4][:, :], RD, ALU.mult, r=[("ps", 4), "RD"], w=["A0"])
                tt(A1, ps[5][:, :], RDb, ALU.mult, r=[("ps", 5), "SQ"], w=["A1"])
                stt(A0, A1, small[:, 0:1], A0, ALU.mult, ALU.add, r=["A0", "A1", "neglam"], w=["A0"])
                act(SQq, A0, AF.Square, r=["A0"], w=["QKb"])
                mm(ps[6][:, :], ones[:, :], SQq, True, True, r=["ones", "QKb"], w=[("ps", 6)])
                act(RD, ps[6][:, :], AF.Ln, r=[("ps", 6), "epsc"], w=["RD"], scale=1.0 / 128.0, bias=small[:, 6:7])
                act(RD, RD, AF.Exp, r=["RD"], w=["RD"], scale=-0.5)
                tt(A0, A0, RD, ALU.mult, r=["A0", "RD"], w=["A0"])
                tsc(BT[:, h, qs], A0, cols[:, COL_SUBLN:COL_SUBLN + 1], 0.8, ALU.mult, ALU.mult,
                    r=["A0", "cols"], w=[("BT", qb)])

            emit_scores(0)
            for i in range(len(steps)):
                if i + 1 < len(steps):
                    emit_scores(i + 1)
                emit_exp(i)
                emit_pv(i)
                hh, qb, kt, nkt = steps[i]
                if kt == nkt - 1:
                    finalize(hh, qb)

        def group_rms_pre(psrc, G, d, SQ, keyp, par=0):
            n = G * d
            act(SQ[:, par * 512:par * 512 + n], psrc, AF.Square, r=[keyp], w=[("SQ", par)])

        def group_rms_main(psrc, G, d, gain_off, WKf, SQ, QBb, keyp, par=0, mid=None):
            n = G * d
            o = par * 512
            c0 = 16 + 2 * par
            sm = small[:, c0:c0 + G]
            P.op("dve", lambda e: e.reduce_sum(sm, SQ[:, o:o + n].rearrange("p (g d) -> p g d", g=G), AX.X),
                 r=[("SQ", par)], w=[("ssg", par)])
            tsc(sm, sm, 1.0 / d, EPS, ALU.mult, ALU.add, r=[("ssg", par)], w=[("ssg", par)])
            act(sm, sm, AF.Sqrt, r=[("ssg", par)], w=[("ssg", par)])
            recip(sm, sm, r=[("ssg", par)], w=[("ssg", par)])
            tt(WKf[:, o:o + n].rearrange("p (g d) -> p g d", g=G), psrc.rearrange("p (g d) -> p g d", g=G),
               sm.to_broadcast([128, G, d]), ALU.mult, r=[keyp, ("ssg", par)], w=[("WKf", par)])
            if mid is not None:
                mid()
            tt(QBb[:, o:o + n].rearrange("p (g d) -> p g d", g=G), WKf[:, o:o + n].rearrange("p (g d) -> p g d", g=G),
               mid_bcast(rep[:, gain_off:gain_off + d], G), ALU.mult, r=[("WKf", par), "rep"], w=[("QBb", par)])

        def group_rms(psrc, G, d, gain_off, WKf, SQ, QBb, keyp):
            group_rms_pre(psrc, G, d, SQ, keyp, 0)
            group_rms_main(psrc, G, d, gain_off, WKf, SQ, QBb, keyp, 0)

        def mem_phase():
            BT = carve_bf(BT_OFF, [128, 8, S])
            MQT = carve_bf(RO_, [128, 4, S])
            MKT = carve_bf(RO_ + 16384, [128, 8, 256])
            MV = carve_bf(RO_ + 20480, [128, 2, 1024])
            MNT = carve_bf(RO_ + 24576, [128, 8, 256])
            PTm = [[carve_bf(RO_ + 28672 + m * 1024, [128, 512]) for m in range(2)] for b in range(2)]
            WKf = carve_f32(RO_ + 30720, [128, 1024])
            SQ = carve_f32(RO_ + 34816, [128, 1024])
            QBb = carve_bf(RO_ + 38912, [128, 1024])
            assert RO_ + 40960 <= ARENA_BYTES
            RD = SQ[:, 0:512]
            for mt in range(2):
                P.dma("sp", "mem", lambda e, mt=mt: e.dma_start(out=WKf, in_=mem_d[mt * 128:(mt + 1) * 128, :]), r=(), w=[("WKf", 0), ("WKf", 1)])
                act(SQ, WKf, AF.Square, r=[("WKf", 0), ("WKf", 1)], w=[("SQ", 0), ("SQ", 1), "mss"], accum_out=small[:, 24:25])
                tsc(small[:, 24:25], small[:, 24:25], 1.0 / D, EPS, ALU.mult, ALU.add, r=["mss"], w=["mss"])
                act(small[:, 24:25], small[:, 24:25], AF.Sqrt, r=["mss"], w=["mss"])
                recip(small[:, 24:25], small[:, 24:25], r=["mss"], w=["mss"])
                act(QBb, WKf, AF.Copy, r=[("WKf", 0), ("WKf", 1), "mss"], w=[("QBb", 0), ("QBb", 1)], scale=small[:, 24:25])
                for k in range(8):
                    tr(psb[0][:, k * 128:(k + 1) * 128], QBb[:, k * 128:(k + 1) * 128], r=[("QBb", 0), ("QBb", 1)], w=[("ps", 0)])
                tt(MNT[:, :, mt * 128:(mt + 1) * 128], psb[0][:, 0:1024].rearrange("p (k c) -> p k c", k=8),
                   cols[:, COL_MEMN:COL_MEMN + 8].to_broadcast([128, 8, 128]), ALU.mult, r=[("ps", 0), "cols"], w=["MNT"])
            for g in range(4):
                w, kw = load_slot([(dr["mem_kv"][g], 4096)])
                wv = w[:, :].rearrange("p (k n) -> p k n", k=8)
                for mt in range(2):
                    for k in range(8):
                        mm(ps[1][:, :], MNT[:, k, mt * 128:(mt + 1) * 128], wv[:, k, :], k == 0, k == 7, r=["MNT", kw], w=[("ps", 1)])
                    if g < 2:
                        group_rms(ps[1][:, :], 2, 256, REP_MKN, WKf, SQ, QBb, ("ps", 1))
                        for j in range(4):
                            tr(psb[2][:, j * 128:(j + 1) * 128], QBb[:, j * 128:(j + 1) * 128], r=[("QBb", 0)], w=[("ps", 2)])
                        cp(MKT[:, g * 4:(g + 1) * 4, mt * 128:(mt + 1) * 128], psb[2][:, 0:512].rearrange("p (j c) -> p j c", j=4),
                           r=[("ps", 2)], w=["MKT"], eng="act")
                    else:
                        act(MV[:, mt, (g - 2) * 512:(g - 1) * 512], ps[1][:, :], AF.Copy, r=[("ps", 1)], w=["MV"])
            pcnt = 0
            for half in range(2):
                w, kw = load_slot([(dr["mem_q"][half], 4096)])
                wv = w[:, :].rearrange("p (k n) -> p k n", k=8)
                def mproj(t):
                    ts_ = slice(t * 128, (t + 1) * 128)
                    bq = t % 2
                    for k in range(8):
                        mm(ps[bq][:, :], HT[:, k, ts_], wv[:, k, :], k == 0, k == 7, r=[("HT", t // 4), kw], w=[("ps", bq)])

                mproj(0)
                group_rms_pre(ps[0][:, :], 2, 256, SQ, ("ps", 0), 0)
                for t in range(NT):
                    ts_ = slice(t * 128, (t + 1) * 128)
                    bq = t % 2
                    par = t % 2
                    btr = 2 + t % 2
                    if t + 1 < NT:
                        mproj(t + 1)

                    def pre_next(t=t):
                        if t + 1 < NT:
                            group_rms_pre(ps[(t + 1) % 2][:, :], 2, 256, SQ, ("ps", (t + 1) % 2), (t + 1) % 2)

                    group_rms_main(ps[bq][:, :], 2, 256, REP_MQN, WKf, SQ, QBb, ("ps", bq), par, mid=pre_next)
                    for j in range(4):
                        tr(psb[btr][:, j * 128:(j + 1) * 128], QBb[:, par * 512 + j * 128:par * 512 + (j + 1) * 128],
                           r=[("QBb", par)], w=[("ps", btr)])
                    cp(MQT[:, :, ts_], psb[btr][:, 0:512].rearrange("p (j c) -> p j c", j=4), r=[("ps", btr)], w=[("MQT", t // 4)], eng="act")
                msteps = [(hh, qb) for hh in range(2) for qb in range(NB)]
                PT2 = [[PTm[0][0], PTm[0][1]],
                       [carve_bf(RO_ + 36864, [128, 512]), carve_bf(RO_ + 36864 + 1024, [128, 512])]]

                def ptkey(b, mt):
                    return ("PTm", 0, mt) if b == 0 else ("SQ", 1)

                def sbk(i):
                    return (3, 4) if i % 2 == 0 else (0, 1)

                def m_scores(i):
                    hh, qb = msteps[i]
                    h = half * 2 + hh
                    qs = slice(qb * 512, (qb + 1) * 512)
                    sb_ = sbk(i)
                    for mt in range(2):
                        for c in range(2):
                            mm(ps[sb_[mt]][:, :], MKT[:, h * 2 + c, mt * 128:(mt + 1) * 128], MQT[:, hh * 2 + c, qs], c == 0, c == 1,
                               r=["MKT", ("MQT", qb)], w=[("ps", sb_[mt])])

                def m_exp(i):
                    b = i % 2
                    sb_ = sbk(i)
                    for mt in range(2):
                        act(PT2[b][mt], ps[sb_[mt]][:, :], AF.Exp, r=[("ps", sb_[mt])], w=[ptkey(b, mt)], scale=1.0 / 16.0)

                def m_pv(i):
                    hh, qb = msteps[i]
                    h = half * 2 + hh
                    qs = slice(qb * 512, (qb + 1) * 512)
                    b = i % 2
                    for c in range(2):
                        for mt in range(2):
                            mm(ps[5 + c][:, :], MV[:, mt, h * 256 + c * 128:h * 256 + (c + 1) * 128], PT2[b][mt], mt == 0, mt == 1,
                               r=["MV", ptkey(b, mt)], w=[("ps", 5 + c)])
                    for mt in range(2):
                        mm(ps[7][:, :], ones[:, :], PT2[b][mt], mt == 0, mt == 1, r=["ones", ptkey(b, mt)], w=[("ps", 7)])
                    act(RD, ps[7][:, :], AF.Ln, r=[("ps", 7)], w=[("SQ", 0)])
                    act(RD, RD, AF.Exp, r=[("SQ", 0)], w=[("SQ", 0)], scale=-1.0)
                    for c in range(2):
                        tt(BT[:, h * 2 + c, qs], ps[5 + c][:, :], RD, ALU.mult, r=[("ps", 5 + c), ("SQ", 0)], w=[("BT", qb)])

                m_scores(0)
                for i in range(len(msteps)):
                    if i + 1 < len(msteps):
                        m_scores(i + 1)
                    m_exp(i)
                    m_pv(i)

        def mixer():
            junk = carve_bf(0, [128, 1024])
            xn = [carve_bf(2048, [128, 1024]), carve_bf(4096, [128, 1024])]
            do_ret = stage in (2, 21) or stage >= 3
            do_diff = stage in (2, 22) or stage >= 3
            do_mem = stage in (2, 23) or stage >= 3
            if stage >= 20:
                do_ret, do_diff, do_mem = stage == 21, stage == 22, stage == 23
            if do_ret:
                ret_tables()
            norm_to_HT(COL_MIX, junk, xn)
            P.barrier()
            if do_ret:
                for pair in range(2):
                    for hh in range(2):
                        ret_head(pair * 2 + hh, hh * 4)
                    P.barrier()
                    apply_branch(pair, 0)
                    P.barrier()
            if do_diff:
                lam_compute()
                diff_tables()
                P.barrier()
                for gi in range(4):
                    diff_group(gi)
                P.barrier()
                apply_branch(2, 1)
                P.barrier()
            if do_mem:
                mem_phase()
                P.barrier()
                apply_branch(3, 2)
                P.barrier()

        def final_out(do_norm=True):
            junk = carve_bf(0, [128, 1024])
            OB = [carve_f32(4096 + i * 4096, [128, 1024]) for i in range(2)]
            FIN = carve_f32(12288, [128, 1024])
            if do_norm:
                P.dma("sp", "c3", lambda e: e.dma_start(out=FIN, in_=dr["fin"]), r=(), w=["FIN"])
                for t in range(NT):
                    act(junk, X[:, t, :], AF.Square, r=[("X", t)], w=["junk", ("ss", t)], accum_out=ss16[:, t:t + 1])
                tsc(rs16[:], ss16[:], 1.0 / D, EPS, ALU.mult, ALU.add, r=[("ss", t) for t in range(NT)], w=["rs16"])
                act(rs16[:], rs16[:], AF.Sqrt, r=["rs16"], w=["rs16"])
                recip(rs16[:], rs16[:], r=["rs16"], w=["rs16"])
            for t in range(NT):
                ob = OB[t % 2]
                if do_norm:
                    stt(ob, X[:, t, :], rs16[:, t:t + 1], FIN, ALU.mult, ALU.mult,
                        r=[("X", t), "rs16", "FIN"], w=[("OB", t % 2)])
                else:
                    cp(ob, X[:, t, :], r=[("X", t)], w=[("OB", t % 2)])
                P.dma("sp", f"y{t % 2}", lambda e, ob=ob, t=t: e.dma_start(out=y_d[t * 128:(t + 1) * 128, :], in_=ob),
                      r=[("OB", t % 2)], w=[("Y", t)])

        ffn("ffn1", COL_FFN1)
        P.barrier()
        if stage >= 2:
            mixer()
            P.barrier()
        if stage >= 3 and stage < 20:
            ffn("ffn2", COL_FFN2)
            P.barrier()
        final_out(do_norm=(stage >= 3 and stage < 20))
        P.barrier()
        print('PROG check:', P.check())
        P.emit(nc, es)
    return nc


_CACHE = {}


def kernel(**inputs):
    inp = {k: np.asarray(v) for k, v in inputs.items()}
    B = inp["x"].shape[0]
    W = _prep_weights(inp)
    C = _prep_consts()
    shared = dict(W)
    shared["ident"] = C["ident"]
    shared["dmask"] = C["dmask"].reshape(128, RET_H * 128)
    shared["rdec"] = C["rdec"]
    shared["ret_inv"] = C["ret_inv"]
    shared["rope_inv"] = C["rope_inv"]
    wshapes = {k: v.shape for k, v in shared.items()}
    key = ("nc", STAGE)
    if key not in _CACHE:
        _CACHE[key] = build_program(wshapes, STAGE)
    nc = _CACHE[key]
    in_maps = []
    for b in range(B):
        m = dict(shared)
        m["x"] = np.ascontiguousarray(inp["x"][b])
        m["mem"] = np.ascontiguousarray(inp["mem"][b])
        pos = inp["positions"][b].astype(np.int32)
        m["pos_rep"] = np.ascontiguousarray(np.broadcast_to(pos[None, :], (128, S)))
        m["pos_tok"] = np.ascontiguousarray(pos.reshape(NT, 128).T)
        in_maps.append(m)
    res = run_bass_kernel_spmd(nc, in_maps, core_ids=list(range(B)))
    out = np.stack([np.asarray(r["y"]) for r in res.results], axis=0)
    return out.astype(np.float32, copy=False)
```
